# Optimizing a Trainium2 kernel written in Bass

```python
import jax, jax.numpy as jnp
from jax import lax
import numpy as np

D_MODEL = 1024
BATCH = 8
SEQ = 2048
DEPTH = 2
DEC_BATCH = 128
DEC_SEQ = 1
PAST_LEN = 16384
PAGE_SIZE = 128

W_POOL = D_MODEL // 2
POOL_WINDOWS = (2, 4, 8, 16)
N_POOL_GROUPS = len(POOL_WINDOWS)
POOL_GROUP_DIM = W_POOL // N_POOL_GROUPS
POOL_BUF = max(POOL_WINDOWS) - 1
W_GMLP = D_MODEL // 2
GMLP_CHUNK = 128
N_GMLP_HEADS = 4
GMLP_HEAD_DIM = W_GMLP // N_GMLP_HEADS
W_CONV = D_MODEL // 2
CONV_K = 31
W_SC = D_MODEL // 2
SC_K = 3
IN_COLS = W_POOL + 2 * W_GMLP + 2 * W_CONV + 3 * W_SC
N_BRANCH = 4
N_MEM = 256
N_XHEADS = 4
XHEAD_DIM = D_MODEL // N_XHEADS
D_FF = ((8 * D_MODEL // 3 + 127) // 128) * 128
DN_ALPHA = (2.0 * DEPTH) ** 0.25
DN_BETA = (8.0 * DEPTH) ** -0.25
LN_EPS = 1e-5

kernel_name = 'hybrid_pool_sgu_conv_decoder_step'


def _layernorm(x, g, b):
    xf = x.astype(jnp.float32)
    mu = jnp.mean(xf, axis=-1, keepdims=True)
    var = jnp.mean(jnp.square(xf - mu), axis=-1, keepdims=True)
    y = (xf - mu) * lax.rsqrt(var + LN_EPS) * g.astype(jnp.float32) + b.astype(jnp.float32)
    return y.astype(x.dtype)


def _swiglu(x, w1, w3, w2):
    return (jax.nn.silu(x @ w1) * (x @ w3)) @ w2


def _causal_depthwise(ext, w):
    return lax.conv_general_dilated(ext, w[:, None, :].astype(ext.dtype), window_strides=(1,),
                                    padding='VALID', dimension_numbers=('NWC', 'WIO', 'NWC'),
                                    feature_group_count=ext.shape[-1])


def _pool_mix(a_ext, pos0, w_group, scale):
    n_b = a_ext.shape[0]
    L = a_ext.shape[1] - POOL_BUF
    af = a_ext.astype(jnp.float32)
    cs = jnp.concatenate([jnp.zeros_like(af[:, :1]), jnp.cumsum(af, axis=1)], axis=1)
    hi = cs[:, POOL_BUF + 1:]
    pos = pos0 + jnp.arange(L)
    means = []
    for g, win in enumerate(POOL_WINDOWS):
        c = slice(g * POOL_GROUP_DIM, (g + 1) * POOL_GROUP_DIM)
        lo = cs[:, POOL_BUF + 1 - win:POOL_BUF + 1 - win + L, c]
        cnt = jnp.minimum(pos + 1, win).astype(jnp.float32)[None, :, None]
        means.append((hi[:, :, c] - lo) / cnt)
    pooled = (jnp.concatenate(means, axis=-1) - af[:, POOL_BUF:]).astype(a_ext.dtype)
    grouped = pooled.reshape(n_b, L, N_POOL_GROUPS, POOL_GROUP_DIM)
    y = jnp.einsum('blgc,gcd->blgd', grouped, w_group).reshape(n_b, L, W_POOL)
    return y * scale


def _gmlp_sgu(u, v, ws, bias):
    n_b, L, _ = v.shape
    n = min(L, GMLP_CHUNK)
    mask = jnp.tril(jnp.ones((n, n), dtype=bool))
    w = jnp.where(mask[None], ws[:, :n, :n], 0)
    vc = v.reshape(n_b, L // n, n, N_GMLP_HEADS, GMLP_HEAD_DIM)
    z = jnp.einsum('hts,bcshd->bcthd', w, vc) + bias[:, :n].T[None, None, :, :, None]
    return u * z.reshape(n_b, L, W_GMLP)


def _token_mixer(h, pos0, pool_buf, conv_buf, sc_buf, p):
    n_b, L, _ = h.shape
    s1 = W_POOL
    s2 = s1 + W_GMLP
    s3 = s2 + W_GMLP
    s4 = s3 + W_CONV
    s5 = s4 + W_CONV
    s6 = s5 + W_SC
    s7 = s6 + W_SC
    proj = h @ p['w_in']
    a, gu, gv, ca, cb, sbg, scg, sx = jnp.split(proj, [s1, s2, s3, s4, s5, s6, s7], axis=-1)
    a_ext = jnp.concatenate([pool_buf, a], axis=1)
    y_a = _pool_mix(a_ext, pos0, p['pool_w'], p['pool_scale']) @ p['pool_proj']
    new_pool = a_ext[:, -POOL_BUF:]
    v = _layernorm(gv, p['gmlp_ln_g'], p['gmlp_ln_b'])
    y_b = _gmlp_sgu(gu, v, p['gmlp_ws'], p['gmlp_b']) @ p['gmlp_proj']
    glu = ca * jax.nn.sigmoid(cb)
    c_ext = jnp.concatenate([conv_buf, glu], axis=1)
    c = _causal_depthwise(c_ext, p['conv_dw']) + p['conv_db']
    y_c = jax.nn.silu(_layernorm(c, p['conv_ln_g'], p['conv_ln_b'])) @ p['conv_proj']
    new_conv = c_ext[:, -(CONV_K - 1):]
    z_ext = jnp.concatenate([sc_buf, scg * sx], axis=1)
    y_d = (sbg * _causal_depthwise(z_ext, p['sc_w'])) @ p['sc_proj']
    new_sc = z_ext[:, -(SC_K - 1):]
    gates = jax.nn.sigmoid(h @ p['w_gate'] + p['b_gate']).reshape(n_b, L, N_BRANCH, D_MODEL)
    merged = gates[:, :, 0] * y_a + gates[:, :, 1] * y_b + gates[:, :, 2] * y_c + gates[:, :, 3] * y_d
    return merged @ p['w_o'], new_pool, new_conv, new_sc, v


def _cross_attn(h, k, v, wq, wo):
    n_b, L, _ = h.shape
    q = (h @ wq).reshape(n_b, L, N_XHEADS, XHEAD_DIM)
    s = jnp.einsum('blhd,bmhd->bhlm', q, k).astype(jnp.float32) * (XHEAD_DIM ** -0.5)
    pr = jax.nn.softmax(s, axis=-1).astype(v.dtype)
    o = jnp.einsum('bhlm,bmhd->blhd', pr, v).reshape(n_b, L, D_MODEL)
    return o @ wo


def _layer(x, pos0, pool_buf, conv_buf, sc_buf, mem_k, mem_v, p):
    x = _layernorm(DN_ALPHA * x + 0.5 * _swiglu(x, p['ffn1_w1'], p['ffn1_w3'], p['ffn1_w2']), p['ln1_g'], p['ln1_b'])
    mix, new_pool, new_conv, new_sc, v = _token_mixer(x, pos0, pool_buf, conv_buf, sc_buf, p)
    x = _layernorm(DN_ALPHA * x + mix, p['ln2_g'], p['ln2_b'])
    x = _layernorm(DN_ALPHA * x + _cross_attn(x, mem_k, mem_v, p['xa_wq'], p['xa_wo']), p['ln3_g'], p['ln3_b'])
    x = _layernorm(DN_ALPHA * x + 0.5 * _swiglu(x, p['ffn2_w1'], p['ffn2_w3'], p['ffn2_w2']), p['ln4_g'], p['ln4_b'])
    return x, new_pool, new_conv, new_sc, v


def setup_inputs(seed: int = 0) -> dict:
    key = jax.random.key(seed)
    ks = iter(jax.random.split(key, 80))

    def nrm(shape, scale):
        return jax.random.normal(next(ks), shape, jnp.float32) * scale

    def gain(shape):
        return 1.0 + nrm(shape, 0.05)

    def bias(shape):
        return nrm(shape, 0.02)

    L = DEPTH
    return {
        'x_prompt': nrm((BATCH, SEQ, D_MODEL), 1.0),
        'x_sample': nrm((DEC_BATCH, DEC_SEQ, D_MODEL), 1.0),
        'mem_prompt': nrm((BATCH, N_MEM, D_MODEL), 1.0),
        'state_pool': nrm((DEPTH, DEC_BATCH, POOL_BUF, W_POOL), 1.0),
        'state_conv': nrm((DEPTH, DEC_BATCH, CONV_K - 1, W_CONV), 0.5),
        'state_shortconv': nrm((DEPTH, DEC_BATCH, SC_K - 1, W_SC), 0.5),
        'cache_mem_k': nrm((DEPTH, DEC_BATCH, N_MEM, N_XHEADS, XHEAD_DIM), 1.0),
        'cache_mem_v': nrm((DEPTH, DEC_BATCH, N_MEM, N_XHEADS, XHEAD_DIM), 1.0),
        'ln1_g': gain((L, D_MODEL)),
        'ln1_b': bias((L, D_MODEL)),
        'ffn1_w1': nrm((L, D_MODEL, D_FF), D_MODEL ** -0.5),
        'ffn1_w3': nrm((L, D_MODEL, D_FF), D_MODEL ** -0.5),
        'ffn1_w2': nrm((L, D_FF, D_MODEL), DN_BETA * D_FF ** -0.5),
        'w_in': nrm((L, D_MODEL, IN_COLS), D_MODEL ** -0.5),
        'w_gate': nrm((L, D_MODEL, N_BRANCH * D_MODEL), D_MODEL ** -0.5),
        'b_gate': bias((L, N_BRANCH * D_MODEL)),
        'pool_w': nrm((L, N_POOL_GROUPS, POOL_GROUP_DIM, POOL_GROUP_DIM), POOL_GROUP_DIM ** -0.5),
        'pool_scale': 1.0 + nrm((L, W_POOL), 0.1),
        'pool_proj': nrm((L, W_POOL, D_MODEL), W_POOL ** -0.5),
        'gmlp_ln_g': gain((L, W_GMLP)),
        'gmlp_ln_b': bias((L, W_GMLP)),
        'gmlp_ws': nrm((L, N_GMLP_HEADS, GMLP_CHUNK, GMLP_CHUNK), GMLP_CHUNK ** -0.5),
        'gmlp_b': 1.0 + nrm((L, N_GMLP_HEADS, GMLP_CHUNK), 0.1),
        'gmlp_proj': nrm((L, W_GMLP, D_MODEL), W_GMLP ** -0.5),
        'conv_dw': nrm((L, CONV_K, W_CONV), CONV_K ** -0.5),
        'conv_db': bias((L, W_CONV)),
        'conv_ln_g': gain((L, W_CONV)),
        'conv_ln_b': bias((L, W_CONV)),
        'conv_proj': nrm((L, W_CONV, D_MODEL), W_CONV ** -0.5),
        'sc_w': nrm((L, SC_K, W_SC), SC_K ** -0.5),
        'sc_proj': nrm((L, W_SC, D_MODEL), W_SC ** -0.5),
        'w_o': nrm((L, D_MODEL, D_MODEL), DN_BETA * D_MODEL ** -0.5),
        'ln2_g': gain((L, D_MODEL)),
        'ln2_b': bias((L, D_MODEL)),
        'xa_wq': nrm((L, D_MODEL, D_MODEL), D_MODEL ** -0.5),
        'xa_wk': nrm((L, D_MODEL, D_MODEL), D_MODEL ** -0.5),
        'xa_wv': nrm((L, D_MODEL, D_MODEL), D_MODEL ** -0.5),
        'xa_wo': nrm((L, D_MODEL, D_MODEL), DN_BETA * D_MODEL ** -0.5),
        'ln3_g': gain((L, D_MODEL)),
        'ln3_b': bias((L, D_MODEL)),
        'ffn2_w1': nrm((L, D_MODEL, D_FF), D_MODEL ** -0.5),
        'ffn2_w3': nrm((L, D_MODEL, D_FF), D_MODEL ** -0.5),
        'ffn2_w2': nrm((L, D_FF, D_MODEL), DN_BETA * D_FF ** -0.5),
        'ln4_g': gain((L, D_MODEL)),
        'ln4_b': bias((L, D_MODEL)),
    }


def reference(x_prompt, x_sample, mem_prompt, state_pool, state_conv, state_shortconv, cache_mem_k, cache_mem_v,
              ln1_g, ln1_b, ffn1_w1, ffn1_w3, ffn1_w2, w_in, w_gate, b_gate, pool_w, pool_scale, pool_proj,
              gmlp_ln_g, gmlp_ln_b, gmlp_ws, gmlp_b, gmlp_proj, conv_dw, conv_db, conv_ln_g, conv_ln_b, conv_proj,
              sc_w, sc_proj, w_o, ln2_g, ln2_b, xa_wq, xa_wk, xa_wv, xa_wo, ln3_g, ln3_b,
              ffn2_w1, ffn2_w3, ffn2_w2, ln4_g, ln4_b):
    n_p = x_prompt.shape[0]
    dt = x_prompt.dtype
    hp, hs = x_prompt, x_sample
    pool_p, conv_p, sc_p, mk_p, mv_p = [], [], [], [], []
    pool_s, conv_s, sc_s, gv_s = [], [], [], []
    for l in range(DEPTH):
        p = dict(ln1_g=ln1_g[l], ln1_b=ln1_b[l], ffn1_w1=ffn1_w1[l], ffn1_w3=ffn1_w3[l], ffn1_w2=ffn1_w2[l],
                 w_in=w_in[l], w_gate=w_gate[l], b_gate=b_gate[l], pool_w=pool_w[l], pool_scale=pool_scale[l],
                 pool_proj=pool_proj[l], gmlp_ln_g=gmlp_ln_g[l], gmlp_ln_b=gmlp_ln_b[l], gmlp_ws=gmlp_ws[l],
                 gmlp_b=gmlp_b[l], gmlp_proj=gmlp_proj[l], conv_dw=conv_dw[l], conv_db=conv_db[l],
                 conv_ln_g=conv_ln_g[l], conv_ln_b=conv_ln_b[l], conv_proj=conv_proj[l], sc_w=sc_w[l],
                 sc_proj=sc_proj[l], w_o=w_o[l], ln2_g=ln2_g[l], ln2_b=ln2_b[l], xa_wq=xa_wq[l], xa_wo=xa_wo[l],
                 ln3_g=ln3_g[l], ln3_b=ln3_b[l], ffn2_w1=ffn2_w1[l], ffn2_w3=ffn2_w3[l], ffn2_w2=ffn2_w2[l],
                 ln4_g=ln4_g[l], ln4_b=ln4_b[l])
        mk = (mem_prompt @ xa_wk[l]).reshape(n_p, N_MEM, N_XHEADS, XHEAD_DIM)
        mv = (mem_prompt @ xa_wv[l]).reshape(n_p, N_MEM, N_XHEADS, XHEAD_DIM)
        hp, npool, nconv, nsc, _ = _layer(
            hp, 0,
            jnp.zeros((n_p, POOL_BUF, W_POOL), dt),
            jnp.zeros((n_p, CONV_K - 1, W_CONV), dt),
            jnp.zeros((n_p, SC_K - 1, W_SC), dt),
            mk, mv, p)
        pool_p.append(npool)
        conv_p.append(nconv)
        sc_p.append(nsc)
        mk_p.append(mk)
        mv_p.append(mv)
        hs, spool, sconv, ssc, sv = _layer(hs, PAST_LEN, state_pool[l], state_conv[l], state_shortconv[l],
                                           cache_mem_k[l], cache_mem_v[l], p)
        pool_s.append(spool)
        conv_s.append(sconv)
        sc_s.append(ssc)
        gv_s.append(sv)
    return (hp, hs,
            jnp.stack(pool_p), jnp.stack(conv_p), jnp.stack(sc_p), jnp.stack(mk_p), jnp.stack(mv_p),
            jnp.stack(pool_s), jnp.stack(conv_s), jnp.stack(sc_s), jnp.stack(gv_s))
```

```python
import numpy as np
from contextlib import ExitStack
import concourse.bass as bass
import concourse.mybir as mybir
from concourse.bass_utils import run_bass_kernel_spmd

F32 = mybir.dt.float32
BF16 = mybir.dt.bfloat16
AF = mybir.ActivationFunctionType
ALU = mybir.AluOpType
AX = mybir.AxisListType

NCORES = 8
D = 1024
KD = 8
DFF = 2816
KF = 22
SEQ = 2048
NS = 16
L = 2
TILE = 512
TPG = 1
NGRP = (SEQ // TILE) // TPG
NPG = TPG * TILE
TG = NPG + NS
HP = 30
NMEM = 256
ALPHA = float((2.0 * L) ** 0.25)
EPS = 1e-5
WINS = (2, 4, 8, 16)
NSLOT = 7
NSTG = 4
NKV = 3
SLOT_E = 2048
NBANK = 7
GCH = 24
NTA = 6

ENGS = ("pe", "act", "dve", "pool", "sp")
STRICT_SAME_ENGINE = True
STRICT_FLAG = [True]


class Cell:
    __slots__ = ("w", "r", "name")

    def __init__(self, name=""):
        self.w = None
        self.r = {}
        self.name = name


class Sched:
    def __init__(self):
        self.streams = {e: [] for e in ENGS}
        self.count = {}
        self.waited = {e: {} for e in ENGS}
        self.dma_sems = []
        self.stage = ""
        self.group = -1
        self.labels = {}

    def _deps(self, eng, reads, writes):
        need = {}

        def add(dep, same_ok):
            if dep is None:
                return
            k, v = dep
            if k == eng and same_ok and not STRICT_SAME_ENGINE:
                return
            if self.waited[eng].get(k, 0) >= v:
                return
            if need.get(k, 0) < v:
                need[k] = v

        for c in reads:
            add(c.w, False)
        for c in writes:
            add(c.w, True)
            for k, v in c.r.items():
                add((k, v), True)
        for k, v in need.items():
            self.waited[eng][k] = v
        return need

    def op(self, eng, reads, writes, fn, after=()):
        if after:
            tmp = Cell("after")
            for c in after:
                if c.w is not None:
                    k, v = c.w
                    if tmp.r.get(k, 0) < v:
                        tmp.r[k] = v
                for k, v in c.r.items():
                    if tmp.r.get(k, 0) < v:
                        tmp.r[k] = v
            writes = list(writes) + [tmp]
            saved = STRICT_FLAG[0]
            STRICT_FLAG[0] = True
            need = self._deps(eng, reads, writes)
            STRICT_FLAG[0] = saved
            writes = writes[:-1]
        else:
            need = self._deps(eng, reads, writes)
        self.count[eng] = self.count.get(eng, 0) + 1
        v = self.count[eng]
        self.streams[eng].append((need, fn, eng, 1))
        self.labels.setdefault(eng, []).append("g%d.%s" % (self.group, self.stage))
        for c in reads:
            c.r[eng] = v
        for c in writes:
            c.w = (eng, v)
            c.r = {}

    def dma(self, q, sem, n, reads, writes, fn):
        if sem not in self.count:
            self.count[sem] = 0
            self.dma_sems.append(sem)
        need = self._deps(q, reads, writes)
        self.count[sem] += 16 * n
        v = self.count[sem]
        self.streams[q].append((need, fn, sem, 0))
        for c in reads:
            c.r[sem] = v
        for c in writes:
            c.w = (sem, v)
            c.r = {}


class Slab:
    def __init__(self, slot, cell, cw):
        self.slot = slot
        self.cell = cell
        self.cw = cw


class Builder:
    def __init__(self, step_limit=None):
        self.step_limit = step_limit
        self.nc = bass.Bass("TRN2", target_bir_lowering=False)
        self.s = Sched()
        self.bank_i = 0
        self.ta_i = 0
        self.stg_i = 0
        self.setup_cells = []

    def declare_dram(self):
        nc = self.nc

        def inp(name, shape):
            return nc.dram_tensor(name, list(shape), F32, kind="ExternalInput").ap()

        def outp(name, shape):
            return nc.dram_tensor(name, list(shape), F32, kind="ExternalOutput").ap()

        d = {}
        d["x_prompt"] = inp("x_prompt", (SEQ, D))
        d["x_sample"] = inp("x_sample", (NS, D))
        d["mem_prompt"] = inp("mem_prompt", (NMEM, D))
        d["state_pool"] = inp("state_pool", (L, NS, 15, 512))
        d["state_conv"] = inp("state_conv", (L, NS, 30, 512))
        d["state_shortconv"] = inp("state_shortconv", (L, NS, 2, 512))
        d["cache_mem_k"] = inp("cache_mem_k", (L, NS, NMEM, D))
        d["cache_mem_v"] = inp("cache_mem_v", (L, NS, NMEM, D))
        for nm in ("ln1_g", "ln1_b", "ln2_g", "ln2_b", "ln3_g", "ln3_b", "ln4_g", "ln4_b"):
            d[nm] = inp(nm, (L, D))
        for nm in ("ffn1_w1", "ffn1_w3", "ffn2_w1", "ffn2_w3"):
            d[nm] = inp(nm, (L, D, DFF))
        for nm in ("ffn1_w2", "ffn2_w2"):
            d[nm] = inp(nm, (L, DFF, D))
        d["w_in"] = inp("w_in", (L, D, 4096))
        d["w_gate"] = inp("w_gate", (L, D, 4096))
        d["b_gate"] = inp("b_gate", (L, 4096))
        d["pool_w"] = inp("pool_w", (L, 4, 128, 128))
        d["pool_scale"] = inp("pool_scale", (L, 512))
        for nm in ("pool_proj", "gmlp_proj", "conv_proj", "sc_proj"):
            d[nm] = inp(nm, (L, 512, D))
        d["gmlp_ln_g"] = inp("gmlp_ln_g", (L, 512))
        d["gmlp_ln_b"] = inp("gmlp_ln_b", (L, 512))
        d["gmlp_ws"] = inp("gmlp_ws", (L, 4, 128, 128))
        d["gmlp_b"] = inp("gmlp_b", (L, 4, 128))
        d["conv_dw"] = inp("conv_dw", (L, 31, 512))
        d["conv_db"] = inp("conv_db", (L, 512))
        d["conv_ln_g"] = inp("conv_ln_g", (L, 512))
        d["conv_ln_b"] = inp("conv_ln_b", (L, 512))
        d["sc_w"] = inp("sc_w", (L, 3, 512))
        for nm in ("w_o", "xa_wq", "xa_wk", "xa_wv", "xa_wo"):
            d[nm] = inp(nm, (L, D, D))
        o = {}
        o["y_prompt"] = outp("y_prompt", (SEQ, D))
        o["y_sample"] = outp("y_sample", (NS, D))
        o["pool_p"] = outp("pool_p", (L, 15, 512))
        o["conv_p"] = outp("conv_p", (L, 30, 512))
        o["sc_p"] = outp("sc_p", (L, 2, 512))
        o["mk_p"] = outp("mk_p", (L, NMEM, D))
        o["mv_p"] = outp("mv_p", (L, NMEM, D))
        o["pool_s"] = outp("pool_s", (L, NS, 15, 512))
        o["conv_s"] = outp("conv_s", (L, NS, 30, 512))
        o["sc_s"] = outp("sc_s", (L, NS, 2, 512))
        o["gv_s"] = outp("gv_s", (L, NS, 512))
        self.d = d
        self.o = o

    def alloc(self, es):
        nc = self.nc

        def sb(name, shape, dt):
            return es.enter_context(nc.sbuf_tensor(name, list(shape), dt))

        self.X = sb("X", (128, KD, TG), F32)
        self.XB = sb("XB", (128, KD, TG), BF16)
        self.G = sb("G", (128, GCH, TG), BF16)
        self.F = sb("F", (128, 8, HP + TG), F32)
        self.ST = sb("ST", (128, 5, 512), F32)
        self.TA = sb("TA", (128, NTA, 512), F32)
        self.WS = sb("WS", (128, NSLOT, SLOT_E), BF16)
        self.WST = sb("WST", (128, NSTG - 2, SLOT_E), F32)
        self.STG = sb("STG", (128, 2, 1024), F32)
        self.IDF = sb("IDF", (128, 128), F32)
        self.IDB = sb("IDB", (128, 128), BF16)
        self.ONESB = sb("ONESB", (128, 128), BF16)
        self.ONESF = sb("ONESF", (128, 128), F32)
        self.RC0 = sb("RC0", (128, 4, 16), F32)
        self.PRAW = sb("PRAW", (128, L, 128), F32)
        self.PAR = sb("PAR", (128, L, 128), F32)
        self.DWT = sb("DWT", (128, L, 4, 31), F32)
        self.SCW = sb("SCW", (128, L, 4, 3), F32)
        self.POOLW = sb("POOLW", (128, L, 4, 128), BF16)
        self.WT = sb("WT", (128, L, 4, 128), BF16)
        self.GB = sb("GB", (128, L, 512), F32)
        self.GS = sb("GS", (128, L, 8), F32)
        self.KT = sb("KT", (128, L, KD, NMEM), BF16)
        self.VB = sb("VB", (128, L, 2, D), BF16)
        self.CA = sb("CA", (128, L, 4, 15), F32)
        self.CC = sb("CC", (128, L, 4, 30), F32)
        self.CD = sb("CD", (128, L, 4, 2), F32)
        self.SPOOL = sb("SPOOL", (128, 4, NS, 15), F32)
        self.SCONV = sb("SCONV", (128, 4, NS, 31), F32)
        self.SSC = sb("SSC", (128, 4, NS, 3), F32)
        self.KVS = sb("KVS", (128, NKV, D), F32)
        self.kv_i = 0
        self.EXTB = sb("EXTB", (128, 4, HP + NPG + 2), BF16)
        self.QS = sb("QS", (16, D), F32)
        self.SS = sb("SS", (128, 128), F32)
        self.ES = sb("ES", (128, 128), F32)
        self.RS = sb("RS", (128, 4, NS), F32)
        self.PS = es.enter_context(nc.psum_tensor("PS", [128, NBANK, 512], F32))
        self.PSB = es.enter_context(nc.psum_tensor("PSB", [128, 2, 512], BF16))

        C = Cell
        self.Xc = [C("X%d" % i) for i in range(TPG + 1)]
        self.XBc = [C("XB%d" % i) for i in range(TPG + 1)]
        self.Gc = [[C("G%d_%d" % (c, i)) for i in range(TPG + 1)] for c in range(GCH)]
        self.Fc = [C("F%d" % j) for j in range(8)]
        self.STc = [C("ST%d" % j) for j in range(5)]
        self.TAc = [C("TA%d" % j) for j in range(NTA)]
        self.WSc = [C("WS%d" % j) for j in range(NSLOT)]
        self.WSTc = [C("WST%d" % j) for j in range(NSTG - 2)]
        self.STGc = [C("STG%d" % j) for j in range(2)]
        self.bankc = [C("bank%d" % j) for j in range(NBANK)]
        _psb = C("psb")
        self.PSBc = [_psb, _psb]
        self.constc = C("const")
        self.PRAWc = C("praw")
        self.PARc = C("par")
        self.DWTc = C("dwt")
        self.POOLWc = C("poolw")
        self.WTc = C("wt")
        self.GBc = C("gb")
        self.MEMTc = C("memt")
        self.KTc = [C("kt%d" % l) for l in range(L)]
        self.VBc = [C("vb%d" % l) for l in range(L)]
        self.CAc = [[C() for j in range(4)] for l in range(L)]
        self.CCc = [[C() for j in range(4)] for l in range(L)]
        self.CDc = [[C() for j in range(4)] for l in range(L)]
        self.SPOOLc = C("spool")
        self.SCONVc = [C("sconv%d" % j) for j in range(4)]
        self.SSCc = [C("ssc%d" % j) for j in range(4)]
        self.KVSc = [C("kvs%d" % j) for j in range(NKV)]
        self.EXTBc = [C("extb%d" % j) for j in range(4)]
        self.QSc = C("qs")
        self.SSc = C("ss")
        self.ESc = C("es")
        self.RSc = C("rs")

    def next_bank(self):
        b = self.bank_i
        self.bank_i = (self.bank_i + 1) % NBANK
        return b

    def next_ta(self):
        a = self.ta_i
        self.ta_i = (self.ta_i + 1) % NTA
        return a

    def next_stg(self):
        a = self.stg_i
        self.stg_i = (self.stg_i + 1) % 2
        return a

    def mm(self, pairs, n, reads, m=128, bank=None, c0=0, writes_extra=()):
        b = self.next_bank() if bank is None else bank
        out = self.PS[0:m, b, c0:c0 + n]
        npair = len(pairs)

        def fn(e):
            ins = None
            for i, (l, r) in enumerate(pairs):
                ins = e.matmul(out, lhsT=l, rhs=r, start=(i == 0), stop=(i == npair - 1))
            return ins

        self.s.op("pe", reads, [self.bankc[b]] + list(writes_extra), fn)
        return b

    def act(self, out, in_, func, reads, writes, **kw):
        self.s.op("act", reads, writes, lambda e: e.activation(out=out, in_=in_, func=func, **kw))

    def tt(self, eng, out, in0, in1, op, reads, writes):
        self.s.op(eng, reads, writes, lambda e: e.tensor_tensor(out=out, in0=in0, in1=in1, op=op))

    def stt(self, eng, out, in0, scalar, in1, op0, op1, reads, writes):
        self.s.op(eng, reads, writes, lambda e: e.scalar_tensor_tensor(out=out, in0=in0, scalar=scalar, in1=in1,
                                                                      op0=op0, op1=op1))

    def ts(self, eng, out, in0, s1, s2, op0, op1, reads, writes):
        if s2 is None:
            self.s.op(eng, reads, writes, lambda e: e.tensor_scalar(out=out, in0=in0, scalar1=s1, scalar2=None, op0=op0))
        else:
            self.s.op(eng, reads, writes, lambda e: e.tensor_scalar(out=out, in0=in0, scalar1=s1, scalar2=s2,
                                                                    op0=op0, op1=op1))

    def cp(self, eng, out, in_, reads, writes):
        if eng == "act":
            self.s.op("act", reads, writes, lambda e: e.activation(out=out, in_=in_, func=AF.Copy))
        else:
            self.s.op(eng, reads, writes, lambda e: e.tensor_copy(out=out, in_=in_))

    def dma(self, q, sem, out, in_, reads, writes, slow=False):
        def fn(e, semh):
            if slow:
                return e.dma_start(out=out, in_=in_, allow_slow_non_contiguous=True).then_inc(semh, 16)
            return e.dma_start(out=out, in_=in_).then_inc(semh, 16)

        self.s.dma(q, sem, 1, reads, writes, fn)

    def setup_dma(self, q, out, in_, cell, slow=False):
        self.dma(q, "setup", out, in_, [], [], slow=slow)
        if cell not in self.setup_cells:
            self.setup_cells.append(cell)

    def tr_to_bank(self, in_ap, k, m, b, c0, reads, p0=0):
        out = self.PS[0:m, b, c0:c0 + k]
        idn = self.IDF[p0:p0 + k, p0:p0 + k]
        self.s.op("pe", reads + [self.constc], [self.bankc[b]],
                  lambda e: e.transpose(out=out, in_=in_ap, identity=idn))

    def out_rows(self, src_aps, m, dst_ap, reads, width=128):
        b = self.next_bank()
        for j, a in enumerate(src_aps):
            self.tr_to_bank(a, 128, m, b, j * 128, reads)
        sg = self.next_stg()
        w = 128 * len(src_aps)
        self.cp("act", self.STG[0:m, sg, 0:w], self.PS[0:m, b, 0:w], [self.bankc[b]], [self.STGc[sg]])
        self.dma("sp", "stg%d" % sg, dst_ap, self.STG[0:m, sg, 0:w], [self.STGc[sg]], [])

    def wload(self, kind, l, idx, which=0):
        d = self.d
        slot = self.ws_i
        self.ws_i = (self.ws_i + 1) % NSLOT
        sslot = self.wst_i
        self.wst_i = (self.wst_i + 1) % NSTG
        cell = self.WSc[slot]
        if sslot < NSTG - 2:
            scells = [self.WSTc[sslot]]
            stg_flat = self.WST[:, sslot, :]
        elif sslot == NSTG - 2:
            scells = [self.STGc[0], self.STGc[1]]
            stg_flat = self.STG[:, :, :].rearrange("p a c -> p (a c)")
        else:
            scells = [self.KVSc[0], self.KVSc[1]]
            stg_flat = self.KVS[:, 0:2, :].rearrange("p a c -> p (a c)")

        def src_cols(w2d, c0, cw, r0=0, kt=None):
            v = w2d.rearrange("(k p) n -> p k n", p=128)
            if kt is not None:
                v = v[:, r0:r0 + kt, :]
            return v[:, :, c0:c0 + cw]

        def dst(kt, cw, e0=0):
            return stg_flat[:, e0:e0 + kt * cw].rearrange("p (k c) -> p k c", c=cw)

        dmas = []
        if kind in ("w1", "w3"):
            cw = 256
            dmas.append((dst(KD, cw), src_cols(d["ffn%d_%s" % (which + 1, kind)][l], idx * 256, cw)))
            ne = KD * cw
        elif kind == "w2":
            cw = 128
            dc, half = idx // 2, idx % 2
            dmas.append((dst(11, cw), src_cols(d["ffn%d_w2" % (which + 1)][l], dc * 128, cw, r0=half * 11, kt=11)))
            ne = 11 * cw
        elif kind in ("w_in", "w_o", "xa_wq", "xa_wk", "xa_wv", "xa_wo"):
            cw = 256
            dmas.append((dst(KD, cw), src_cols(d[kind][l], idx * 256, cw)))
            ne = KD * cw
        elif kind == "gate":
            cw = 256
            j, pr = idx // 2, idx % 2
            wd = stg_flat[:, 0:KD * 256].rearrange("p (k b c) -> p k b c", b=2, c=128)
            for b2 in range(2):
                dmas.append((wd[:, :, b2, :], src_cols(d["w_gate"][l], (pr * 2 + b2) * 1024 + j * 128, 128)))
            ne = KD * cw
        elif kind == "proj":
            cw = 128
            for bi, nm in enumerate(("pool_proj", "gmlp_proj", "conv_proj", "sc_proj")):
                dmas.append((dst(4, cw, e0=bi * 512), src_cols(d[nm][l], idx * 128, cw)))
            ne = 2048
        else:
            raise ValueError(kind)
        n = len(dmas)
        spec = (kind, l, idx, which)
        cached = self.wcache.get(spec)
        if cached is not None:
            si, ccell = cached
            src = self.wscr[si, :, 0:ne]

            def fnl(e, semh):
                return e.dma_start(out=self.WS[:, slot, 0:ne], in_=src).then_inc(semh, 16)

            self.wst_i = (self.wst_i - 1) % NSTG
            self.s.dma("sp", "wsl%d" % slot, 1, [ccell], [cell], fnl)
            return Slab(slot, cell, cw)

        def fn(e, semh):
            ins = None
            for (o, i) in dmas:
                ins = e.dma_start(out=o, in_=i).then_inc(semh, 16)
            return ins

        self.s.dma("sp", "wst%d" % sslot, n, [], scells, fn)
        self.cast_i = getattr(self, "cast_i", 0) + 1
        self.cp("act" if self.cast_i % 2 else "dve", self.WS[:, slot, 0:ne], stg_flat[:, 0:ne], scells, [cell])
        sl = Slab(slot, cell, cw)
        if spec in self.wreuse:
            si = len(self.wcache)
            ccell = Cell("wc%d" % si)
            self.wcache[spec] = (si, ccell)
            sl.store = (si, ccell, ne)
        return sl

    def wplan(self, specs):
        seen, reuse = set(), set()
        for sp in specs:
            if sp in seen:
                reuse.add(sp)
            seen.add(sp)
        self.wreuse = reuse
        self.wcache = {}
        self.wscr = self.nc.dram_tensor("wscratch", [max(1, len(reuse)), 128, SLOT_E], BF16, kind="Internal").ap()
        self.wq_specs = list(specs)
        self.wq_loaded = []
        self.wq_pos = 0
        self.ws_i = 0
        self.wst_i = 0

    def wtake(self, kind, l, idx, which=0, keep=1):
        spec = (kind, l, idx, which)
        while len(self.wq_loaded) < min(max(self.wq_pos + NSLOT - keep, self.wq_pos + 1), len(self.wq_specs)):
            sp = self.wq_specs[len(self.wq_loaded)]
            self.wq_loaded.append(self.wload(*sp))
        assert self.wq_specs[self.wq_pos] == spec, (self.wq_specs[self.wq_pos], spec)
        sl = self.wq_loaded[self.wq_pos]
        self.wq_pos += 1
        st_ = getattr(sl, "store", None)
        if st_ is not None:
            si, ccell, ne = st_
            sl.store = None
            dstc = self.wscr[si, :, 0:ne]
            src = self.WS[:, sl.slot, 0:ne]

            def fns(e, semh):
                return e.dma_start(out=dstc, in_=src).then_inc(semh, 16)

            self.s.dma("sp", "wss%d" % sl.slot, 1, [sl.cell], [ccell], fns)
        return sl

    def wap(self, sl, k, c0, cn, e0=0):
        base = e0 + k * sl.cw + c0
        return self.WS[:, sl.slot, base:base + cn]

    def specs_setup(self):
        sp = []
        for l in range(L):
            for nm in ("xa_wk", "xa_wv"):
                for i in range(4):
                    sp.append((nm, l, i, 0))
        return sp

    def specs_ffn(self, l, which):
        sp = []
        for s_ in range(11):
            sp.append(("w1", l, s_, which))
            sp.append(("w3", l, s_, which))
        for i in range(16):
            sp.append(("w2", l, i, which))
        return sp

    def specs_layer(self, l):
        sp = self.specs_ffn(l, 0)
        for i in range(16):
            sp.append(("w_in", l, i, 0))
        for j in range(8):
            sp.append(("gate", l, 2 * j, 0))
            sp.append(("gate", l, 2 * j + 1, 0))
            sp.append(("proj", l, j, 0))
        for nm in ("w_o", "xa_wq", "xa_wo"):
            for i in range(4):
                sp.append((nm, l, i, 0))
        sp += self.specs_ffn(l, 1)
        return sp

    def tiles_of(self, gi):
        tl = []
        for t in range(TPG):
            tl.append(dict(kind="p", off=t * TILE, n=TILE, seq0=(gi * TPG + t) * TILE, ti=t))
        if gi == NGRP - 1:
            tl.append(dict(kind="s", off=NPG, n=NS, seq0=0, ti=TPG))
        return tl

    def Xv(self, k, t):
        return self.X[:, k, t["off"]:t["off"] + t["n"]]

    def X3(self, t):
        return self.X[:, :, t["off"]:t["off"] + t["n"]]

    def XBv(self, k, t):
        return self.XB[:, k, t["off"]:t["off"] + t["n"]]

    def XB3(self, t):
        return self.XB[:, :, t["off"]:t["off"] + t["n"]]

    def Gv(self, c, t):
        return self.G[:, c, t["off"]:t["off"] + t["n"]]

    def G3(self, c0, cn, t):
        return self.G[:, c0:c0 + cn, t["off"]:t["off"] + t["n"]]

    def Fv(self, j, t):
        return self.F[:, j, HP + t["off"]:HP + t["off"] + t["n"]]

    def F3(self, j0, jn, t):
        return self.F[:, j0:j0 + jn, HP + t["off"]:HP + t["off"] + t["n"]]

    def par(self, l, col):
        return self.PAR[:, l, col:col + 1]

    def layernorm(self, t, C, src3, src_cells, zb3, zb_cells, zq3, zq_cells, eps, l, gcol, bcol, crit, post=None,
                  chunk_cells=None, pre=False):
        n = t["n"]
        ST = self.ST
        inv = 1.0 / (C * 128.0)
        if not pre:
            self.cp("dve", zb3, src3, src_cells, zb_cells)
            self.act(zq3, src3, AF.Square, src_cells, zq_cells)
        bs = self.mm([(self.ONESB[:, :], zb3[:, c, :]) for c in range(C)], n, zb_cells + [self.constc])
        bq = self.mm([(self.ONESB[:, :], zq3[:, c, :]) for c in range(C)], n, zq_cells + [self.constc])
        mean, msq, rstd, nmr = ST[:, 0, 0:n], ST[:, 1, 0:n], ST[:, 2, 0:n], ST[:, 3, 0:n]
        Sc = self.STc
        self.ts("dve", mean, self.PS[:, bs, 0:n], inv, None, ALU.mult, None, [self.bankc[bs]], [Sc[0]])
        self.tt("dve", msq, mean, mean, ALU.mult, [Sc[0]], [Sc[1]])
        self.stt("dve", msq, self.PS[:, bq, 0:n], inv, msq, ALU.mult, ALU.subtract, [self.bankc[bq], Sc[1]], [Sc[1]])
        self.ts("dve", msq, msq, 0.0, None, ALU.max, None, [Sc[1]], [Sc[1]])
        self.act(rstd, msq, AF.Sqrt, [Sc[1]], [Sc[2]], bias=float(eps), scale=1.0)
        self.s.op("dve", [Sc[2]], [Sc[2]], lambda e: e.reciprocal(out=rstd, in_=rstd))
        self.stt("dve", nmr, mean, -1.0, rstd, ALU.mult, ALU.mult, [Sc[0], Sc[2]], [Sc[3]])
        if chunk_cells is None:
            cc = [Cell("lnc%d" % c) for c in range(C)]
            after = list(src_cells)
        else:
            cc = chunk_cells
            after = []
        for c in range(C):
            x = src3[:, c, :]
            self.s.op("dve", [Sc[2], cc[c]] if chunk_cells is not None else [Sc[2]], [cc[c]],
                      lambda e, x=x: e.tensor_tensor(out=x, in0=x, in1=rstd, op=ALU.mult), after=after)
            self.tt("dve", x, x, nmr, ALU.add, [cc[c], Sc[3]], [cc[c]])
            o, func, cells = crit(c)
            self.act(o, x, func, [cc[c], self.PARc], cells, scale=self.par(l, gcol + c), bias=self.par(l, bcol + c))
        if post is not None:
            for c in range(C):
                o, func, cells = post(c)
                self.ts("dve", o, src3[:, c, :], self.par(l, gcol + c), self.par(l, bcol + c), ALU.mult, ALU.add,
                        [cc[c], self.PARc], cells + [cc[c]])

    def ln_x(self, tiles, l, idx, eps, pre=False):
        gcol = 16 * idx
        bcol = 16 * idx + 8
        for t in tiles:
            ti = t["ti"]
            xc = [self.Xc[ti]]
            xbc = [self.XBc[ti]]
            zq_cells = [self.Gc[c][ti] for c in range(8)]
            self.layernorm(t, 8, self.X3(t), xc, self.XB3(t), xbc, self.G3(0, 8, t), zq_cells, eps, l, gcol, bcol,
                           crit=lambda c, t=t, xbc=xbc: (self.XBv(c, t), AF.Identity, xbc),
                           post=lambda c, t=t, xc=xc: (self.Xv(c, t), AF.Identity, xc), pre=pre)

    def ln_pre(self, dc, t):
        ti = t["ti"]
        self.act(self.Gv(dc, t), self.Xv(dc, t), AF.Square, [self.Xc[ti]], [self.Gc[dc][ti]])
        self.cp("act", self.XBv(dc, t), self.Xv(dc, t), [self.Xc[ti]], [self.XBc[ti]])

    def ffn(self, tiles, l, which):
        self.s.stage = 'ffn.up'
        for s_ in range(11):
            s1 = self.wtake("w1", l, s_, which)
            s3 = self.wtake("w3", l, s_, which)
            for fc in range(2):
                f = s_ * 2 + fc
                for t in tiles:
                    n, ti = t["n"], t["ti"]
                    b1 = self.mm([(self.wap(s1, k, fc * 128, 128), self.XBv(k, t)) for k in range(KD)], n,
                                 [s1.cell, self.XBc[ti]])
                    b3 = self.mm([(self.wap(s3, k, fc * 128, 128), self.XBv(k, t)) for k in range(KD)], n,
                                 [s3.cell, self.XBc[ti]])
                    a = self.next_ta()
                    self.act(self.TA[:, a, 0:n], self.PS[:, b1, 0:n], AF.Silu, [self.bankc[b1]], [self.TAc[a]])
                    self.tt("dve", self.Gv(f, t), self.TA[:, a, 0:n], self.PS[:, b3, 0:n], ALU.mult,
                            [self.TAc[a], self.bankc[b3]], [self.Gc[f][ti]])
        self.s.stage = 'ffn.down'
        for dc in range(8):
            s2a = self.wtake("w2", l, 2 * dc, which)
            s2b = self.wtake("w2", l, 2 * dc + 1, which)
            for t in tiles:
                n, ti = t["n"], t["ti"]
                b = self.mm([(self.wap(s2a if f < 11 else s2b, f % 11, 0, 128), self.Gv(f, t)) for f in range(KF)], n,
                            [s2a.cell, s2b.cell] + [self.Gc[f][ti] for f in range(KF)])
                self.stt("dve", self.Xv(dc, t), self.Xv(dc, t), 2.0 * ALPHA, self.PS[:, b, 0:n], ALU.mult, ALU.add,
                         [self.Xc[ti], self.bankc[b]], [self.Xc[ti]])
        self.s.stage = 'ffn.ln'
        self.ln_x(tiles, l, 0 if which == 0 else 3, 4.0 * EPS)

    def win_evac(self, sl, fc, tiles, fn_evac):
        for t in tiles:
            n, ti = t["n"], t["ti"]
            b = self.mm([(self.wap(sl, k, fc * 128, 128), self.XBv(k, t)) for k in range(KD)], n,
                        [sl.cell, self.XBc[ti]])
            fn_evac(t, b)

    def mixer(self, gi, tiles, l):
        F, G, TA, PS = self.F, self.G, self.TA, self.PS
        Fc, Gc, TAc, bankc = self.Fc, self.Gc, self.TAc, self.bankc
        ptiles = [t for t in tiles if t["kind"] == "p"]
        st = [t for t in tiles if t["kind"] == "s"]
        st = st[0] if st else None
        d, o = self.d, self.o
        first = (gi == 0)
        if st is not None:
            sp_rows = d["state_pool"][l].rearrange("b r c -> (b r) c")
            cv_rows = d["state_conv"][l].rearrange("b r c -> (b r) c")
            sc_rows = d["state_shortconv"][l].rearrange("b r c -> (b r) c")
            for (rows, nrow, per, dst_t, dcells, rr) in ((sp_rows, 240, 15, self.SPOOL, [self.SPOOLc] * 4, 15),
                                                         (cv_rows, 480, 30, self.SCONV, self.SCONVc, 31),
                                                         (sc_rows, 32, 2, self.SSC, self.SSCc, 3)):
                r0 = 0
                while r0 < nrow:
                    nb = min(128 // per, (nrow - r0) // per)
                    m = nb * per
                    sg = self.next_stg()
                    self.dma("sp", "stg%d" % sg, self.STG[0:m, sg, 0:512], rows[r0:r0 + m, :], [], [self.STGc[sg]])
                    b0 = r0 // per
                    for j in range(4):
                        b = self.next_bank()
                        self.tr_to_bank(self.STG[0:m, sg, j * 128:(j + 1) * 128], m, 128, b, 0, [self.STGc[sg]])
                        self.cp("act", dst_t[:, j, b0:b0 + nb, 0:per],
                                PS[:, b, 0:m].rearrange("p (b r) -> p b r", r=per), [bankc[b]], [dcells[j]])
                    r0 += m
            self.dma("sp", "out", o["pool_s"][l, :, 0:14, :], d["state_pool"][l, :, 1:15, :], [], [])
            self.dma("sp", "out", o["conv_s"][l, :, 0:29, :], d["state_conv"][l, :, 1:30, :], [], [])
            self.dma("sp", "out", o["sc_s"][l, :, 0:1, :], d["state_shortconv"][l, :, 1:2, :], [], [])

        npg = len(ptiles) * TILE
        c_lo, c_hi = HP, HP + npg

        self.s.stage = 'mix.A'
        def role(r, fn_evac):
            for hs in range(2):
                sl_ = self.wtake("w_in", l, 2 * r + hs)
                for c2 in range(2):
                    j_ = hs * 2 + c2
                    self.win_evac(sl_, c2, tiles, lambda t, b, j_=j_: fn_evac(j_, t, b))

        for j in range(4):
            self.cp("act", F[:, j, HP - 15:HP], self.CA[:, l, j, :], [self.CAc[l][j]], [Fc[j]])
        role(0, lambda j, t, b: self.cp("act", self.Fv(j, t), PS[:, b, 0:t["n"]], [bankc[b]], [Fc[j]]))
        for g in range(4):
            win = WINS[g]
            Lx = 15 + npg
            e0 = HP - 15
            B1, B2 = 4 + 2 * (g % 2), 5 + 2 * (g % 2)
            src, cur = g, None
            bufs = [B1, B2]
            sh = 1
            step = 0
            while sh < win:
                dstb = bufs[step % 2]
                lo = 2 * sh - 1
                self.tt("dve", F[:, dstb, e0 + lo:e0 + Lx], F[:, src, e0 + lo:e0 + Lx], F[:, src, e0 + lo - sh:e0 + Lx - sh],
                        ALU.add, [Fc[src]], [Fc[dstb]])
                src = dstb
                sh *= 2
                step += 1
            S = src
            for t in ptiles:
                ti = t["ti"]
                cs = slice(HP + t["off"], HP + t["off"] + t["n"])
                self.stt("dve", self.Gv(16 + g, t), F[:, S, cs], 1.0 / win, F[:, g, cs], ALU.mult, ALU.subtract,
                         [Fc[S], Fc[g]], [Gc[16 + g][ti]])
            if first:
                w1_ = win - 1
                a = self.next_ta()
                self.tt("dve", TA[:, a, 0:w1_], F[:, S, HP:HP + w1_], self.RC0[:, g, 0:w1_], ALU.mult,
                        [Fc[S], self.constc], [TAc[a]])
                self.tt("dve", G[:, 16 + g, 0:w1_], TA[:, a, 0:w1_], F[:, g, HP:HP + w1_], ALU.subtract,
                        [TAc[a], Fc[g]], [Gc[16 + g][0]])
            self.cp("act", self.CA[:, l, g, :], F[:, g, c_hi - 15:c_hi], [Fc[g]], [self.CAc[l][g]])
            if st is not None:
                a = self.next_ta()
                tmp = TA[:, a, 0:NS]
                self.s.op("dve", [self.SPOOLc], [TAc[a]],
                          lambda e, g=g, win=win, tmp=tmp: e.tensor_reduce(out=tmp, in_=self.SPOOL[:, g, :, 16 - win:15],
                                                                          axis=AX.X, op=ALU.add))
                self.tt("dve", tmp, tmp, self.Fv(g, st), ALU.add, [TAc[a], Fc[g]], [TAc[a]])
                self.stt("dve", self.Gv(16 + g, st), tmp, 1.0 / win, self.Fv(g, st), ALU.mult, ALU.subtract,
                         [TAc[a], Fc[g]], [Gc[16 + g][st["ti"]]])
        if st is not None:
            self.out_rows([self.Fv(j, st) for j in range(4)], NS, o["pool_s"][l, :, 14, :], [Fc[j] for j in range(4)])
        for g in range(4):
            for t in tiles:
                n, ti = t["n"], t["ti"]
                b = self.mm([(self.POOLW[:, l, g, :], self.Gv(16 + g, t))], n, [self.POOLWc, Gc[16 + g][ti]])
                self.act(self.Gv(g, t), PS[:, b, 0:n], AF.Identity, [bankc[b], self.PARc], [Gc[g][ti]],
                         scale=self.par(l, 96 + g))

        self.s.stage = 'mix.B'
        role(1, lambda j, t, b: self.cp("act", self.Fv(j, t), PS[:, b, 0:t["n"]], [bankc[b]], [Fc[j]]))
        role(2, lambda j, t, b: self.cp("act", self.Fv(4 + j, t), PS[:, b, 0:t["n"]], [bankc[b]], [Fc[4 + j]]))
        for t in tiles:
            ti = t["ti"]

            self.layernorm(t, 4, self.F3(4, 4, t), [Fc[4 + c] for c in range(4)],
                           self.G3(12, 4, t), [Gc[12 + c][ti] for c in range(4)],
                           self.G3(16, 4, t), [Gc[16 + c][ti] for c in range(4)], EPS, l, 100, 104,
                           crit=lambda c, t=t, ti=ti: (self.Gv(8 + c, t), AF.Identity, [Gc[8 + c][ti]]),
                           post=lambda c, t=t: (self.Fv(4 + c, t), AF.Identity, [Fc[4 + c]]),
                           chunk_cells=[Fc[4 + c] for c in range(4)])
        for t in ptiles:
            ti, n = t["ti"], t["n"]
            for c in range(4):
                pb = c % 2
                for h in range(4):
                    vin = self.G[:, 8 + h, t["off"] + c * 128:t["off"] + (c + 1) * 128]
                    outp = self.PSB[:, pb, h * 128:(h + 1) * 128]
                    self.s.op("pe", [Gc[8 + h][ti], self.constc], [self.PSBc[pb]],
                              lambda e, vin=vin, outp=outp: e.transpose(out=outp, in_=vin, identity=self.IDB[:, :]))
                self.cp("act", self.G[:, 20 + c, 0:512], self.PSB[:, pb, :], [self.PSBc[pb]], [Gc[20 + c][0]])
            for h in range(4):
                b = self.next_bank()

                def fn(e, b=b, h=h):
                    ins = None
                    for c in range(4):
                        ins = e.matmul(PS[:, b, c * 128:(c + 1) * 128], lhsT=self.G[:, 20 + c, h * 128:(h + 1) * 128],
                                       rhs=self.WT[:, l, h, :], start=True, stop=True)
                    return ins

                self.s.op("pe", [Gc[20 + c][0] for c in range(4)] + [self.WTc], [bankc[b]], fn)
                a = self.next_ta()
                self.tt("dve", TA[:, a, :].rearrange("p (c t) -> p c t", c=4),
                        PS[:, b, :].rearrange("p (c t) -> p c t", c=4),
                        self.GB[:, l, h * 128:(h + 1) * 128].unsqueeze(1).broadcast_to([128, 4, 128]), ALU.add,
                        [bankc[b], self.GBc], [TAc[a]])
                self.tt("dve", self.Gv(4 + h, t), TA[:, a, 0:n], self.Fv(h, t), ALU.mult, [TAc[a], Fc[h]], [Gc[4 + h][ti]])
        if st is not None:
            self.out_rows([self.Fv(4 + j, st) for j in range(4)], NS, o["gv_s"][l, :, :], [Fc[4 + j] for j in range(4)])
            for h in range(4):
                a = self.next_ta()
                self.ts("dve", TA[:, a, 0:NS], self.Fv(4 + h, st), self.GS[:, l, h:h + 1], self.GS[:, l, 4 + h:5 + h],
                        ALU.mult, ALU.add, [Fc[4 + h], self.GBc], [TAc[a]])
                self.tt("dve", self.Gv(4 + h, st), TA[:, a, 0:NS], self.Fv(h, st), ALU.mult, [TAc[a], Fc[h]],
                        [Gc[4 + h][st["ti"]]])

        self.s.stage = 'mix.C'
        for j in range(4):
            self.cp("act", F[:, j, HP - 30:HP], self.CC[:, l, j, :], [self.CCc[l][j]], [Fc[j]])
        role(3, lambda j, t, b: self.cp("act", self.Fv(j, t), PS[:, b, 0:t["n"]], [bankc[b]], [Fc[j]]))

        def ev_glu(j, t, b):
            a = self.next_ta()
            n = t["n"]
            self.act(TA[:, a, 0:n], PS[:, b, 0:n], AF.Sigmoid, [bankc[b]], [TAc[a]])
            self.tt("dve", self.Fv(j, t), self.Fv(j, t), TA[:, a, 0:n], ALU.mult, [Fc[j], TAc[a]], [Fc[j]])

        role(4, ev_glu)
        for j in range(4):
            self.cp("act", self.CC[:, l, j, :], F[:, j, c_hi - 30:c_hi], [Fc[j]], [self.CCc[l][j]])
            self.cp("act", self.EXTB[:, j, 0:30 + npg], F[:, j, c_lo - 30:c_hi], [Fc[j]], [self.EXTBc[j]])
            g0_ = 8 + 8 * (j % 2)
            dgv = G[:, g0_:g0_ + 8, :].rearrange("p a b -> p (a b)")[:, 0:31 * 128].rearrange("p (k c) -> p k c", c=128)
            dg_cells = [Gc[c][t_["ti"]] for c in range(g0_, g0_ + 8) for t_ in tiles]
            self.tt("dve", dgv, self.IDB[:, :].unsqueeze(1).broadcast_to([128, 31, 128]),
                    self.DWT[:, l, j, :].unsqueeze(2).broadcast_to([128, 31, 128]), ALU.mult,
                    [self.constc, self.DWTc], dg_cells)
            for t in ptiles:
                n = t["n"]
                b = self.mm([(dgv[:, k, :], self.EXTB[:, j, t["off"] + k:t["off"] + k + n]) for k in range(31)], n,
                            [self.EXTBc[j]] + dg_cells)
                self.act(self.Fv(4 + j, t), PS[:, b, 0:n], AF.Identity, [bankc[b], self.PARc], [Fc[4 + j]],
                         bias=self.par(l, 108 + j), scale=1.0)
            if st is not None:
                self.cp("act", self.SCONV[:, j, :, 30], self.Fv(j, st), [Fc[j]], [self.SCONVc[j]])
                a = self.next_ta()
                pr = TA[:, a, 0:NS * 31].rearrange("p (b r) -> p b r", r=31)
                self.tt("dve", pr, self.SCONV[:, j, :, :], self.DWT[:, l, j, :].unsqueeze(1).broadcast_to([128, NS, 31]),
                        ALU.mult, [self.SCONVc[j], self.DWTc], [TAc[a]])
                a2 = self.next_ta()
                tmp = TA[:, a2, 0:NS]
                self.s.op("dve", [TAc[a]], [TAc[a2]],
                          lambda e, pr=pr, tmp=tmp: e.tensor_reduce(out=tmp, in_=pr, axis=AX.X, op=ALU.add))
                self.ts("dve", self.Fv(4 + j, st), tmp, self.par(l, 108 + j), None, ALU.add, None,
                        [TAc[a2], self.PARc], [Fc[4 + j]])
        if st is not None:
            self.out_rows([self.Fv(j, st) for j in range(4)], NS, o["conv_s"][l, :, 29, :], [Fc[j] for j in range(4)])
        for t in tiles:
            ti = t["ti"]

            self.layernorm(t, 4, self.F3(4, 4, t), [Fc[4 + c] for c in range(4)],
                           self.G3(12, 4, t), [Gc[12 + c][ti] for c in range(4)],
                           self.G3(16, 4, t), [Gc[16 + c][ti] for c in range(4)], EPS, l, 112, 116,
                           crit=lambda c, t=t, ti=ti: (self.Gv(8 + c, t), AF.Silu, [Gc[8 + c][ti]]),
                           chunk_cells=[Fc[4 + c] for c in range(4)])

        self.s.stage = 'mix.D'
        role(5, lambda j, t, b: self.cp("act", self.Fv(j, t), PS[:, b, 0:t["n"]], [bankc[b]], [Fc[j]]))
        for j in range(4):
            self.cp("act", F[:, 4 + j, HP - 2:HP], self.CD[:, l, j, :], [self.CDc[l][j]], [Fc[4 + j]])
        role(6, lambda j, t, b: self.cp("act", self.Fv(4 + j, t), PS[:, b, 0:t["n"]], [bankc[b]], [Fc[4 + j]]))
        role(7, lambda j, t, b: self.tt("dve", self.Fv(4 + j, t), self.Fv(4 + j, t), PS[:, b, 0:t["n"]], ALU.mult,
                                        [Fc[4 + j], bankc[b]], [Fc[4 + j]]))
        for j in range(4):
            self.cp("act", self.CD[:, l, j, :], F[:, 4 + j, c_hi - 2:c_hi], [Fc[4 + j]], [self.CDc[l][j]])
            for t in ptiles:
                ti, n = t["ti"], t["n"]
                c0 = HP + t["off"]
                a = self.next_ta()
                y = TA[:, a, 0:n]
                self.ts("dve", y, F[:, 4 + j, c0 - 2:c0 - 2 + n], self.SCW[:, l, j, 0:1], None, ALU.mult, None,
                        [Fc[4 + j], self.DWTc], [TAc[a]])
                for k in (1, 2):
                    self.stt("dve", y, F[:, 4 + j, c0 - 2 + k:c0 - 2 + k + n], self.SCW[:, l, j, k:k + 1], y,
                             ALU.mult, ALU.add, [Fc[4 + j], TAc[a], self.DWTc], [TAc[a]])
                self.tt("dve", self.Gv(12 + j, t), y, self.Fv(j, t), ALU.mult, [TAc[a], Fc[j]], [Gc[12 + j][ti]])
            if st is not None:
                self.cp("act", self.SSC[:, j, :, 2], self.Fv(4 + j, st), [Fc[4 + j]], [self.SSCc[j]])
                a = self.next_ta()
                pr = TA[:, a, 0:NS * 3].rearrange("p (b r) -> p b r", r=3)
                self.tt("dve", pr, self.SSC[:, j, :, :], self.SCW[:, l, j, :].unsqueeze(1).broadcast_to([128, NS, 3]),
                        ALU.mult, [self.SSCc[j], self.DWTc], [TAc[a]])
                a2 = self.next_ta()
                tmp = TA[:, a2, 0:NS]
                self.s.op("dve", [TAc[a]], [TAc[a2]],
                          lambda e, pr=pr, tmp=tmp: e.tensor_reduce(out=tmp, in_=pr, axis=AX.X, op=ALU.add))
                self.tt("dve", self.Gv(12 + j, st), tmp, self.Fv(j, st), ALU.mult, [TAc[a2], Fc[j]],
                        [Gc[12 + j][st["ti"]]])
        if st is not None:
            self.out_rows([self.Fv(4 + j, st) for j in range(4)], NS, o["sc_s"][l, :, 1, :], [Fc[4 + j] for j in range(4)])

        self.s.stage = 'mix.gate'
        for j in range(8):
            ga = self.wtake("gate", l, 2 * j)
            gb_ = self.wtake("gate", l, 2 * j + 1)
            ps_ = self.wtake("proj", l, j, keep=2)
            for t in tiles:
                n, ti = t["n"], t["ti"]
                tas = []
                for bi in range(4):
                    gs = ga if bi < 2 else gb_
                    go = (bi % 2) * 128
                    bg = self.mm([(self.WS[:, gs.slot, k * 256 + go:k * 256 + go + 128], self.XBv(k, t))
                                  for k in range(KD)], n, [gs.cell, self.XBc[ti]])
                    a = self.next_ta()
                    tas.append(a)
                    self.act(TA[:, a, 0:n], PS[:, bg, 0:n], AF.Sigmoid, [bankc[bg], self.PARc], [TAc[a]],
                             bias=self.par(l, 64 + bi * 8 + j), scale=1.0)
                for bi in range(4):
                    by = self.mm([(self.WS[:, ps_.slot, bi * 512 + k * 128:bi * 512 + k * 128 + 128], self.Gv(4 * bi + k, t))
                                  for k in range(4)], n, [ps_.cell] + [Gc[4 * bi + k][ti] for k in range(4)])
                    a = tas[bi]
                    self.tt("dve", TA[:, a, 0:n], TA[:, a, 0:n], PS[:, by, 0:n], ALU.mult, [TAc[a], bankc[by]], [TAc[a]])
                a0, a1, a2, a3 = tas
                self.tt("dve", TA[:, a0, 0:n], TA[:, a0, 0:n], TA[:, a1, 0:n], ALU.add, [TAc[a0], TAc[a1]], [TAc[a0]])
                self.tt("dve", TA[:, a2, 0:n], TA[:, a2, 0:n], TA[:, a3, 0:n], ALU.add, [TAc[a2], TAc[a3]], [TAc[a2]])
                self.tt("dve", self.Gv(16 + j, t), TA[:, a0, 0:n], TA[:, a2, 0:n], ALU.add, [TAc[a0], TAc[a2]],
                        [Gc[16 + j][ti]])
        self.s.stage = 'mix.wo'
        for i in range(4):
            so = self.wtake("w_o", l, i)
            for c in range(2):
                dc = i * 2 + c
                for t in tiles:
                    n, ti = t["n"], t["ti"]
                    b = self.mm([(self.wap(so, k, c * 128, 128), self.Gv(16 + k, t)) for k in range(KD)], n,
                                [so.cell] + [Gc[16 + k][ti] for k in range(KD)])
                    self.stt("dve", self.Xv(dc, t), self.Xv(dc, t), ALPHA, PS[:, b, 0:n], ALU.mult, ALU.add,
                             [self.Xc[ti], bankc[b]], [self.Xc[ti]])
                    self.ln_pre(dc, t)
        self.s.stage = 'mix.ln2'
        self.ln_x(tiles, l, 1, EPS, pre=True)

    def attn(self, gi, tiles, l):
        G, TA, PS = self.G, self.TA, self.PS
        Gc, TAc, bankc = self.Gc, self.TAc, self.bankc
        ptiles = [t for t in tiles if t["kind"] == "p"]
        st = [t for t in tiles if t["kind"] == "s"]
        st = st[0] if st else None
        d = self.d
        self.s.stage = 'attn.q'
        for i in range(4):
            sq = self.wtake("xa_wq", l, i)
            for c in range(2):
                dc = i * 2 + c
                for t in ptiles:
                    n, ti = t["n"], t["ti"]
                    b = self.mm([(self.wap(sq, k, c * 128, 128), self.XBv(k, t)) for k in range(KD)], n,
                                [sq.cell, self.XBc[ti]])
                    self.cp("act", self.Gv(dc, t), PS[:, b, 0:n], [bankc[b]], [Gc[dc][ti]])
            if st is not None:
                b = self.mm([(self.XBv(k, st), self.wap(sq, k, 0, 256)) for k in range(KD)], 256,
                            [sq.cell, self.XBc[st["ti"]]], m=NS)
                self.cp("act", self.QS[0:NS, i * 256:(i + 1) * 256], PS[0:NS, b, 0:256], [bankc[b]], [self.QSc])
        self.s.stage = 'attn.core'
        scale = 256.0 ** -0.5
        for t in ptiles:
            n, ti = t["n"], t["ti"]
            for h in range(4):
                for mt in range(2):
                    b = self.mm([(self.KT[:, l, 2 * h + dd, mt * 128:(mt + 1) * 128], self.Gv(2 * h + dd, t)) for dd in range(2)],
                                n, [self.KTc[l], Gc[2 * h][ti], Gc[2 * h + 1][ti]])
                    self.act(self.Gv(8 + 2 * h + mt, t), PS[:, b, 0:n], AF.Exp, [bankc[b]], [Gc[8 + 2 * h + mt][ti]],
                             scale=scale)
                b = self.mm([(self.ONESB[:, :], self.Gv(8 + 2 * h + mt, t)) for mt in range(2)], n,
                            [self.constc, Gc[8 + 2 * h][ti], Gc[9 + 2 * h][ti]])
                a = self.next_ta()
                self.s.op("dve", [bankc[b]], [TAc[a]],
                          lambda e, a=a, b=b, n=n: e.reciprocal(out=TA[:, a, 0:n], in_=PS[:, b, 0:n]))
                for c in range(2):
                    bo = self.mm([(self.VB[:, l, mt, h * 256 + c * 128:h * 256 + (c + 1) * 128], self.Gv(8 + 2 * h + mt, t))
                                  for mt in range(2)], n, [self.VBc[l], Gc[8 + 2 * h][ti], Gc[9 + 2 * h][ti]])
                    self.tt("dve", self.Gv(16 + 2 * h + c, t), PS[:, bo, 0:n], TA[:, a, 0:n], ALU.mult,
                            [bankc[bo], TAc[a]], [Gc[16 + 2 * h + c][ti]])
        if st is not None:
            sti = st["ti"]
            selv = self.ST[0:NS, 0:4, :].rearrange("p a (b m) -> p (a b) m", m=128)
            self.cp("dve", selv, self.IDF[0:NS, 0:NS].unsqueeze(2).broadcast_to([NS, NS, 128]), [self.constc],
                    [self.STc[0], self.STc[1], self.STc[2], self.STc[3]])
            for bsm in range(NS):
                def fnq(e, bsm=bsm):
                    e.matmul(PS[:, 5, :], lhsT=selv[:, bsm, :], rhs=self.QS[0:NS, 0:512], start=True, stop=True)
                    return e.matmul(PS[:, 6, :], lhsT=selv[:, bsm, :], rhs=self.QS[0:NS, 512:1024], start=True, stop=True)

                self.s.op("pe", [self.QSc, self.STc[0], self.STc[1], self.STc[2], self.STc[3]], [bankc[5], bankc[6]], fnq)
                qb = PS[:, 5:7, :].rearrange("p a c -> p (a c)")
                for mt in range(2):
                    kv = self.kv_i
                    self.kv_i = (self.kv_i + 1) % NKV
                    self.dma("sp", "kv%d" % kv, self.KVS[:, kv, :], d["cache_mem_k"][l, bsm, mt * 128:(mt + 1) * 128, :],
                             [], [self.KVSc[kv]])
                    prod = TA[:, 0:2, :].rearrange("p a c -> p (a c)")
                    self.tt("dve", prod, self.KVS[:, kv, :], qb, ALU.mult, [self.KVSc[kv], bankc[5], bankc[6]],
                            [TAc[0], TAc[1]])
                    pr4 = TA[:, 0:2, :].rearrange("p a (m c) -> p (a m) c", m=2)
                    c0 = bsm * 8 + mt * 4
                    self.s.op("dve", [TAc[0], TAc[1]], [self.SSc],
                              lambda e, c0=c0, pr4=pr4: e.tensor_reduce(out=self.SS[:, c0:c0 + 4], in_=pr4,
                                                                        axis=AX.X, op=ALU.add))
            self.act(self.ES[:, :], self.SS[:, :], AF.Exp, [self.SSc], [self.ESc], scale=scale)
            b = self.mm([(self.ONESF[:, :], self.ES[:, :])], 128, [self.constc, self.ESc])
            a = self.next_ta()
            self.cp("act", TA[:, a, 0:128], PS[:, b, 0:128], [bankc[b]], [TAc[a]])
            den = TA[:, a, 0:128].rearrange("p (b m h) -> p m h b", m=2, h=4)
            self.tt("dve", self.RS[:, :, :], den[:, 0, :, :], den[:, 1, :, :], ALU.add, [TAc[a]], [self.RSc])
            self.s.op("dve", [self.RSc], [self.RSc], lambda e: e.reciprocal(out=self.RS[:, :, :], in_=self.RS[:, :, :]))
            bo2 = (self.next_bank(), self.next_bank())
            for bsm in range(NS):
                for mt in range(2):
                    kv = self.kv_i
                    self.kv_i = (self.kv_i + 1) % NKV
                    self.dma("sp", "kv%d" % kv, self.KVS[:, kv, :], d["cache_mem_v"][l, bsm, mt * 128:(mt + 1) * 128, :],
                             [], [self.KVSc[kv]])

                    def fnv(e, bsm=bsm, kv=kv, bo2=bo2, mt=mt):
                        ins = None
                        for c in range(8):
                            col = bsm * 8 + mt * 4 + c // 2
                            ins = e.matmul(PS[:, bo2[mt], c * NS + bsm:c * NS + bsm + 1],
                                           lhsT=self.KVS[:, kv, c * 128:(c + 1) * 128],
                                           rhs=self.ES[:, col:col + 1], start=True, stop=True)
                        return ins

                    self.s.op("pe", [self.KVSc[kv], self.ESc], [bankc[bo2[mt]]], fnv)
            for h in range(4):
                rs_b = self.RS[:, h, :].unsqueeze(1).broadcast_to([128, 2, NS])
                tmps = []
                for mt in range(2):
                    a = self.next_ta()
                    tv = TA[:, a, 0:2 * NS].rearrange("p (c b) -> p c b", c=2)
                    self.tt("dve", tv, PS[:, bo2[mt], 2 * h * NS:(2 * h + 2) * NS].rearrange("p (c b) -> p c b", c=2),
                            rs_b, ALU.mult, [bankc[bo2[mt]], self.RSc], [TAc[a]])
                    tmps.append((a, tv))
                self.tt("dve", G[:, 16 + 2 * h:18 + 2 * h, st["off"]:st["off"] + NS], tmps[0][1], tmps[1][1], ALU.add,
                        [TAc[tmps[0][0]], TAc[tmps[1][0]]], [Gc[16 + 2 * h][sti], Gc[17 + 2 * h][sti]])
        self.s.stage = 'attn.wo'
        for i in range(4):
            so = self.wtake("xa_wo", l, i)
            for c in range(2):
                dc = i * 2 + c
                for t in tiles:
                    n, ti = t["n"], t["ti"]
                    b = self.mm([(self.wap(so, k, c * 128, 128), self.Gv(16 + k, t)) for k in range(KD)], n,
                                [so.cell] + [Gc[16 + k][ti] for k in range(KD)])
                    self.stt("dve", self.Xv(dc, t), self.Xv(dc, t), ALPHA, PS[:, b, 0:n], ALU.mult, ALU.add,
                             [self.Xc[ti], bankc[b]], [self.Xc[ti]])
                    self.ln_pre(dc, t)
        self.s.stage = 'attn.ln3'
        self.ln_x(tiles, l, 2, EPS, pre=True)

    def setup(self):
        d, o = self.d, self.o
        PS, bankc = self.PS, self.bankc
        cst = self.constc
        pool_ops = []

        def P(fn, rd=False):
            self.s.op("pool", [cst] if rd else [], [cst], fn)

        P(lambda e: e.memset(self.IDF[:, :], 0.0))
        P(lambda e: e.affine_select(out=self.IDF[:, :], in_=self.IDF[:, :], pattern=[[-1, 128]], compare_op=ALU.not_equal,
                                    fill=1.0, base=0, channel_multiplier=1), rd=True)
        P(lambda e: e.tensor_copy(out=self.IDB[:, :], in_=self.IDF[:, :]), rd=True)
        P(lambda e: e.memset(self.ONESB[:, :], 1.0))
        P(lambda e: e.memset(self.ONESF[:, :], 1.0))
        for g in range(4):
            win = WINS[g]
            P(lambda e, g=g, win=win: e.memset(self.RC0[:, g, :], 1.0 / win))
            for tcol in range(win - 1):
                P(lambda e, g=g, tcol=tcol: e.memset(self.RC0[:, g, tcol:tcol + 1], 1.0 / (tcol + 1)))
        for l in range(L):
            for j in range(4):
                self.s.op("pool", [], [self.CAc[l][j]], lambda e, l=l, j=j: e.memset(self.CA[:, l, j, :], 0.0))
                self.s.op("pool", [], [self.CCc[l][j]], lambda e, l=l, j=j: e.memset(self.CC[:, l, j, :], 0.0))
                self.s.op("pool", [], [self.CDc[l][j]], lambda e, l=l, j=j: e.memset(self.CD[:, l, j, :], 0.0))
        TAc, STc = self.TAc, self.STc
        for l in range(L):
            r = 0
            for nm in ("ln1_g", "ln1_b", "ln2_g", "ln2_b", "ln3_g", "ln3_b", "ln4_g", "ln4_b"):
                self.setup_dma("sp", self.PRAW[r:r + 8, l, :], d[nm][l].rearrange("(k p) -> k p", p=128), self.PRAWc)
                r += 8
            self.setup_dma("sp", self.PRAW[r:r + 32, l, :], d["b_gate"][l].rearrange("(k p) -> k p", p=128), self.PRAWc)
            r += 32
            for nm in ("pool_scale", "gmlp_ln_g", "gmlp_ln_b", "conv_db", "conv_ln_g", "conv_ln_b"):
                self.setup_dma("sp", self.PRAW[r:r + 4, l, :], d[nm][l].rearrange("(k p) -> k p", p=128), self.PRAWc)
                r += 4
            assert r == 120
            self.setup_dma("sp", self.TA[0:31, l, :], d["conv_dw"][l], TAc[l])
            self.setup_dma("sp", self.TA[32:35, l, :], d["sc_w"][l], TAc[l])
            self.setup_dma("sp", self.ST[:, l, :].rearrange("p (h s) -> p h s", h=4),
                           d["gmlp_ws"][l].rearrange("h t s -> t h s"), STc[l])
            self.setup_dma("sp", self.GB[:, l, :], d["gmlp_b"][l:l + 1].rearrange("o h t -> o (h t)").partition_broadcast(128),
                           self.GBc)
            self.setup_dma("sp", self.GS[:, l, 0:4],
                           d["gmlp_ws"][l:l + 1, :, 0, 0:1].rearrange("o h a -> o (h a)").partition_broadcast(128),
                           self.GBc, slow=True)
            self.setup_dma("sp", self.GS[:, l, 4:8],
                           d["gmlp_b"][l:l + 1, :, 0:1].rearrange("o h a -> o (h a)").partition_broadcast(128),
                           self.GBc, slow=True)
            self.setup_dma("sp", self.F[:, l, 0:512].rearrange("p (g dd) -> p g dd", g=4),
                           d["pool_w"][l].rearrange("g c dd -> c g dd"), self.Fc[l])
        self.setup_dma("sp", self.STG[:, :, :], d["mem_prompt"].rearrange("(mt p) dd -> p mt dd", p=128), self.STGc[0])
        self.setup_cells.append(self.STGc[1])
        tot = self.s.count["setup"]
        for c in self.setup_cells:
            c.w = ("setup", tot)
            c.r = {}
        for l in range(L):
            self.cp("act", self.POOLW[:, l, :, :], self.F[:, l, 0:512].rearrange("p (g dd) -> p g dd", g=4),
                    [self.Fc[l]], [self.POOLWc])
        for l in range(L):
            for h in range(4):
                wv = self.ST[:, l, h * 128:(h + 1) * 128]
                self.s.op("pool", [STc[l]], [STc[l]],
                          lambda e, wv=wv: e.affine_select(out=wv, in_=wv, pattern=[[-1, 128]], compare_op=ALU.is_ge,
                                                           fill=0.0, base=0, channel_multiplier=1))
        for l in range(L):
            b = self.next_bank()
            self.tr_to_bank(self.PRAW[0:120, l, :], 120, 128, b, 0, [self.PRAWc])
            self.cp("act", self.PAR[:, l, 0:120], PS[:, b, 0:120], [bankc[b]], [self.PARc])
            for j in range(4):
                b = self.next_bank()
                self.tr_to_bank(self.TA[0:31, l, j * 128:(j + 1) * 128], 31, 128, b, 0, [TAc[l]])
                self.cp("act", self.DWT[:, l, j, :], PS[:, b, 0:31], [bankc[b]], [self.DWTc])
                b = self.next_bank()
                self.tr_to_bank(self.TA[32:35, l, j * 128:(j + 1) * 128], 3, 128, b, 0, [TAc[l]], p0=32)
                self.cp("act", self.SCW[:, l, j, :], PS[:, b, 0:3], [bankc[b]], [self.DWTc])
            for h in range(4):
                wv = self.ST[:, l, h * 128:(h + 1) * 128]
                b = self.next_bank()
                self.tr_to_bank(wv, 128, 128, b, 0, [STc[l]])
                self.cp("act", self.WT[:, l, h, :], PS[:, b, 0:128], [bankc[b]], [self.WTc])
        for k in range(KD):
            b = self.next_bank()
            for mt in range(2):
                self.tr_to_bank(self.STG[:, mt, k * 128:(k + 1) * 128], 128, 128, b, mt * 128, [self.STGc[0], self.STGc[1]])
            self.cp("act", self.G[:, k, 0:NMEM], PS[:, b, 0:256], [bankc[b]], [self.Gc[k][0]])
        for l in range(L):
            for (nm, okey) in (("xa_wk", "mk_p"), ("xa_wv", "mv_p")):
                for i in range(4):
                    sl = self.wtake(nm, l, i)
                    for mt in range(2):
                        b = self.mm([(self.G[:, k, mt * 128:(mt + 1) * 128], self.wap(sl, k, 0, 256)) for k in range(KD)],
                                    256, [self.Gc[k][0] for k in range(KD)] + [sl.cell])
                        sg = self.next_stg()
                        self.cp("act", self.STG[:, sg, 0:256], PS[:, b, 0:256], [bankc[b]], [self.STGc[sg]])
                        self.dma("sp", "stg%d" % sg, o[okey][l, mt * 128:(mt + 1) * 128, i * 256:(i + 1) * 256],
                                 self.STG[:, sg, 0:256], [self.STGc[sg]], [])
                        if nm == "xa_wv":
                            self.cp("dve", self.VB[:, l, mt, i * 256:(i + 1) * 256], PS[:, b, 0:256], [bankc[b]], [self.VBc[l]])
                    if nm == "xa_wk":
                        for c in range(2):
                            dc = i * 2 + c
                            b = self.mm([(self.wap(sl, k, c * 128, 128), self.G[:, k, 0:NMEM]) for k in range(KD)], NMEM,
                                        [self.Gc[k][0] for k in range(KD)] + [sl.cell])
                            self.cp("dve", self.KT[:, l, dc, :], PS[:, b, 0:NMEM], [bankc[b]], [self.KTc[l]])

    def load_x(self, gi, tiles):
        d = self.d
        PS, bankc = self.PS, self.bankc
        for t in tiles:
            ti = t["ti"]
            if t["kind"] == "p":
                for tb in range(t["n"] // 128):
                    sg = self.next_stg()
                    r0 = t["seq0"] + tb * 128
                    self.dma("sp", "stg%d" % sg, self.STG[:, sg, :], d["x_prompt"][r0:r0 + 128, :], [], [self.STGc[sg]])
                    for half in range(2):
                        b = self.next_bank()
                        for c in range(4):
                            k = half * 4 + c
                            self.tr_to_bank(self.STG[:, sg, k * 128:(k + 1) * 128], 128, 128, b, c * 128, [self.STGc[sg]])
                        col = t["off"] + tb * 128
                        self.cp("act", self.X[:, half * 4:half * 4 + 4, col:col + 128],
                                PS[:, b, :].rearrange("p (c t) -> p c t", c=4), [bankc[b]], [self.Xc[ti]])
                        self.cp("dve", self.XB[:, half * 4:half * 4 + 4, col:col + 128],
                                self.X[:, half * 4:half * 4 + 4, col:col + 128], [self.Xc[ti]], [self.XBc[ti]])
            else:
                sg = self.next_stg()
                self.dma("sp", "stg%d" % sg, self.STG[0:NS, sg, :], d["x_sample"], [], [self.STGc[sg]])
                b = self.next_bank()
                for k in range(KD):
                    self.tr_to_bank(self.STG[0:NS, sg, k * 128:(k + 1) * 128], NS, 128, b, k * NS, [self.STGc[sg]])
                self.cp("act", self.X3(t), PS[:, b, 0:KD * NS].rearrange("p (k t) -> p k t", k=KD), [bankc[b]], [self.Xc[ti]])
                self.cp("dve", self.XB3(t), self.X3(t), [self.Xc[ti]], [self.XBc[ti]])

    def store_y(self, gi, tiles):
        o = self.o
        PS, bankc = self.PS, self.bankc
        for t in tiles:
            ti = t["ti"]
            if t["kind"] == "p":
                for tb in range(t["n"] // 128):
                    sg = self.next_stg()
                    col = t["off"] + tb * 128
                    for half in range(2):
                        b = self.next_bank()
                        for c in range(4):
                            k = half * 4 + c
                            self.tr_to_bank(self.X[:, k, col:col + 128], 128, 128, b, c * 128, [self.Xc[ti]])
                        self.cp("act" if half == 0 else "dve", self.STG[:, sg, half * 512:(half + 1) * 512], PS[:, b, :],
                                [bankc[b]], [self.STGc[sg]])
                    r0 = t["seq0"] + tb * 128
                    self.dma("sp", "stg%d" % sg, o["y_prompt"][r0:r0 + 128, :], self.STG[:, sg, :], [self.STGc[sg]], [])
            else:
                sg = self.next_stg()
                for half in range(2):
                    b = self.next_bank()
                    for c in range(4):
                        k = half * 4 + c
                        self.tr_to_bank(self.Xv(k, t), 128, NS, b, c * 128, [self.Xc[ti]])
                    self.cp("act", self.STG[0:NS, sg, half * 512:(half + 1) * 512], PS[0:NS, b, :], [bankc[b]], [self.STGc[sg]])
                self.dma("sp", "stg%d" % sg, o["y_sample"], self.STG[0:NS, sg, :], [self.STGc[sg]], [])

    def final_outputs(self):
        o = self.o
        for l in range(L):
            self.out_rows([self.CA[:, l, j, :] for j in range(4)], 15, o["pool_p"][l], self.CAc[l])
            self.out_rows([self.CC[:, l, j, :] for j in range(4)], 30, o["conv_p"][l], self.CCc[l])
            self.out_rows([self.CD[:, l, j, :] for j in range(4)], 2, o["sc_p"][l], self.CDc[l])

    def build(self):
        nc = self.nc
        self.declare_dram()
        with ExitStack() as es:
            self.alloc(es)
            specs = self.specs_setup()
            for gi in range(NGRP):
                for l in range(L):
                    specs += self.specs_layer(l)
            self.wplan(specs)
            steps = [self.setup]
            for gi in range(NGRP):
                tiles = self.tiles_of(gi)
                steps.append(lambda gi=gi, tiles=tiles: (setattr(self.s, "group", gi), self.load_x(gi, tiles)))
                for l in range(L):
                    steps.append(lambda tiles=tiles, l=l: self.ffn(tiles, l, 0))
                    steps.append(lambda gi=gi, tiles=tiles, l=l: self.mixer(gi, tiles, l))
                    steps.append(lambda gi=gi, tiles=tiles, l=l: self.attn(gi, tiles, l))
                    steps.append(lambda tiles=tiles, l=l: self.ffn(tiles, l, 1))
                steps.append(lambda gi=gi, tiles=tiles: self.store_y(gi, tiles))
            steps.append(self.final_outputs)
            limit = self.step_limit if self.step_limit is not None else len(steps)
            for st_ in steps[:limit]:
                st_()
            if limit >= len(steps):
                assert self.wq_pos == len(self.wq_specs)
            self.emit(es)
        return nc

    def emit(self, es):
        nc = self.nc
        s = self.s
        semnames = list(ENGS) + s.dma_sems
        sems = {}
        for nm in semnames:
            sems[nm] = es.enter_context(nc.semaphore("s_" + nm))
        block = es.enter_context(nc.Block())
        final_waits = [(nm, s.count[nm]) for nm in s.dma_sems]

        def run(engname, e):
            if engname in ("dve", "act") and s.count.get("pool", 0):
                e.wait_ge(sems["pool"], s.count["pool"])
            for (need, fn, semkey, inc) in s.streams[engname]:
                for k, v in need.items():
                    e.wait_ge(sems[k], v)
                if inc:
                    fn(e).then_inc(sems[semkey], 1)
                else:
                    fn(e, sems[semkey])
            if engname == "sp":
                for nm, v in final_waits:
                    e.wait_ge(sems[nm], v)

        @block.tensor
        def _(e):
            run("pe", e)

        @block.scalar
        def _(e):
            run("act", e)

        @block.vector
        def _(e):
            run("dve", e)

        @block.gpsimd
        def _(e):
            run("pool", e)

        @block.sync
        def _(e):
            run("sp", e)


_WEIGHT_KEYS = ["ln1_g", "ln1_b", "ffn1_w1", "ffn1_w3", "ffn1_w2", "w_in", "w_gate", "b_gate", "pool_w", "pool_scale",
                "pool_proj", "gmlp_ln_g", "gmlp_ln_b", "gmlp_ws", "gmlp_b", "gmlp_proj", "conv_dw", "conv_db",
                "conv_ln_g", "conv_ln_b", "conv_proj", "sc_w", "sc_proj", "w_o", "ln2_g", "ln2_b", "xa_wq", "xa_wk",
                "xa_wv", "xa_wo", "ln3_g", "ln3_b", "ffn2_w1", "ffn2_w3", "ffn2_w2", "ln4_g", "ln4_b"]


def kernel(**inputs):
    inp = {k: np.ascontiguousarray(np.asarray(v), dtype=np.float32) for k, v in inputs.items()}
    nc = Builder().build()
    in_maps = []
    for c in range(NCORES):
        m = {k: inp[k] for k in _WEIGHT_KEYS}
        m["x_prompt"] = np.ascontiguousarray(inp["x_prompt"][c])
        m["x_sample"] = np.ascontiguousarray(inp["x_sample"][c * NS:(c + 1) * NS, 0, :])
        m["mem_prompt"] = np.ascontiguousarray(inp["mem_prompt"][c])
        m["state_pool"] = np.ascontiguousarray(inp["state_pool"][:, c * NS:(c + 1) * NS])
        m["state_conv"] = np.ascontiguousarray(inp["state_conv"][:, c * NS:(c + 1) * NS])
        m["state_shortconv"] = np.ascontiguousarray(inp["state_shortconv"][:, c * NS:(c + 1) * NS])
        m["cache_mem_k"] = np.ascontiguousarray(inp["cache_mem_k"][:, c * NS:(c + 1) * NS]).reshape(L, NS, NMEM, D)
        m["cache_mem_v"] = np.ascontiguousarray(inp["cache_mem_v"][:, c * NS:(c + 1) * NS]).reshape(L, NS, NMEM, D)
        in_maps.append(m)
    res = run_bass_kernel_spmd(nc, in_maps, core_ids=list(range(NCORES)))
    R = res.results
    y_prompt = np.stack([R[c]["y_prompt"] for c in range(NCORES)], 0)
    y_sample = np.concatenate([R[c]["y_sample"] for c in range(NCORES)], 0)[:, None, :]
    pool_p = np.stack([R[c]["pool_p"] for c in range(NCORES)], 1)
    conv_p = np.stack([R[c]["conv_p"] for c in range(NCORES)], 1)
    sc_p = np.stack([R[c]["sc_p"] for c in range(NCORES)], 1)
    mk_p = np.stack([R[c]["mk_p"] for c in range(NCORES)], 1).reshape(L, NCORES, NMEM, 4, 256)
    mv_p = np.stack([R[c]["mv_p"] for c in range(NCORES)], 1).reshape(L, NCORES, NMEM, 4, 256)
    pool_s = np.concatenate([R[c]["pool_s"] for c in range(NCORES)], 1)
    conv_s = np.concatenate([R[c]["conv_s"] for c in range(NCORES)], 1)
    sc_s = np.concatenate([R[c]["sc_s"] for c in range(NCORES)], 1)
    gv_s = np.concatenate([R[c]["gv_s"] for c in range(NCORES)], 1)[:, :, None, :]
    outs = (y_prompt, y_sample, pool_p, conv_p, sc_p, mk_p, mv_p, pool_s, conv_s, sc_s, gv_s)
    return tuple(np.ascontiguousarray(a, dtype=np.float32) for a in outs)
```

```python
import numpy as np
from contextlib import ExitStack
import concourse.bass as bass
import concourse.mybir as mybir
from concourse.bass_utils import run_bass_kernel_spmd

F32 = mybir.dt.float32
BF16 = mybir.dt.bfloat16
AF = mybir.ActivationFunctionType
ALU = mybir.AluOpType
AX = mybir.AxisListType

NCORES = 8
D = 1024
KD = 8
DFF = 2816
KF = 22
SEQ = 2048
NS = 16
L = 2
TILE = 512
TPG = 1
NGRP = (SEQ // TILE) // TPG
NPG = TPG * TILE
TG = NPG + NS
HP = 30
NMEM = 256
ALPHA = float((2.0 * L) ** 0.25)
EPS = 1e-5
WINS = (2, 4, 8, 16)
NSLOT = 6
NSTG = 3
NKV = 3
SLOT_E = 2048
NBANK = 7
GCH = 24
NTA = 6

ENGS = ("pe", "act", "dve", "pool", "sp")
STRICT_SAME_ENGINE = True
STRICT_FLAG = [True]


class Cell:
    __slots__ = ("w", "r", "name")

    def __init__(self, name=""):
        self.w = None
        self.r = {}
        self.name = name


class Sched:
    def __init__(self):
        self.streams = {e: [] for e in ENGS}
        self.count = {}
        self.waited = {e: {} for e in ENGS}
        self.dma_sems = []
        self.stage = ""
        self.group = -1
        self.labels = {}

    def _deps(self, eng, reads, writes):
        need = {}

        def add(dep, same_ok):
            if dep is None:
                return
            k, v = dep
            if k == eng and same_ok and not STRICT_SAME_ENGINE:
                return
            if self.waited[eng].get(k, 0) >= v:
                return
            if need.get(k, 0) < v:
                need[k] = v

        for c in reads:
            add(c.w, False)
        for c in writes:
            add(c.w, True)
            for k, v in c.r.items():
                add((k, v), True)
        for k, v in need.items():
            self.waited[eng][k] = v
        return need

    def op(self, eng, reads, writes, fn, after=()):
        if after:
            tmp = Cell("after")
            for c in after:
                if c.w is not None:
                    k, v = c.w
                    if tmp.r.get(k, 0) < v:
                        tmp.r[k] = v
                for k, v in c.r.items():
                    if tmp.r.get(k, 0) < v:
                        tmp.r[k] = v
            writes = list(writes) + [tmp]
            saved = STRICT_FLAG[0]
            STRICT_FLAG[0] = True
            need = self._deps(eng, reads, writes)
            STRICT_FLAG[0] = saved
            writes = writes[:-1]
        else:
            need = self._deps(eng, reads, writes)
        self.count[eng] = self.count.get(eng, 0) + 1
        v = self.count[eng]
        self.streams[eng].append((need, fn, eng, 1))
        self.labels.setdefault(eng, []).append("g%d.%s" % (self.group, self.stage))
        for c in reads:
            c.r[eng] = v
        for c in writes:
            c.w = (eng, v)
            c.r = {}

    def dma(self, q, sem, n, reads, writes, fn):
        if sem not in self.count:
            self.count[sem] = 0
            self.dma_sems.append(sem)
        need = self._deps(q, reads, writes)
        self.count[sem] += 16 * n
        v = self.count[sem]
        self.streams[q].append((need, fn, sem, 0))
        for c in reads:
            c.r[sem] = v
        for c in writes:
            c.w = (sem, v)
            c.r = {}


class Slab:
    def __init__(self, slot, cell, cw):
        self.slot = slot
        self.cell = cell
        self.cw = cw


class Builder:
    def __init__(self, step_limit=None):
        self.step_limit = step_limit
        self.nc = bass.Bass("TRN2", target_bir_lowering=False)
        self.s = Sched()
        self.bank_i = 0
        self.ta_i = 0
        self.stg_i = 0
        self.setup_cells = []

    def declare_dram(self):
        nc = self.nc

        def inp(name, shape):
            return nc.dram_tensor(name, list(shape), F32, kind="ExternalInput").ap()

        def outp(name, shape):
            return nc.dram_tensor(name, list(shape), F32, kind="ExternalOutput").ap()

        d = {}
        d["x_prompt"] = inp("x_prompt", (SEQ, D))
        d["x_sample"] = inp("x_sample", (NS, D))
        d["mem_prompt"] = inp("mem_prompt", (NMEM, D))
        d["state_pool"] = inp("state_pool", (L, NS, 15, 512))
        d["state_conv"] = inp("state_conv", (L, NS, 30, 512))
        d["state_shortconv"] = inp("state_shortconv", (L, NS, 2, 512))
        d["cache_mem_k"] = inp("cache_mem_k", (L, NS, NMEM, D))
        d["cache_mem_v"] = inp("cache_mem_v", (L, NS, NMEM, D))
        for nm in ("ln1_g", "ln1_b", "ln2_g", "ln2_b", "ln3_g", "ln3_b", "ln4_g", "ln4_b"):
            d[nm] = inp(nm, (L, D))
        for nm in ("ffn1_w1", "ffn1_w3", "ffn2_w1", "ffn2_w3"):
            d[nm] = inp(nm, (L, D, DFF))
        for nm in ("ffn1_w2", "ffn2_w2"):
            d[nm] = inp(nm, (L, DFF, D))
        d["w_in"] = inp("w_in", (L, D, 4096))
        d["w_gate"] = inp("w_gate", (L, D, 4096))
        d["b_gate"] = inp("b_gate", (L, 4096))
        d["pool_w"] = inp("pool_w", (L, 4, 128, 128))
        d["pool_scale"] = inp("pool_scale", (L, 512))
        for nm in ("pool_proj", "gmlp_proj", "conv_proj", "sc_proj"):
            d[nm] = inp(nm, (L, 512, D))
        d["gmlp_ln_g"] = inp("gmlp_ln_g", (L, 512))
        d["gmlp_ln_b"] = inp("gmlp_ln_b", (L, 512))
        d["gmlp_ws"] = inp("gmlp_ws", (L, 4, 128, 128))
        d["gmlp_b"] = inp("gmlp_b", (L, 4, 128))
        d["conv_dw"] = inp("conv_dw", (L, 31, 512))
        d["conv_db"] = inp("conv_db", (L, 512))
        d["conv_ln_g"] = inp("conv_ln_g", (L, 512))
        d["conv_ln_b"] = inp("conv_ln_b", (L, 512))
        d["sc_w"] = inp("sc_w", (L, 3, 512))
        for nm in ("w_o", "xa_wq", "xa_wk", "xa_wv", "xa_wo"):
            d[nm] = inp(nm, (L, D, D))
        o = {}
        o["y_prompt"] = outp("y_prompt", (SEQ, D))
        o["y_sample"] = outp("y_sample", (NS, D))
        o["pool_p"] = outp("pool_p", (L, 15, 512))
        o["conv_p"] = outp("conv_p", (L, 30, 512))
        o["sc_p"] = outp("sc_p", (L, 2, 512))
        o["mk_p"] = outp("mk_p", (L, NMEM, D))
        o["mv_p"] = outp("mv_p", (L, NMEM, D))
        o["pool_s"] = outp("pool_s", (L, NS, 15, 512))
        o["conv_s"] = outp("conv_s", (L, NS, 30, 512))
        o["sc_s"] = outp("sc_s", (L, NS, 2, 512))
        o["gv_s"] = outp("gv_s", (L, NS, 512))
        self.d = d
        self.o = o

    def alloc(self, es):
        nc = self.nc

        def sb(name, shape, dt):
            return es.enter_context(nc.sbuf_tensor(name, list(shape), dt))

        self.X = sb("X", (128, KD, TG), F32)
        self.XB = sb("XB", (128, KD, TG), BF16)
        self.G = sb("G", (128, GCH, TG), BF16)
        self.F = sb("F", (128, 8, HP + TG), F32)
        self.ST = sb("ST", (128, 5, 512), F32)
        self.TA = sb("TA", (128, NTA, 512), F32)
        self.WS = sb("WS", (128, NSLOT, SLOT_E), BF16)
        self.WST = sb("WST", (128, NSTG - 1, SLOT_E), F32)
        self.STG = sb("STG", (128, 2, 1024), F32)
        self.IDF = sb("IDF", (128, 128), F32)
        self.IDB = sb("IDB", (128, 128), BF16)
        self.ONESB = sb("ONESB", (128, 128), BF16)
        self.ONESF = sb("ONESF", (128, 128), F32)
        self.RC0 = sb("RC0", (128, 4, 16), F32)
        self.PRAW = sb("PRAW", (128, L, 128), F32)
        self.PAR = sb("PAR", (128, L, 128), F32)
        self.DWT = sb("DWT", (128, L, 4, 31), F32)
        self.SCW = sb("SCW", (128, L, 4, 3), F32)
        self.POOLW = sb("POOLW", (128, L, 4, 128), BF16)
        self.WT = sb("WT", (128, L, 4, 128), BF16)
        self.GB = sb("GB", (128, L, 512), F32)
        self.GS = sb("GS", (128, L, 8), F32)
        self.KT = sb("KT", (128, L, KD, NMEM), BF16)
        self.VB = sb("VB", (128, L, 2, D), BF16)
        self.CA = sb("CA", (128, L, 4, 15), F32)
        self.CC = sb("CC", (128, L, 4, 30), F32)
        self.CD = sb("CD", (128, L, 4, 2), F32)
        self.SPOOL = sb("SPOOL", (128, 4, NS, 15), F32)
        self.SCONV = sb("SCONV", (128, 4, NS, 31), F32)
        self.SSC = sb("SSC", (128, 4, NS, 3), F32)
        self.KVS = sb("KVS", (128, NKV, D), F32)
        self.kv_i = 0
        self.EXTB = sb("EXTB", (128, 4, HP + NPG + 2), BF16)
        self.QS = sb("QS", (16, D), F32)
        self.SS = sb("SS", (128, 128), F32)
        self.ES = sb("ES", (128, 128), F32)
        self.RS = sb("RS", (128, 4, NS), F32)
        self.PS = es.enter_context(nc.psum_tensor("PS", [128, NBANK, 512], F32))
        self.PSB = es.enter_context(nc.psum_tensor("PSB", [128, 2, 512], BF16))

        C = Cell
        self.Xc = [C("X%d" % i) for i in range(TPG + 1)]
        self.XBc = [C("XB%d" % i) for i in range(TPG + 1)]
        self.Gc = [[C("G%d_%d" % (c, i)) for i in range(TPG + 1)] for c in range(GCH)]
        self.Fc = [C("F%d" % j) for j in range(8)]
        self.STc = [C("ST%d" % j) for j in range(5)]
        self.TAc = [C("TA%d" % j) for j in range(NTA)]
        self.WSc = [C("WS%d" % j) for j in range(NSLOT)]
        self.WSTc = [C("WST%d" % j) for j in range(NSTG - 1)]
        self.STGc = [C("STG%d" % j) for j in range(2)]
        self.bankc = [C("bank%d" % j) for j in range(NBANK)]
        _psb = C("psb")
        self.PSBc = [_psb, _psb]
        self.constc = C("const")
        self.PRAWc = C("praw")
        self.PARc = C("par")
        self.DWTc = C("dwt")
        self.POOLWc = C("poolw")
        self.WTc = C("wt")
        self.GBc = C("gb")
        self.MEMTc = C("memt")
        self.KTc = [C("kt%d" % l) for l in range(L)]
        self.VBc = [C("vb%d" % l) for l in range(L)]
        self.CAc = [[C() for j in range(4)] for l in range(L)]
        self.CCc = [[C() for j in range(4)] for l in range(L)]
        self.CDc = [[C() for j in range(4)] for l in range(L)]
        self.SPOOLc = C("spool")
        self.SCONVc = [C("sconv%d" % j) for j in range(4)]
        self.SSCc = [C("ssc%d" % j) for j in range(4)]
        self.KVSc = [C("kvs%d" % j) for j in range(NKV)]
        self.EXTBc = [C("extb%d" % j) for j in range(4)]
        self.QSc = C("qs")
        self.SSc = C("ss")
        self.ESc = C("es")
        self.RSc = C("rs")

    def next_bank(self):
        b = self.bank_i
        self.bank_i = (self.bank_i + 1) % NBANK
        return b

    def next_ta(self):
        a = self.ta_i
        self.ta_i = (self.ta_i + 1) % NTA
        return a

    def next_stg(self):
        a = self.stg_i
        self.stg_i = (self.stg_i + 1) % 2
        return a

    def mm(self, pairs, n, reads, m=128, bank=None, c0=0, writes_extra=()):
        b = self.next_bank() if bank is None else bank
        out = self.PS[0:m, b, c0:c0 + n]
        npair = len(pairs)

        def fn(e):
            ins = None
            for i, (l, r) in enumerate(pairs):
                ins = e.matmul(out, lhsT=l, rhs=r, start=(i == 0), stop=(i == npair - 1))
            return ins

        self.s.op("pe", reads, [self.bankc[b]] + list(writes_extra), fn)
        return b

    def act(self, out, in_, func, reads, writes, **kw):
        self.s.op("act", reads, writes, lambda e: e.activation(out=out, in_=in_, func=func, **kw))

    def tt(self, eng, out, in0, in1, op, reads, writes):
        self.s.op(eng, reads, writes, lambda e: e.tensor_tensor(out=out, in0=in0, in1=in1, op=op))

    def stt(self, eng, out, in0, scalar, in1, op0, op1, reads, writes):
        self.s.op(eng, reads, writes, lambda e: e.scalar_tensor_tensor(out=out, in0=in0, scalar=scalar, in1=in1,
                                                                      op0=op0, op1=op1))

    def ts(self, eng, out, in0, s1, s2, op0, op1, reads, writes):
        if s2 is None:
            self.s.op(eng, reads, writes, lambda e: e.tensor_scalar(out=out, in0=in0, scalar1=s1, scalar2=None, op0=op0))
        else:
            self.s.op(eng, reads, writes, lambda e: e.tensor_scalar(out=out, in0=in0, scalar1=s1, scalar2=s2,
                                                                    op0=op0, op1=op1))

    def cp(self, eng, out, in_, reads, writes):
        if eng == "act":
            self.s.op("act", reads, writes, lambda e: e.activation(out=out, in_=in_, func=AF.Copy))
        else:
            self.s.op(eng, reads, writes, lambda e: e.tensor_copy(out=out, in_=in_))

    def dma(self, q, sem, out, in_, reads, writes, slow=False):
        def fn(e, semh):
            if slow:
                return e.dma_start(out=out, in_=in_, allow_slow_non_contiguous=True).then_inc(semh, 16)
            return e.dma_start(out=out, in_=in_).then_inc(semh, 16)

        self.s.dma(q, sem, 1, reads, writes, fn)

    def setup_dma(self, q, out, in_, cell, slow=False):
        self.dma(q, "setup", out, in_, [], [], slow=slow)
        if cell not in self.setup_cells:
            self.setup_cells.append(cell)

    def tr_to_bank(self, in_ap, k, m, b, c0, reads, p0=0):
        out = self.PS[0:m, b, c0:c0 + k]
        idn = self.IDF[p0:p0 + k, p0:p0 + k]
        self.s.op("pe", reads + [self.constc], [self.bankc[b]],
                  lambda e: e.transpose(out=out, in_=in_ap, identity=idn))

    def out_rows(self, src_aps, m, dst_ap, reads, width=128):
        b = self.next_bank()
        for j, a in enumerate(src_aps):
            self.tr_to_bank(a, 128, m, b, j * 128, reads)
        sg = self.next_stg()
        w = 128 * len(src_aps)
        self.cp("act", self.STG[0:m, sg, 0:w], self.PS[0:m, b, 0:w], [self.bankc[b]], [self.STGc[sg]])
        self.dma("sp", "stg%d" % sg, dst_ap, self.STG[0:m, sg, 0:w], [self.STGc[sg]], [])

    def wload(self, kind, l, idx, which=0):
        d = self.d
        slot = self.ws_i
        self.ws_i = (self.ws_i + 1) % NSLOT
        sslot = self.wst_i
        self.wst_i = (self.wst_i + 1) % NSTG
        cell = self.WSc[slot]
        if sslot < NSTG - 1:
            scells = [self.WSTc[sslot]]
            stg_flat = self.WST[:, sslot, :]
        else:
            scells = [self.KVSc[0], self.KVSc[1]]
            stg_flat = self.KVS[:, 0:2, :].rearrange("p a c -> p (a c)")

        def src_cols(w2d, c0, cw, r0=0, kt=None):
            v = w2d.rearrange("(k p) n -> p k n", p=128)
            if kt is not None:
                v = v[:, r0:r0 + kt, :]
            return v[:, :, c0:c0 + cw]

        def dst(kt, cw, e0=0):
            return stg_flat[:, e0:e0 + kt * cw].rearrange("p (k c) -> p k c", c=cw)

        dmas = []
        if kind in ("w1", "w3"):
            cw = 256
            dmas.append((dst(KD, cw), src_cols(d["ffn%d_%s" % (which + 1, kind)][l], idx * 256, cw)))
            ne = KD * cw
        elif kind == "w2":
            cw = 128
            dc, half = idx // 2, idx % 2
            dmas.append((dst(11, cw), src_cols(d["ffn%d_w2" % (which + 1)][l], dc * 128, cw, r0=half * 11, kt=11)))
            ne = 11 * cw
        elif kind in ("w_in", "w_o", "xa_wq", "xa_wk", "xa_wv", "xa_wo"):
            cw = 256
            dmas.append((dst(KD, cw), src_cols(d[kind][l], idx * 256, cw)))
            ne = KD * cw
        elif kind == "gate":
            cw = 256
            j, pr = idx // 2, idx % 2
            wd = stg_flat[:, 0:KD * 256].rearrange("p (k b c) -> p k b c", b=2, c=128)
            for b2 in range(2):
                dmas.append((wd[:, :, b2, :], src_cols(d["w_gate"][l], (pr * 2 + b2) * 1024 + j * 128, 128)))
            ne = KD * cw
        elif kind == "proj":
            cw = 128
            for bi, nm in enumerate(("pool_proj", "gmlp_proj", "conv_proj", "sc_proj")):
                dmas.append((dst(4, cw, e0=bi * 512), src_cols(d[nm][l], idx * 128, cw)))
            ne = 2048
        else:
            raise ValueError(kind)
        n = len(dmas)
        spec = (kind, l, idx, which)
        cached = self.wcache.get(spec)
        if cached is not None:
            si, ccell = cached
            src = self.wscr[si, :, 0:ne]

            def fnl(e, semh):
                return e.dma_start(out=self.WS[:, slot, 0:ne], in_=src).then_inc(semh, 16)

            self.wst_i = (self.wst_i - 1) % NSTG
            self.s.dma("sp", "wsl%d" % slot, 1, [ccell], [cell], fnl)
            return Slab(slot, cell, cw)

        def fn(e, semh):
            ins = None
            for (o, i) in dmas:
                ins = e.dma_start(out=o, in_=i).then_inc(semh, 16)
            return ins

        self.s.dma("sp", "wst%d" % sslot, n, [], scells, fn)
        self.cast_i = getattr(self, "cast_i", 0) + 1
        self.cp("act" if self.cast_i % 2 else "dve", self.WS[:, slot, 0:ne], stg_flat[:, 0:ne], scells, [cell])
        sl = Slab(slot, cell, cw)
        if spec in self.wreuse:
            si = len(self.wcache)
            ccell = Cell("wc%d" % si)
            self.wcache[spec] = (si, ccell)
            sl.store = (si, ccell, ne)
        return sl

    def wplan(self, specs):
        seen, reuse = set(), set()
        for sp in specs:
            if sp in seen:
                reuse.add(sp)
            seen.add(sp)
        self.wreuse = reuse
        self.wcache = {}
        self.wscr = self.nc.dram_tensor("wscratch", [max(1, len(reuse)), 128, SLOT_E], BF16, kind="Internal").ap()
        self.wq_specs = list(specs)
        self.wq_loaded = []
        self.wq_pos = 0
        self.ws_i = 0
        self.wst_i = 0

    def wtake(self, kind, l, idx, which=0, keep=1):
        spec = (kind, l, idx, which)
        while len(self.wq_loaded) < min(max(self.wq_pos + NSLOT - keep, self.wq_pos + 1), len(self.wq_specs)):
            sp = self.wq_specs[len(self.wq_loaded)]
            self.wq_loaded.append(self.wload(*sp))
        assert self.wq_specs[self.wq_pos] == spec, (self.wq_specs[self.wq_pos], spec)
        sl = self.wq_loaded[self.wq_pos]
        self.wq_pos += 1
        st_ = getattr(sl, "store", None)
        if st_ is not None:
            si, ccell, ne = st_
            sl.store = None
            dstc = self.wscr[si, :, 0:ne]
            src = self.WS[:, sl.slot, 0:ne]

            def fns(e, semh):
                return e.dma_start(out=dstc, in_=src).then_inc(semh, 16)

            self.s.dma("sp", "wss%d" % sl.slot, 1, [sl.cell], [ccell], fns)
        return sl

    def wap(self, sl, k, c0, cn, e0=0):
        base = e0 + k * sl.cw + c0
        return self.WS[:, sl.slot, base:base + cn]

    def specs_setup(self):
        sp = []
        for l in range(L):
            for nm in ("xa_wk", "xa_wv"):
                for i in range(4):
                    sp.append((nm, l, i, 0))
        return sp

    def specs_ffn(self, l, which):
        sp = []
        for s_ in range(11):
            sp.append(("w1", l, s_, which))
            sp.append(("w3", l, s_, which))
        for i in range(16):
            sp.append(("w2", l, i, which))
        return sp

    def specs_layer(self, l):
        sp = self.specs_ffn(l, 0)
        for i in range(16):
            sp.append(("w_in", l, i, 0))
        for j in range(8):
            sp.append(("gate", l, 2 * j, 0))
            sp.append(("gate", l, 2 * j + 1, 0))
            sp.append(("proj", l, j, 0))
        for nm in ("w_o", "xa_wq", "xa_wo"):
            for i in range(4):
                sp.append((nm, l, i, 0))
        sp += self.specs_ffn(l, 1)
        return sp

    def tiles_of(self, gi):
        tl = []
        for t in range(TPG):
            tl.append(dict(kind="p", off=t * TILE, n=TILE, seq0=(gi * TPG + t) * TILE, ti=t))
        if gi == NGRP - 1:
            tl.append(dict(kind="s", off=NPG, n=NS, seq0=0, ti=TPG))
        return tl

    def Xv(self, k, t):
        return self.X[:, k, t["off"]:t["off"] + t["n"]]

    def X3(self, t):
        return self.X[:, :, t["off"]:t["off"] + t["n"]]

    def XBv(self, k, t):
        return self.XB[:, k, t["off"]:t["off"] + t["n"]]

    def XB3(self, t):
        return self.XB[:, :, t["off"]:t["off"] + t["n"]]

    def Gv(self, c, t):
        return self.G[:, c, t["off"]:t["off"] + t["n"]]

    def G3(self, c0, cn, t):
        return self.G[:, c0:c0 + cn, t["off"]:t["off"] + t["n"]]

    def Fv(self, j, t):
        return self.F[:, j, HP + t["off"]:HP + t["off"] + t["n"]]

    def F3(self, j0, jn, t):
        return self.F[:, j0:j0 + jn, HP + t["off"]:HP + t["off"] + t["n"]]

    def par(self, l, col):
        return self.PAR[:, l, col:col + 1]

    def layernorm(self, t, C, src3, src_cells, zb3, zb_cells, zq3, zq_cells, eps, l, gcol, bcol, crit, post=None,
                  chunk_cells=None, pre=False):
        n = t["n"]
        ST = self.ST
        inv = 1.0 / (C * 128.0)
        if not pre:
            self.cp("dve", zb3, src3, src_cells, zb_cells)
            self.act(zq3, src3, AF.Square, src_cells, zq_cells)
        bs = self.mm([(self.ONESB[:, :], zb3[:, c, :]) for c in range(C)], n, zb_cells + [self.constc])
        bq = self.mm([(self.ONESB[:, :], zq3[:, c, :]) for c in range(C)], n, zq_cells + [self.constc])
        mean, msq, rstd, nmr = ST[:, 0, 0:n], ST[:, 1, 0:n], ST[:, 2, 0:n], ST[:, 3, 0:n]
        Sc = self.STc
        self.ts("dve", mean, self.PS[:, bs, 0:n], inv, None, ALU.mult, None, [self.bankc[bs]], [Sc[0]])
        self.tt("dve", msq, mean, mean, ALU.mult, [Sc[0]], [Sc[1]])
        self.stt("dve", msq, self.PS[:, bq, 0:n], inv, msq, ALU.mult, ALU.subtract, [self.bankc[bq], Sc[1]], [Sc[1]])
        self.ts("dve", msq, msq, 0.0, None, ALU.max, None, [Sc[1]], [Sc[1]])
        self.act(rstd, msq, AF.Sqrt, [Sc[1]], [Sc[2]], bias=float(eps), scale=1.0)
        self.s.op("dve", [Sc[2]], [Sc[2]], lambda e: e.reciprocal(out=rstd, in_=rstd))
        self.stt("dve", nmr, mean, -1.0, rstd, ALU.mult, ALU.mult, [Sc[0], Sc[2]], [Sc[3]])
        if chunk_cells is None:
            cc = [Cell("lnc%d" % c) for c in range(C)]
            after = list(src_cells)
        else:
            cc = chunk_cells
            after = []
        for c in range(C):
            x = src3[:, c, :]
            self.s.op("dve", [Sc[2], cc[c]] if chunk_cells is not None else [Sc[2]], [cc[c]],
                      lambda e, x=x: e.tensor_tensor(out=x, in0=x, in1=rstd, op=ALU.mult), after=after)
            self.tt("dve", x, x, nmr, ALU.add, [cc[c], Sc[3]], [cc[c]])
            o, func, cells = crit(c)
            self.act(o, x, func, [cc[c], self.PARc], cells, scale=self.par(l, gcol + c), bias=self.par(l, bcol + c))
        if post is not None:
            for c in range(C):
                o, func, cells = post(c)
                self.ts("dve", o, src3[:, c, :], self.par(l, gcol + c), self.par(l, bcol + c), ALU.mult, ALU.add,
                        [cc[c], self.PARc], cells + [cc[c]])

    def ln_x(self, tiles, l, idx, eps, pre=False):
        gcol = 16 * idx
        bcol = 16 * idx + 8
        for t in tiles:
            ti = t["ti"]
            xc = [self.Xc[ti]]
            xbc = [self.XBc[ti]]
            zq_cells = [self.Gc[c][ti] for c in range(8)]
            self.layernorm(t, 8, self.X3(t), xc, self.XB3(t), xbc, self.G3(0, 8, t), zq_cells, eps, l, gcol, bcol,
                           crit=lambda c, t=t, xbc=xbc: (self.XBv(c, t), AF.Identity, xbc),
                           post=lambda c, t=t, xc=xc: (self.Xv(c, t), AF.Identity, xc), pre=pre)

    def ln_pre(self, dc, t):
        ti = t["ti"]
        self.act(self.Gv(dc, t), self.Xv(dc, t), AF.Square, [self.Xc[ti]], [self.Gc[dc][ti]])
        self.cp("act", self.XBv(dc, t), self.Xv(dc, t), [self.Xc[ti]], [self.XBc[ti]])

    def ffn(self, tiles, l, which):
        self.s.stage = 'ffn.up'
        for s_ in range(11):
            s1 = self.wtake("w1", l, s_, which)
            s3 = self.wtake("w3", l, s_, which)
            for fc in range(2):
                f = s_ * 2 + fc
                for t in tiles:
                    n, ti = t["n"], t["ti"]
                    b1 = self.mm([(self.wap(s1, k, fc * 128, 128), self.XBv(k, t)) for k in range(KD)], n,
                                 [s1.cell, self.XBc[ti]])
                    b3 = self.mm([(self.wap(s3, k, fc * 128, 128), self.XBv(k, t)) for k in range(KD)], n,
                                 [s3.cell, self.XBc[ti]])
                    a = self.next_ta()
                    self.act(self.TA[:, a, 0:n], self.PS[:, b1, 0:n], AF.Silu, [self.bankc[b1]], [self.TAc[a]])
                    self.tt("dve", self.Gv(f, t), self.TA[:, a, 0:n], self.PS[:, b3, 0:n], ALU.mult,
                            [self.TAc[a], self.bankc[b3]], [self.Gc[f][ti]])
        self.s.stage = 'ffn.down'
        for dc in range(8):
            s2a = self.wtake("w2", l, 2 * dc, which)
            s2b = self.wtake("w2", l, 2 * dc + 1, which)
            for t in tiles:
                n, ti = t["n"], t["ti"]
                b = self.mm([(self.wap(s2a if f < 11 else s2b, f % 11, 0, 128), self.Gv(f, t)) for f in range(KF)], n,
                            [s2a.cell, s2b.cell] + [self.Gc[f][ti] for f in range(KF)])
                self.stt("dve", self.Xv(dc, t), self.Xv(dc, t), 2.0 * ALPHA, self.PS[:, b, 0:n], ALU.mult, ALU.add,
                         [self.Xc[ti], self.bankc[b]], [self.Xc[ti]])
        self.s.stage = 'ffn.ln'
        self.ln_x(tiles, l, 0 if which == 0 else 3, 4.0 * EPS)

    def win_evac(self, sl, fc, tiles, fn_evac):
        for t in tiles:
            n, ti = t["n"], t["ti"]
            b = self.mm([(self.wap(sl, k, fc * 128, 128), self.XBv(k, t)) for k in range(KD)], n,
                        [sl.cell, self.XBc[ti]])
            fn_evac(t, b)

    def mixer(self, gi, tiles, l):
        F, G, TA, PS = self.F, self.G, self.TA, self.PS
        Fc, Gc, TAc, bankc = self.Fc, self.Gc, self.TAc, self.bankc
        ptiles = [t for t in tiles if t["kind"] == "p"]
        st = [t for t in tiles if t["kind"] == "s"]
        st = st[0] if st else None
        d, o = self.d, self.o
        first = (gi == 0)
        if st is not None:
            sp_rows = d["state_pool"][l].rearrange("b r c -> (b r) c")
            cv_rows = d["state_conv"][l].rearrange("b r c -> (b r) c")
            sc_rows = d["state_shortconv"][l].rearrange("b r c -> (b r) c")
            for (rows, nrow, per, dst_t, dcells, rr) in ((sp_rows, 240, 15, self.SPOOL, [self.SPOOLc] * 4, 15),
                                                         (cv_rows, 480, 30, self.SCONV, self.SCONVc, 31),
                                                         (sc_rows, 32, 2, self.SSC, self.SSCc, 3)):
                r0 = 0
                while r0 < nrow:
                    nb = min(128 // per, (nrow - r0) // per)
                    m = nb * per
                    sg = self.next_stg()
                    self.dma("sp", "stg%d" % sg, self.STG[0:m, sg, 0:512], rows[r0:r0 + m, :], [], [self.STGc[sg]])
                    b0 = r0 // per
                    for j in range(4):
                        b = self.next_bank()
                        self.tr_to_bank(self.STG[0:m, sg, j * 128:(j + 1) * 128], m, 128, b, 0, [self.STGc[sg]])
                        self.cp("act", dst_t[:, j, b0:b0 + nb, 0:per],
                                PS[:, b, 0:m].rearrange("p (b r) -> p b r", r=per), [bankc[b]], [dcells[j]])
                    r0 += m
            self.dma("sp", "out", o["pool_s"][l, :, 0:14, :], d["state_pool"][l, :, 1:15, :], [], [])
            self.dma("sp", "out", o["conv_s"][l, :, 0:29, :], d["state_conv"][l, :, 1:30, :], [], [])
            self.dma("sp", "out", o["sc_s"][l, :, 0:1, :], d["state_shortconv"][l, :, 1:2, :], [], [])

        npg = len(ptiles) * TILE
        c_lo, c_hi = HP, HP + npg

        self.s.stage = 'mix.A'
        def role(r, fn_evac):
            for hs in range(2):
                sl_ = self.wtake("w_in", l, 2 * r + hs)
                for c2 in range(2):
                    j_ = hs * 2 + c2
                    self.win_evac(sl_, c2, tiles, lambda t, b, j_=j_: fn_evac(j_, t, b))

        for j in range(4):
            self.cp("act", F[:, j, HP - 15:HP], self.CA[:, l, j, :], [self.CAc[l][j]], [Fc[j]])
        role(0, lambda j, t, b: self.cp("act", self.Fv(j, t), PS[:, b, 0:t["n"]], [bankc[b]], [Fc[j]]))
        for g in range(4):
            win = WINS[g]
            Lx = 15 + npg
            e0 = HP - 15
            B1, B2 = 4 + 2 * (g % 2), 5 + 2 * (g % 2)
            src, cur = g, None
            bufs = [B1, B2]
            sh = 1
            step = 0
            while sh < win:
                dstb = bufs[step % 2]
                lo = 2 * sh - 1
                self.tt("dve", F[:, dstb, e0 + lo:e0 + Lx], F[:, src, e0 + lo:e0 + Lx], F[:, src, e0 + lo - sh:e0 + Lx - sh],
                        ALU.add, [Fc[src]], [Fc[dstb]])
                src = dstb
                sh *= 2
                step += 1
            S = src
            for t in ptiles:
                ti = t["ti"]
                cs = slice(HP + t["off"], HP + t["off"] + t["n"])
                self.stt("dve", self.Gv(16 + g, t), F[:, S, cs], 1.0 / win, F[:, g, cs], ALU.mult, ALU.subtract,
                         [Fc[S], Fc[g]], [Gc[16 + g][ti]])
            if first:
                w1_ = win - 1
                a = self.next_ta()
                self.tt("dve", TA[:, a, 0:w1_], F[:, S, HP:HP + w1_], self.RC0[:, g, 0:w1_], ALU.mult,
                        [Fc[S], self.constc], [TAc[a]])
                self.tt("dve", G[:, 16 + g, 0:w1_], TA[:, a, 0:w1_], F[:, g, HP:HP + w1_], ALU.subtract,
                        [TAc[a], Fc[g]], [Gc[16 + g][0]])
            self.cp("act", self.CA[:, l, g, :], F[:, g, c_hi - 15:c_hi], [Fc[g]], [self.CAc[l][g]])
            if st is not None:
                a = self.next_ta()
                tmp = TA[:, a, 0:NS]
                self.s.op("dve", [self.SPOOLc], [TAc[a]],
                          lambda e, g=g, win=win, tmp=tmp: e.tensor_reduce(out=tmp, in_=self.SPOOL[:, g, :, 16 - win:15],
                                                                          axis=AX.X, op=ALU.add))
                self.tt("dve", tmp, tmp, self.Fv(g, st), ALU.add, [TAc[a], Fc[g]], [TAc[a]])
                self.stt("dve", self.Gv(16 + g, st), tmp, 1.0 / win, self.Fv(g, st), ALU.mult, ALU.subtract,
                         [TAc[a], Fc[g]], [Gc[16 + g][st["ti"]]])
        if st is not None:
            self.out_rows([self.Fv(j, st) for j in range(4)], NS, o["pool_s"][l, :, 14, :], [Fc[j] for j in range(4)])

        self.s.stage = 'mix.B'
        role(1, lambda j, t, b: self.cp("act", self.Fv(j, t), PS[:, b, 0:t["n"]], [bankc[b]], [Fc[j]]))
        role(2, lambda j, t, b: self.cp("act", self.Fv(4 + j, t), PS[:, b, 0:t["n"]], [bankc[b]], [Fc[4 + j]]))
        for g in range(4):
            for t in tiles:
                n, ti = t["n"], t["ti"]
                b = self.mm([(self.POOLW[:, l, g, :], self.Gv(16 + g, t))], n, [self.POOLWc, Gc[16 + g][ti]])
                self.act(self.Gv(g, t), PS[:, b, 0:n], AF.Identity, [bankc[b], self.PARc], [Gc[g][ti]],
                         scale=self.par(l, 96 + g))
        for t in tiles:
            ti = t["ti"]

            self.layernorm(t, 4, self.F3(4, 4, t), [Fc[4 + c] for c in range(4)],
                           self.G3(12, 4, t), [Gc[12 + c][ti] for c in range(4)],
                           self.G3(16, 4, t), [Gc[16 + c][ti] for c in range(4)], EPS, l, 100, 104,
                           crit=lambda c, t=t, ti=ti: (self.Gv(8 + c, t), AF.Identity, [Gc[8 + c][ti]]),
                           post=lambda c, t=t: (self.Fv(4 + c, t), AF.Identity, [Fc[4 + c]]),
                           chunk_cells=[Fc[4 + c] for c in range(4)])
        for t in ptiles:
            ti, n = t["ti"], t["n"]
            for c in range(4):
                pb = c % 2
                for h in range(4):
                    vin = self.G[:, 8 + h, t["off"] + c * 128:t["off"] + (c + 1) * 128]
                    outp = self.PSB[:, pb, h * 128:(h + 1) * 128]
                    self.s.op("pe", [Gc[8 + h][ti], self.constc], [self.PSBc[pb]],
                              lambda e, vin=vin, outp=outp: e.transpose(out=outp, in_=vin, identity=self.IDB[:, :]))
                self.cp("act", self.G[:, 20 + c, 0:512], self.PSB[:, pb, :], [self.PSBc[pb]], [Gc[20 + c][0]])
            for h in range(4):
                b = self.next_bank()

                def fn(e, b=b, h=h):
                    ins = None
                    for c in range(4):
                        ins = e.matmul(PS[:, b, c * 128:(c + 1) * 128], lhsT=self.G[:, 20 + c, h * 128:(h + 1) * 128],
                                       rhs=self.WT[:, l, h, :], start=True, stop=True)
                    return ins

                self.s.op("pe", [Gc[20 + c][0] for c in range(4)] + [self.WTc], [bankc[b]], fn)
                a = self.next_ta()
                self.tt("dve", TA[:, a, :].rearrange("p (c t) -> p c t", c=4),
                        PS[:, b, :].rearrange("p (c t) -> p c t", c=4),
                        self.GB[:, l, h * 128:(h + 1) * 128].unsqueeze(1).broadcast_to([128, 4, 128]), ALU.add,
                        [bankc[b], self.GBc], [TAc[a]])
                self.tt("dve", self.Gv(4 + h, t), TA[:, a, 0:n], self.Fv(h, t), ALU.mult, [TAc[a], Fc[h]], [Gc[4 + h][ti]])
        if st is not None:
            self.out_rows([self.Fv(4 + j, st) for j in range(4)], NS, o["gv_s"][l, :, :], [Fc[4 + j] for j in range(4)])
            for h in range(4):
                a = self.next_ta()
                self.ts("dve", TA[:, a, 0:NS], self.Fv(4 + h, st), self.GS[:, l, h:h + 1], self.GS[:, l, 4 + h:5 + h],
                        ALU.mult, ALU.add, [Fc[4 + h], self.GBc], [TAc[a]])
                self.tt("dve", self.Gv(4 + h, st), TA[:, a, 0:NS], self.Fv(h, st), ALU.mult, [TAc[a], Fc[h]],
                        [Gc[4 + h][st["ti"]]])

        self.s.stage = 'mix.C'
        for j in range(4):
            self.cp("act", F[:, j, HP - 30:HP], self.CC[:, l, j, :], [self.CCc[l][j]], [Fc[j]])
        role(3, lambda j, t, b: self.cp("act", self.Fv(j, t), PS[:, b, 0:t["n"]], [bankc[b]], [Fc[j]]))

        def ev_glu(j, t, b):
            a = self.next_ta()
            n = t["n"]
            self.act(TA[:, a, 0:n], PS[:, b, 0:n], AF.Sigmoid, [bankc[b]], [TAc[a]])
            self.tt("dve", self.Fv(j, t), self.Fv(j, t), TA[:, a, 0:n], ALU.mult, [Fc[j], TAc[a]], [Fc[j]])

        role(4, ev_glu)
        for j in range(4):
            self.cp("act", self.CC[:, l, j, :], F[:, j, c_hi - 30:c_hi], [Fc[j]], [self.CCc[l][j]])
            self.cp("act", self.EXTB[:, j, 0:30 + npg], F[:, j, c_lo - 30:c_hi], [Fc[j]], [self.EXTBc[j]])
            g0_ = 8 + 8 * (j % 2)
            dgv = G[:, g0_:g0_ + 8, :].rearrange("p a b -> p (a b)")[:, 0:31 * 128].rearrange("p (k c) -> p k c", c=128)
            dg_cells = [Gc[c][t_["ti"]] for c in range(g0_, g0_ + 8) for t_ in tiles]
            self.tt("dve", dgv, self.IDB[:, :].unsqueeze(1).broadcast_to([128, 31, 128]),
                    self.DWT[:, l, j, :].unsqueeze(2).broadcast_to([128, 31, 128]), ALU.mult,
                    [self.constc, self.DWTc], dg_cells)
            for t in ptiles:
                n = t["n"]
                b = self.mm([(dgv[:, k, :], self.EXTB[:, j, t["off"] + k:t["off"] + k + n]) for k in range(31)], n,
                            [self.EXTBc[j]] + dg_cells)
                self.act(self.Fv(4 + j, t), PS[:, b, 0:n], AF.Identity, [bankc[b], self.PARc], [Fc[4 + j]],
                         bias=self.par(l, 108 + j), scale=1.0)
            if st is not None:
                self.cp("act", self.SCONV[:, j, :, 30], self.Fv(j, st), [Fc[j]], [self.SCONVc[j]])
                a = self.next_ta()
                pr = TA[:, a, 0:NS * 31].rearrange("p (b r) -> p b r", r=31)
                self.tt("dve", pr, self.SCONV[:, j, :, :], self.DWT[:, l, j, :].unsqueeze(1).broadcast_to([128, NS, 31]),
                        ALU.mult, [self.SCONVc[j], self.DWTc], [TAc[a]])
                a2 = self.next_ta()
                tmp = TA[:, a2, 0:NS]
                self.s.op("dve", [TAc[a]], [TAc[a2]],
                          lambda e, pr=pr, tmp=tmp: e.tensor_reduce(out=tmp, in_=pr, axis=AX.X, op=ALU.add))
                self.ts("dve", self.Fv(4 + j, st), tmp, self.par(l, 108 + j), None, ALU.add, None,
                        [TAc[a2], self.PARc], [Fc[4 + j]])
        if st is not None:
            self.out_rows([self.Fv(j, st) for j in range(4)], NS, o["conv_s"][l, :, 29, :], [Fc[j] for j in range(4)])
        for t in tiles:
            ti = t["ti"]

            self.layernorm(t, 4, self.F3(4, 4, t), [Fc[4 + c] for c in range(4)],
                           self.G3(12, 4, t), [Gc[12 + c][ti] for c in range(4)],
                           self.G3(16, 4, t), [Gc[16 + c][ti] for c in range(4)], EPS, l, 112, 116,
                           crit=lambda c, t=t, ti=ti: (self.Gv(8 + c, t), AF.Silu, [Gc[8 + c][ti]]),
                           chunk_cells=[Fc[4 + c] for c in range(4)])

        self.s.stage = 'mix.D'
        role(5, lambda j, t, b: self.cp("act", self.Fv(j, t), PS[:, b, 0:t["n"]], [bankc[b]], [Fc[j]]))
        for j in range(4):
            self.cp("act", F[:, 4 + j, HP - 2:HP], self.CD[:, l, j, :], [self.CDc[l][j]], [Fc[4 + j]])
        role(6, lambda j, t, b: self.cp("act", self.Fv(4 + j, t), PS[:, b, 0:t["n"]], [bankc[b]], [Fc[4 + j]]))
        role(7, lambda j, t, b: self.tt("dve", self.Fv(4 + j, t), self.Fv(4 + j, t), PS[:, b, 0:t["n"]], ALU.mult,
                                        [Fc[4 + j], bankc[b]], [Fc[4 + j]]))
        for j in range(4):
            self.cp("act", self.CD[:, l, j, :], F[:, 4 + j, c_hi - 2:c_hi], [Fc[4 + j]], [self.CDc[l][j]])
            for t in ptiles:
                ti, n = t["ti"], t["n"]
                c0 = HP + t["off"]
                a = self.next_ta()
                y = TA[:, a, 0:n]
                self.ts("dve", y, F[:, 4 + j, c0 - 2:c0 - 2 + n], self.SCW[:, l, j, 0:1], None, ALU.mult, None,
                        [Fc[4 + j], self.DWTc], [TAc[a]])
                for k in (1, 2):
                    self.stt("dve", y, F[:, 4 + j, c0 - 2 + k:c0 - 2 + k + n], self.SCW[:, l, j, k:k + 1], y,
                             ALU.mult, ALU.add, [Fc[4 + j], TAc[a], self.DWTc], [TAc[a]])
                self.tt("dve", self.Gv(12 + j, t), y, self.Fv(j, t), ALU.mult, [TAc[a], Fc[j]], [Gc[12 + j][ti]])
            if st is not None:
                self.cp("act", self.SSC[:, j, :, 2], self.Fv(4 + j, st), [Fc[4 + j]], [self.SSCc[j]])
                a = self.next_ta()
                pr = TA[:, a, 0:NS * 3].rearrange("p (b r) -> p b r", r=3)
                self.tt("dve", pr, self.SSC[:, j, :, :], self.SCW[:, l, j, :].unsqueeze(1).broadcast_to([128, NS, 3]),
                        ALU.mult, [self.SSCc[j], self.DWTc], [TAc[a]])
                a2 = self.next_ta()
                tmp = TA[:, a2, 0:NS]
                self.s.op("dve", [TAc[a]], [TAc[a2]],
                          lambda e, pr=pr, tmp=tmp: e.tensor_reduce(out=tmp, in_=pr, axis=AX.X, op=ALU.add))
                self.tt("dve", self.Gv(12 + j, st), tmp, self.Fv(j, st), ALU.mult, [TAc[a2], Fc[j]],
                        [Gc[12 + j][st["ti"]]])
        if st is not None:
            self.out_rows([self.Fv(4 + j, st) for j in range(4)], NS, o["sc_s"][l, :, 1, :], [Fc[4 + j] for j in range(4)])

        self.s.stage = 'mix.gate'
        for j in range(8):
            ga = self.wtake("gate", l, 2 * j)
            gb_ = self.wtake("gate", l, 2 * j + 1)
            ps_ = self.wtake("proj", l, j, keep=2)
            for t in tiles:
                n, ti = t["n"], t["ti"]
                tas = []
                for bi in range(4):
                    gs = ga if bi < 2 else gb_
                    go = (bi % 2) * 128
                    bg = self.mm([(self.WS[:, gs.slot, k * 256 + go:k * 256 + go + 128], self.XBv(k, t))
                                  for k in range(KD)], n, [gs.cell, self.XBc[ti]])
                    a = self.next_ta()
                    tas.append(a)
                    self.act(TA[:, a, 0:n], PS[:, bg, 0:n], AF.Sigmoid, [bankc[bg], self.PARc], [TAc[a]],
                             bias=self.par(l, 64 + bi * 8 + j), scale=1.0)
                for bi in range(4):
                    by = self.mm([(self.WS[:, ps_.slot, bi * 512 + k * 128:bi * 512 + k * 128 + 128], self.Gv(4 * bi + k, t))
                                  for k in range(4)], n, [ps_.cell] + [Gc[4 * bi + k][ti] for k in range(4)])
                    a = tas[bi]
                    self.tt("dve", TA[:, a, 0:n], TA[:, a, 0:n], PS[:, by, 0:n], ALU.mult, [TAc[a], bankc[by]], [TAc[a]])
                a0, a1, a2, a3 = tas
                self.tt("dve", TA[:, a0, 0:n], TA[:, a0, 0:n], TA[:, a1, 0:n], ALU.add, [TAc[a0], TAc[a1]], [TAc[a0]])
                self.tt("dve", TA[:, a2, 0:n], TA[:, a2, 0:n], TA[:, a3, 0:n], ALU.add, [TAc[a2], TAc[a3]], [TAc[a2]])
                self.tt("dve", self.Gv(16 + j, t), TA[:, a0, 0:n], TA[:, a2, 0:n], ALU.add, [TAc[a0], TAc[a2]],
                        [Gc[16 + j][ti]])
        self.s.stage = 'mix.wo'
        for i in range(4):
            so = self.wtake("w_o", l, i)
            for c in range(2):
                dc = i * 2 + c
                for t in tiles:
                    n, ti = t["n"], t["ti"]
                    b = self.mm([(self.wap(so, k, c * 128, 128), self.Gv(16 + k, t)) for k in range(KD)], n,
                                [so.cell] + [Gc[16 + k][ti] for k in range(KD)])
                    self.stt("dve", self.Xv(dc, t), self.Xv(dc, t), ALPHA, PS[:, b, 0:n], ALU.mult, ALU.add,
                             [self.Xc[ti], bankc[b]], [self.Xc[ti]])
                    self.ln_pre(dc, t)
        self.s.stage = 'mix.ln2'
        self.ln_x(tiles, l, 1, EPS, pre=True)

    def attn(self, gi, tiles, l):
        G, TA, PS = self.G, self.TA, self.PS
        Gc, TAc, bankc = self.Gc, self.TAc, self.bankc
        ptiles = [t for t in tiles if t["kind"] == "p"]
        st = [t for t in tiles if t["kind"] == "s"]
        st = st[0] if st else None
        d = self.d
        self.s.stage = 'attn.q'
        for i in range(4):
            sq = self.wtake("xa_wq", l, i)
            for c in range(2):
                dc = i * 2 + c
                for t in ptiles:
                    n, ti = t["n"], t["ti"]
                    b = self.mm([(self.wap(sq, k, c * 128, 128), self.XBv(k, t)) for k in range(KD)], n,
                                [sq.cell, self.XBc[ti]])
                    self.cp("act", self.Gv(dc, t), PS[:, b, 0:n], [bankc[b]], [Gc[dc][ti]])
            if st is not None:
                b = self.mm([(self.XBv(k, st), self.wap(sq, k, 0, 256)) for k in range(KD)], 256,
                            [sq.cell, self.XBc[st["ti"]]], m=NS)
                self.cp("act", self.QS[0:NS, i * 256:(i + 1) * 256], PS[0:NS, b, 0:256], [bankc[b]], [self.QSc])
        self.s.stage = 'attn.core'
        scale = 256.0 ** -0.5
        for t in ptiles:
            n, ti = t["n"], t["ti"]
            for h in range(4):
                for mt in range(2):
                    b = self.mm([(self.KT[:, l, 2 * h + dd, mt * 128:(mt + 1) * 128], self.Gv(2 * h + dd, t)) for dd in range(2)],
                                n, [self.KTc[l], Gc[2 * h][ti], Gc[2 * h + 1][ti]])
                    self.act(self.Gv(8 + 2 * h + mt, t), PS[:, b, 0:n], AF.Exp, [bankc[b]], [Gc[8 + 2 * h + mt][ti]],
                             scale=scale)
                b = self.mm([(self.ONESB[:, :], self.Gv(8 + 2 * h + mt, t)) for mt in range(2)], n,
                            [self.constc, Gc[8 + 2 * h][ti], Gc[9 + 2 * h][ti]])
                a = self.next_ta()
                self.s.op("dve", [bankc[b]], [TAc[a]],
                          lambda e, a=a, b=b, n=n: e.reciprocal(out=TA[:, a, 0:n], in_=PS[:, b, 0:n]))
                for c in range(2):
                    bo = self.mm([(self.VB[:, l, mt, h * 256 + c * 128:h * 256 + (c + 1) * 128], self.Gv(8 + 2 * h + mt, t))
                                  for mt in range(2)], n, [self.VBc[l], Gc[8 + 2 * h][ti], Gc[9 + 2 * h][ti]])
                    self.tt("dve", self.Gv(16 + 2 * h + c, t), PS[:, bo, 0:n], TA[:, a, 0:n], ALU.mult,
                            [bankc[bo], TAc[a]], [Gc[16 + 2 * h + c][ti]])
        if st is not None:
            sti = st["ti"]
            selv = self.ST[0:NS, 0:4, :].rearrange("p a (b m) -> p (a b) m", m=128)
            self.cp("dve", selv, self.IDF[0:NS, 0:NS].unsqueeze(2).broadcast_to([NS, NS, 128]), [self.constc],
                    [self.STc[0], self.STc[1], self.STc[2], self.STc[3]])
            for bsm in range(NS):
                def fnq(e, bsm=bsm):
                    e.matmul(PS[:, 5, :], lhsT=selv[:, bsm, :], rhs=self.QS[0:NS, 0:512], start=True, stop=True)
                    return e.matmul(PS[:, 6, :], lhsT=selv[:, bsm, :], rhs=self.QS[0:NS, 512:1024], start=True, stop=True)

                self.s.op("pe", [self.QSc, self.STc[0], self.STc[1], self.STc[2], self.STc[3]], [bankc[5], bankc[6]], fnq)
                qb = PS[:, 5:7, :].rearrange("p a c -> p (a c)")
                for mt in range(2):
                    kv = self.kv_i
                    self.kv_i = (self.kv_i + 1) % NKV
                    self.dma("sp", "kv%d" % kv, self.KVS[:, kv, :], d["cache_mem_k"][l, bsm, mt * 128:(mt + 1) * 128, :],
                             [], [self.KVSc[kv]])
                    prod = TA[:, 0:2, :].rearrange("p a c -> p (a c)")
                    self.tt("dve", prod, self.KVS[:, kv, :], qb, ALU.mult, [self.KVSc[kv], bankc[5], bankc[6]],
                            [TAc[0], TAc[1]])
                    pr4 = TA[:, 0:2, :].rearrange("p a (m c) -> p (a m) c", m=2)
                    c0 = bsm * 8 + mt * 4
                    self.s.op("dve", [TAc[0], TAc[1]], [self.SSc],
                              lambda e, c0=c0, pr4=pr4: e.tensor_reduce(out=self.SS[:, c0:c0 + 4], in_=pr4,
                                                                        axis=AX.X, op=ALU.add))
            self.act(self.ES[:, :], self.SS[:, :], AF.Exp, [self.SSc], [self.ESc], scale=scale)
            b = self.mm([(self.ONESF[:, :], self.ES[:, :])], 128, [self.constc, self.ESc])
            a = self.next_ta()
            self.cp("act", TA[:, a, 0:128], PS[:, b, 0:128], [bankc[b]], [TAc[a]])
            den = TA[:, a, 0:128].rearrange("p (b m h) -> p m h b", m=2, h=4)
            self.tt("dve", self.RS[:, :, :], den[:, 0, :, :], den[:, 1, :, :], ALU.add, [TAc[a]], [self.RSc])
            self.s.op("dve", [self.RSc], [self.RSc], lambda e: e.reciprocal(out=self.RS[:, :, :], in_=self.RS[:, :, :]))
            bo2 = (self.next_bank(), self.next_bank())
            for bsm in range(NS):
                for mt in range(2):
                    kv = self.kv_i
                    self.kv_i = (self.kv_i + 1) % NKV
                    self.dma("sp", "kv%d" % kv, self.KVS[:, kv, :], d["cache_mem_v"][l, bsm, mt * 128:(mt + 1) * 128, :],
                             [], [self.KVSc[kv]])

                    def fnv(e, bsm=bsm, kv=kv, bo2=bo2, mt=mt):
                        ins = None
                        for c in range(8):
                            col = bsm * 8 + mt * 4 + c // 2
                            ins = e.matmul(PS[:, bo2[mt], c * NS + bsm:c * NS + bsm + 1],
                                           lhsT=self.KVS[:, kv, c * 128:(c + 1) * 128],
                                           rhs=self.ES[:, col:col + 1], start=True, stop=True)
                        return ins

                    self.s.op("pe", [self.KVSc[kv], self.ESc], [bankc[bo2[mt]]], fnv)
            for h in range(4):
                rs_b = self.RS[:, h, :].unsqueeze(1).broadcast_to([128, 2, NS])
                tmps = []
                for mt in range(2):
                    a = self.next_ta()
                    tv = TA[:, a, 0:2 * NS].rearrange("p (c b) -> p c b", c=2)
                    self.tt("dve", tv, PS[:, bo2[mt], 2 * h * NS:(2 * h + 2) * NS].rearrange("p (c b) -> p c b", c=2),
                            rs_b, ALU.mult, [bankc[bo2[mt]], self.RSc], [TAc[a]])
                    tmps.append((a, tv))
                self.tt("dve", G[:, 16 + 2 * h:18 + 2 * h, st["off"]:st["off"] + NS], tmps[0][1], tmps[1][1], ALU.add,
                        [TAc[tmps[0][0]], TAc[tmps[1][0]]], [Gc[16 + 2 * h][sti], Gc[17 + 2 * h][sti]])
        self.s.stage = 'attn.wo'
        for i in range(4):
            so = self.wtake("xa_wo", l, i)
            for c in range(2):
                dc = i * 2 + c
                for t in tiles:
                    n, ti = t["n"], t["ti"]
                    b = self.mm([(self.wap(so, k, c * 128, 128), self.Gv(16 + k, t)) for k in range(KD)], n,
                                [so.cell] + [Gc[16 + k][ti] for k in range(KD)])
                    self.stt("dve", self.Xv(dc, t), self.Xv(dc, t), ALPHA, PS[:, b, 0:n], ALU.mult, ALU.add,
                             [self.Xc[ti], bankc[b]], [self.Xc[ti]])
                    self.ln_pre(dc, t)
        self.s.stage = 'attn.ln3'
        self.ln_x(tiles, l, 2, EPS, pre=True)

    def setup(self):
        d, o = self.d, self.o
        PS, bankc = self.PS, self.bankc
        cst = self.constc
        pool_ops = []

        def P(fn, rd=False):
            self.s.op("pool", [cst] if rd else [], [cst], fn)

        P(lambda e: e.memset(self.IDF[:, :], 0.0))
        P(lambda e: e.affine_select(out=self.IDF[:, :], in_=self.IDF[:, :], pattern=[[-1, 128]], compare_op=ALU.not_equal,
                                    fill=1.0, base=0, channel_multiplier=1), rd=True)
        P(lambda e: e.tensor_copy(out=self.IDB[:, :], in_=self.IDF[:, :]), rd=True)
        P(lambda e: e.memset(self.ONESB[:, :], 1.0))
        P(lambda e: e.memset(self.ONESF[:, :], 1.0))
        for g in range(4):
            win = WINS[g]
            P(lambda e, g=g, win=win: e.memset(self.RC0[:, g, :], 1.0 / win))
            for tcol in range(win - 1):
                P(lambda e, g=g, tcol=tcol: e.memset(self.RC0[:, g, tcol:tcol + 1], 1.0 / (tcol + 1)))
        for l in range(L):
            for j in range(4):
                self.s.op("pool", [], [self.CAc[l][j]], lambda e, l=l, j=j: e.memset(self.CA[:, l, j, :], 0.0))
                self.s.op("pool", [], [self.CCc[l][j]], lambda e, l=l, j=j: e.memset(self.CC[:, l, j, :], 0.0))
                self.s.op("pool", [], [self.CDc[l][j]], lambda e, l=l, j=j: e.memset(self.CD[:, l, j, :], 0.0))
        TAc, STc = self.TAc, self.STc
        for l in range(L):
            r = 0
            for nm in ("ln1_g", "ln1_b", "ln2_g", "ln2_b", "ln3_g", "ln3_b", "ln4_g", "ln4_b"):
                self.setup_dma("sp", self.PRAW[r:r + 8, l, :], d[nm][l].rearrange("(k p) -> k p", p=128), self.PRAWc)
                r += 8
            self.setup_dma("sp", self.PRAW[r:r + 32, l, :], d["b_gate"][l].rearrange("(k p) -> k p", p=128), self.PRAWc)
            r += 32
            for nm in ("pool_scale", "gmlp_ln_g", "gmlp_ln_b", "conv_db", "conv_ln_g", "conv_ln_b"):
                self.setup_dma("sp", self.PRAW[r:r + 4, l, :], d[nm][l].rearrange("(k p) -> k p", p=128), self.PRAWc)
                r += 4
            assert r == 120
            self.setup_dma("sp", self.TA[0:31, l, :], d["conv_dw"][l], TAc[l])
            self.setup_dma("sp", self.TA[32:35, l, :], d["sc_w"][l], TAc[l])
            self.setup_dma("sp", self.ST[:, l, :].rearrange("p (h s) -> p h s", h=4),
                           d["gmlp_ws"][l].rearrange("h t s -> t h s"), STc[l])
            self.setup_dma("sp", self.GB[:, l, :], d["gmlp_b"][l:l + 1].rearrange("o h t -> o (h t)").partition_broadcast(128),
                           self.GBc)
            self.setup_dma("sp", self.GS[:, l, 0:4],
                           d["gmlp_ws"][l:l + 1, :, 0, 0:1].rearrange("o h a -> o (h a)").partition_broadcast(128),
                           self.GBc, slow=True)
            self.setup_dma("sp", self.GS[:, l, 4:8],
                           d["gmlp_b"][l:l + 1, :, 0:1].rearrange("o h a -> o (h a)").partition_broadcast(128),
                           self.GBc, slow=True)
            self.setup_dma("sp", self.F[:, l, 0:512].rearrange("p (g dd) -> p g dd", g=4),
                           d["pool_w"][l].rearrange("g c dd -> c g dd"), self.Fc[l])
        self.setup_dma("sp", self.STG[:, :, :], d["mem_prompt"].rearrange("(mt p) dd -> p mt dd", p=128), self.STGc[0])
        self.setup_cells.append(self.STGc[1])
        tot = self.s.count["setup"]
        for c in self.setup_cells:
            c.w = ("setup", tot)
            c.r = {}
        for l in range(L):
            self.cp("act", self.POOLW[:, l, :, :], self.F[:, l, 0:512].rearrange("p (g dd) -> p g dd", g=4),
                    [self.Fc[l]], [self.POOLWc])
        for l in range(L):
            for h in range(4):
                wv = self.ST[:, l, h * 128:(h + 1) * 128]
                self.s.op("pool", [STc[l]], [STc[l]],
                          lambda e, wv=wv: e.affine_select(out=wv, in_=wv, pattern=[[-1, 128]], compare_op=ALU.is_ge,
                                                           fill=0.0, base=0, channel_multiplier=1))
        for l in range(L):
            b = self.next_bank()
            self.tr_to_bank(self.PRAW[0:120, l, :], 120, 128, b, 0, [self.PRAWc])
            self.cp("act", self.PAR[:, l, 0:120], PS[:, b, 0:120], [bankc[b]], [self.PARc])
            for j in range(4):
                b = self.next_bank()
                self.tr_to_bank(self.TA[0:31, l, j * 128:(j + 1) * 128], 31, 128, b, 0, [TAc[l]])
                self.cp("act", self.DWT[:, l, j, :], PS[:, b, 0:31], [bankc[b]], [self.DWTc])
                b = self.next_bank()
                self.tr_to_bank(self.TA[32:35, l, j * 128:(j + 1) * 128], 3, 128, b, 0, [TAc[l]], p0=32)
                self.cp("act", self.SCW[:, l, j, :], PS[:, b, 0:3], [bankc[b]], [self.DWTc])
            for h in range(4):
                wv = self.ST[:, l, h * 128:(h + 1) * 128]
                b = self.next_bank()
                self.tr_to_bank(wv, 128, 128, b, 0, [STc[l]])
                self.cp("act", self.WT[:, l, h, :], PS[:, b, 0:128], [bankc[b]], [self.WTc])
        for k in range(KD):
            b = self.next_bank()
            for mt in range(2):
                self.tr_to_bank(self.STG[:, mt, k * 128:(k + 1) * 128], 128, 128, b, mt * 128, [self.STGc[0], self.STGc[1]])
            self.cp("act", self.G[:, k, 0:NMEM], PS[:, b, 0:256], [bankc[b]], [self.Gc[k][0]])
        for l in range(L):
            for (nm, okey) in (("xa_wk", "mk_p"), ("xa_wv", "mv_p")):
                for i in range(4):
                    sl = self.wtake(nm, l, i)
                    for mt in range(2):
                        b = self.mm([(self.G[:, k, mt * 128:(mt + 1) * 128], self.wap(sl, k, 0, 256)) for k in range(KD)],
                                    256, [self.Gc[k][0] for k in range(KD)] + [sl.cell])
                        sg = self.next_stg()
                        self.cp("act", self.STG[:, sg, 0:256], PS[:, b, 0:256], [bankc[b]], [self.STGc[sg]])
                        self.dma("sp", "stg%d" % sg, o[okey][l, mt * 128:(mt + 1) * 128, i * 256:(i + 1) * 256],
                                 self.STG[:, sg, 0:256], [self.STGc[sg]], [])
                        if nm == "xa_wv":
                            self.cp("dve", self.VB[:, l, mt, i * 256:(i + 1) * 256], PS[:, b, 0:256], [bankc[b]], [self.VBc[l]])
                    if nm == "xa_wk":
                        for c in range(2):
                            dc = i * 2 + c
                            b = self.mm([(self.wap(sl, k, c * 128, 128), self.G[:, k, 0:NMEM]) for k in range(KD)], NMEM,
                                        [self.Gc[k][0] for k in range(KD)] + [sl.cell])
                            self.cp("dve", self.KT[:, l, dc, :], PS[:, b, 0:NMEM], [bankc[b]], [self.KTc[l]])

    def load_x(self, gi, tiles):
        d = self.d
        PS, bankc = self.PS, self.bankc
        for t in tiles:
            ti = t["ti"]
            if t["kind"] == "p":
                for tb in range(t["n"] // 128):
                    sg = self.next_stg()
                    r0 = t["seq0"] + tb * 128
                    self.dma("sp", "stg%d" % sg, self.STG[:, sg, :], d["x_prompt"][r0:r0 + 128, :], [], [self.STGc[sg]])
                    for half in range(2):
                        b = self.next_bank()
                        for c in range(4):
                            k = half * 4 + c
                            self.tr_to_bank(self.STG[:, sg, k * 128:(k + 1) * 128], 128, 128, b, c * 128, [self.STGc[sg]])
                        col = t["off"] + tb * 128
                        self.cp("act", self.X[:, half * 4:half * 4 + 4, col:col + 128],
                                PS[:, b, :].rearrange("p (c t) -> p c t", c=4), [bankc[b]], [self.Xc[ti]])
                        self.cp("dve", self.XB[:, half * 4:half * 4 + 4, col:col + 128],
                                self.X[:, half * 4:half * 4 + 4, col:col + 128], [self.Xc[ti]], [self.XBc[ti]])
            else:
                sg = self.next_stg()
                self.dma("sp", "stg%d" % sg, self.STG[0:NS, sg, :], d["x_sample"], [], [self.STGc[sg]])
                b = self.next_bank()
                for k in range(KD):
                    self.tr_to_bank(self.STG[0:NS, sg, k * 128:(k + 1) * 128], NS, 128, b, k * NS, [self.STGc[sg]])
                self.cp("act", self.X3(t), PS[:, b, 0:KD * NS].rearrange("p (k t) -> p k t", k=KD), [bankc[b]], [self.Xc[ti]])
                self.cp("dve", self.XB3(t), self.X3(t), [self.Xc[ti]], [self.XBc[ti]])

    def store_y(self, gi, tiles):
        o = self.o
        PS, bankc = self.PS, self.bankc
        for t in tiles:
            ti = t["ti"]
            if t["kind"] == "p":
                for tb in range(t["n"] // 128):
                    sg = self.next_stg()
                    col = t["off"] + tb * 128
                    for half in range(2):
                        b = self.next_bank()
                        for c in range(4):
                            k = half * 4 + c
                            self.tr_to_bank(self.X[:, k, col:col + 128], 128, 128, b, c * 128, [self.Xc[ti]])
                        self.cp("act" if half == 0 else "dve", self.STG[:, sg, half * 512:(half + 1) * 512], PS[:, b, :],
                                [bankc[b]], [self.STGc[sg]])
                    r0 = t["seq0"] + tb * 128
                    self.dma("sp", "stg%d" % sg, o["y_prompt"][r0:r0 + 128, :], self.STG[:, sg, :], [self.STGc[sg]], [])
            else:
                sg = self.next_stg()
                for half in range(2):
                    b = self.next_bank()
                    for c in range(4):
                        k = half * 4 + c
                        self.tr_to_bank(self.Xv(k, t), 128, NS, b, c * 128, [self.Xc[ti]])
                    self.cp("act", self.STG[0:NS, sg, half * 512:(half + 1) * 512], PS[0:NS, b, :], [bankc[b]], [self.STGc[sg]])
                self.dma("sp", "stg%d" % sg, o["y_sample"], self.STG[0:NS, sg, :], [self.STGc[sg]], [])

    def final_outputs(self):
        o = self.o
        for l in range(L):
            self.out_rows([self.CA[:, l, j, :] for j in range(4)], 15, o["pool_p"][l], self.CAc[l])
            self.out_rows([self.CC[:, l, j, :] for j in range(4)], 30, o["conv_p"][l], self.CCc[l])
            self.out_rows([self.CD[:, l, j, :] for j in range(4)], 2, o["sc_p"][l], self.CDc[l])

    def build(self):
        nc = self.nc
        self.declare_dram()
        with ExitStack() as es:
            self.alloc(es)
            specs = self.specs_setup()
            for gi in range(NGRP):
                for l in range(L):
                    specs += self.specs_layer(l)
            self.wplan(specs)
            steps = [self.setup]
            for gi in range(NGRP):
                tiles = self.tiles_of(gi)
                steps.append(lambda gi=gi, tiles=tiles: (setattr(self.s, "group", gi), self.load_x(gi, tiles)))
                for l in range(L):
                    steps.append(lambda tiles=tiles, l=l: self.ffn(tiles, l, 0))
                    steps.append(lambda gi=gi, tiles=tiles, l=l: self.mixer(gi, tiles, l))
                    steps.append(lambda gi=gi, tiles=tiles, l=l: self.attn(gi, tiles, l))
                    steps.append(lambda tiles=tiles, l=l: self.ffn(tiles, l, 1))
                steps.append(lambda gi=gi, tiles=tiles: self.store_y(gi, tiles))
            steps.append(self.final_outputs)
            limit = self.step_limit if self.step_limit is not None else len(steps)
            for st_ in steps[:limit]:
                st_()
            if limit >= len(steps):
                assert self.wq_pos == len(self.wq_specs)
            self.emit(es)
        return nc

    def emit(self, es):
        nc = self.nc
        s = self.s
        semnames = list(ENGS) + s.dma_sems
        sems = {}
        for nm in semnames:
            sems[nm] = es.enter_context(nc.semaphore("s_" + nm))
        block = es.enter_context(nc.Block())
        final_waits = [(nm, s.count[nm]) for nm in s.dma_sems]

        def run(engname, e):
            if engname in ("dve", "act") and s.count.get("pool", 0):
                e.wait_ge(sems["pool"], s.count["pool"])
            for (need, fn, semkey, inc) in s.streams[engname]:
                for k, v in need.items():
                    e.wait_ge(sems[k], v)
                if inc:
                    fn(e).then_inc(sems[semkey], 1)
                else:
                    fn(e, sems[semkey])
            if engname == "sp":
                for nm, v in final_waits:
                    e.wait_ge(sems[nm], v)

        @block.tensor
        def _(e):
            run("pe", e)

        @block.scalar
        def _(e):
            run("act", e)

        @block.vector
        def _(e):
            run("dve", e)

        @block.gpsimd
        def _(e):
            run("pool", e)

        @block.sync
        def _(e):
            run("sp", e)


_WEIGHT_KEYS = ["ln1_g", "ln1_b", "ffn1_w1", "ffn1_w3", "ffn1_w2", "w_in", "w_gate", "b_gate", "pool_w", "pool_scale",
                "pool_proj", "gmlp_ln_g", "gmlp_ln_b", "gmlp_ws", "gmlp_b", "gmlp_proj", "conv_dw", "conv_db",
                "conv_ln_g", "conv_ln_b", "conv_proj", "sc_w", "sc_proj", "w_o", "ln2_g", "ln2_b", "xa_wq", "xa_wk",
                "xa_wv", "xa_wo", "ln3_g", "ln3_b", "ffn2_w1", "ffn2_w3", "ffn2_w2", "ln4_g", "ln4_b"]


def kernel(**inputs):
    inp = {k: np.ascontiguousarray(np.asarray(v), dtype=np.float32) for k, v in inputs.items()}
    nc = Builder().build()
    in_maps = []
    for c in range(NCORES):
        m = {k: inp[k] for k in _WEIGHT_KEYS}
        m["x_prompt"] = np.ascontiguousarray(inp["x_prompt"][c])
        m["x_sample"] = np.ascontiguousarray(inp["x_sample"][c * NS:(c + 1) * NS, 0, :])
        m["mem_prompt"] = np.ascontiguousarray(inp["mem_prompt"][c])
        m["state_pool"] = np.ascontiguousarray(inp["state_pool"][:, c * NS:(c + 1) * NS])
        m["state_conv"] = np.ascontiguousarray(inp["state_conv"][:, c * NS:(c + 1) * NS])
        m["state_shortconv"] = np.ascontiguousarray(inp["state_shortconv"][:, c * NS:(c + 1) * NS])
        m["cache_mem_k"] = np.ascontiguousarray(inp["cache_mem_k"][:, c * NS:(c + 1) * NS]).reshape(L, NS, NMEM, D)
        m["cache_mem_v"] = np.ascontiguousarray(inp["cache_mem_v"][:, c * NS:(c + 1) * NS]).reshape(L, NS, NMEM, D)
        in_maps.append(m)
    res = run_bass_kernel_spmd(nc, in_maps, core_ids=list(range(NCORES)))
    R = res.results
    y_prompt = np.stack([R[c]["y_prompt"] for c in range(NCORES)], 0)
    y_sample = np.concatenate([R[c]["y_sample"] for c in range(NCORES)], 0)[:, None, :]
    pool_p = np.stack([R[c]["pool_p"] for c in range(NCORES)], 1)
    conv_p = np.stack([R[c]["conv_p"] for c in range(NCORES)], 1)
    sc_p = np.stack([R[c]["sc_p"] for c in range(NCORES)], 1)
    mk_p = np.stack([R[c]["mk_p"] for c in range(NCORES)], 1).reshape(L, NCORES, NMEM, 4, 256)
    mv_p = np.stack([R[c]["mv_p"] for c in range(NCORES)], 1).reshape(L, NCORES, NMEM, 4, 256)
    pool_s = np.concatenate([R[c]["pool_s"] for c in range(NCORES)], 1)
    conv_s = np.concatenate([R[c]["conv_s"] for c in range(NCORES)], 1)
    sc_s = np.concatenate([R[c]["sc_s"] for c in range(NCORES)], 1)
    gv_s = np.concatenate([R[c]["gv_s"] for c in range(NCORES)], 1)[:, :, None, :]
    outs = (y_prompt, y_sample, pool_p, conv_p, sc_p, mk_p, mv_p, pool_s, conv_s, sc_s, gv_s)
    return tuple(np.ascontiguousarray(a, dtype=np.float32) for a in outs)
```

```python
import numpy as np
from contextlib import ExitStack
import concourse.bass as bass
import concourse.mybir as mybir
from concourse.bass_utils import run_bass_kernel_spmd

F32 = mybir.dt.float32
BF16 = mybir.dt.bfloat16
AF = mybir.ActivationFunctionType
ALU = mybir.AluOpType
AX = mybir.AxisListType

NCORES = 8
D = 1024
KD = 8
DFF = 2816
KF = 22
SEQ = 2048
NS = 16
L = 2
TILE = 512
TPG = 1
NGRP = (SEQ // TILE) // TPG
NPG = TPG * TILE
TG = NPG + NS
HP = 30
NMEM = 256
ALPHA = float((2.0 * L) ** 0.25)
EPS = 1e-5
WINS = (2, 4, 8, 16)
NSLOT = 6
NSTG = 3
NKV = 3
SLOT_E = 2048
NBANK = 7
GCH = 24
NTA = 6

ENGS = ("pe", "act", "dve", "pool", "sp")
STRICT_SAME_ENGINE = True
STRICT_FLAG = [True]


class Cell:
    __slots__ = ("w", "r", "name")

    def __init__(self, name=""):
        self.w = None
        self.r = {}
        self.name = name


class Sched:
    def __init__(self):
        self.streams = {e: [] for e in ENGS}
        self.count = {}
        self.waited = {e: {} for e in ENGS}
        self.dma_sems = []
        self.stage = ""
        self.group = -1
        self.labels = {}

    def _deps(self, eng, reads, writes):
        need = {}

        def add(dep, same_ok):
            if dep is None:
                return
            k, v = dep
            if k == eng and same_ok and not STRICT_SAME_ENGINE:
                return
            if self.waited[eng].get(k, 0) >= v:
                return
            if need.get(k, 0) < v:
                need[k] = v

        for c in reads:
            add(c.w, False)
        for c in writes:
            add(c.w, True)
            for k, v in c.r.items():
                add((k, v), True)
        for k, v in need.items():
            self.waited[eng][k] = v
        return need

    def op(self, eng, reads, writes, fn, after=()):
        if after:
            tmp = Cell("after")
            for c in after:
                if c.w is not None:
                    k, v = c.w
                    if tmp.r.get(k, 0) < v:
                        tmp.r[k] = v
                for k, v in c.r.items():
                    if tmp.r.get(k, 0) < v:
                        tmp.r[k] = v
            writes = list(writes) + [tmp]
            saved = STRICT_FLAG[0]
            STRICT_FLAG[0] = True
            need = self._deps(eng, reads, writes)
            STRICT_FLAG[0] = saved
            writes = writes[:-1]
        else:
            need = self._deps(eng, reads, writes)
        self.count[eng] = self.count.get(eng, 0) + 1
        v = self.count[eng]
        self.streams[eng].append((need, fn, eng, 1))
        self.labels.setdefault(eng, []).append("g%d.%s" % (self.group, self.stage))
        for c in reads:
            c.r[eng] = v
        for c in writes:
            c.w = (eng, v)
            c.r = {}

    def dma(self, q, sem, n, reads, writes, fn):
        if sem not in self.count:
            self.count[sem] = 0
            self.dma_sems.append(sem)
        need = self._deps(q, reads, writes)
        self.count[sem] += 16 * n
        v = self.count[sem]
        self.streams[q].append((need, fn, sem, 0))
        for c in reads:
            c.r[sem] = v
        for c in writes:
            c.w = (sem, v)
            c.r = {}


class Slab:
    def __init__(self, slot, cell, cw):
        self.slot = slot
        self.cell = cell
        self.cw = cw


class Builder:
    def __init__(self, step_limit=None):
        self.step_limit = step_limit
        self.nc = bass.Bass("TRN2", target_bir_lowering=False)
        self.s = Sched()
        self.bank_i = 0
        self.ta_i = 0
        self.stg_i = 0
        self.setup_cells = []

    def declare_dram(self):
        nc = self.nc

        def inp(name, shape):
            return nc.dram_tensor(name, list(shape), F32, kind="ExternalInput").ap()

        def outp(name, shape):
            return nc.dram_tensor(name, list(shape), F32, kind="ExternalOutput").ap()

        d = {}
        d["x_prompt"] = inp("x_prompt", (SEQ, D))
        d["x_sample"] = inp("x_sample", (NS, D))
        d["mem_prompt"] = inp("mem_prompt", (NMEM, D))
        d["state_pool"] = inp("state_pool", (L, NS, 15, 512))
        d["state_conv"] = inp("state_conv", (L, NS, 30, 512))
        d["state_shortconv"] = inp("state_shortconv", (L, NS, 2, 512))
        d["cache_mem_k"] = inp("cache_mem_k", (L, NS, NMEM, D))
        d["cache_mem_v"] = inp("cache_mem_v", (L, NS, NMEM, D))
        for nm in ("ln1_g", "ln1_b", "ln2_g", "ln2_b", "ln3_g", "ln3_b", "ln4_g", "ln4_b"):
            d[nm] = inp(nm, (L, D))
        for nm in ("ffn1_w1", "ffn1_w3", "ffn2_w1", "ffn2_w3"):
            d[nm] = inp(nm, (L, D, DFF))
        for nm in ("ffn1_w2", "ffn2_w2"):
            d[nm] = inp(nm, (L, DFF, D))
        d["w_in"] = inp("w_in", (L, D, 4096))
        d["w_gate"] = inp("w_gate", (L, D, 4096))
        d["b_gate"] = inp("b_gate", (L, 4096))
        d["pool_w"] = inp("pool_w", (L, 4, 128, 128))
        d["pool_scale"] = inp("pool_scale", (L, 512))
        for nm in ("pool_proj", "gmlp_proj", "conv_proj", "sc_proj"):
            d[nm] = inp(nm, (L, 512, D))
        d["gmlp_ln_g"] = inp("gmlp_ln_g", (L, 512))
        d["gmlp_ln_b"] = inp("gmlp_ln_b", (L, 512))
        d["gmlp_ws"] = inp("gmlp_ws", (L, 4, 128, 128))
        d["gmlp_b"] = inp("gmlp_b", (L, 4, 128))
        d["conv_dw"] = inp("conv_dw", (L, 31, 512))
        d["conv_db"] = inp("conv_db", (L, 512))
        d["conv_ln_g"] = inp("conv_ln_g", (L, 512))
        d["conv_ln_b"] = inp("conv_ln_b", (L, 512))
        d["sc_w"] = inp("sc_w", (L, 3, 512))
        for nm in ("w_o", "xa_wq", "xa_wk", "xa_wv", "xa_wo"):
            d[nm] = inp(nm, (L, D, D))
        o = {}
        o["y_prompt"] = outp("y_prompt", (SEQ, D))
        o["y_sample"] = outp("y_sample", (NS, D))
        o["pool_p"] = outp("pool_p", (L, 15, 512))
        o["conv_p"] = outp("conv_p", (L, 30, 512))
        o["sc_p"] = outp("sc_p", (L, 2, 512))
        o["mk_p"] = outp("mk_p", (L, NMEM, D))
        o["mv_p"] = outp("mv_p", (L, NMEM, D))
        o["pool_s"] = outp("pool_s", (L, NS, 15, 512))
        o["conv_s"] = outp("conv_s", (L, NS, 30, 512))
        o["sc_s"] = outp("sc_s", (L, NS, 2, 512))
        o["gv_s"] = outp("gv_s", (L, NS, 512))
        self.d = d
        self.o = o

    def alloc(self, es):
        nc = self.nc

        def sb(name, shape, dt):
            return es.enter_context(nc.sbuf_tensor(name, list(shape), dt))

        self.X = sb("X", (128, KD, TG), F32)
        self.XB = sb("XB", (128, KD, TG), BF16)
        self.G = sb("G", (128, GCH, TG), BF16)
        self.F = sb("F", (128, 8, HP + TG), F32)
        self.ST = sb("ST", (128, 5, 512), F32)
        self.TA = sb("TA", (128, NTA, 512), F32)
        self.WS = sb("WS", (128, NSLOT, SLOT_E), BF16)
        self.WST = sb("WST", (128, NSTG - 1, SLOT_E), F32)
        self.STG = sb("STG", (128, 2, 1024), F32)
        self.IDF = sb("IDF", (128, 128), F32)
        self.IDB = sb("IDB", (128, 128), BF16)
        self.ONESB = sb("ONESB", (128, 128), BF16)
        self.ONESF = sb("ONESF", (128, 128), F32)
        self.RC0 = sb("RC0", (128, 4, 16), F32)
        self.PRAW = sb("PRAW", (128, L, 128), F32)
        self.PAR = sb("PAR", (128, L, 128), F32)
        self.DWT = sb("DWT", (128, L, 4, 31), F32)
        self.SCW = sb("SCW", (128, L, 4, 3), F32)
        self.POOLW = sb("POOLW", (128, L, 4, 128), BF16)
        self.WT = sb("WT", (128, L, 4, 128), BF16)
        self.GB = sb("GB", (128, L, 512), F32)
        self.GS = sb("GS", (128, L, 8), F32)
        self.KT = sb("KT", (128, L, KD, NMEM), BF16)
        self.VB = sb("VB", (128, L, 2, D), BF16)
        self.CA = sb("CA", (128, L, 4, 15), F32)
        self.CC = sb("CC", (128, L, 4, 30), F32)
        self.CD = sb("CD", (128, L, 4, 2), F32)
        self.SPOOL = sb("SPOOL", (128, 4, NS, 15), F32)
        self.SCONV = sb("SCONV", (128, 4, NS, 31), F32)
        self.SSC = sb("SSC", (128, 4, NS, 3), F32)
        self.KVS = sb("KVS", (128, NKV, D), F32)
        self.kv_i = 0
        self.EXTB = sb("EXTB", (128, 4, HP + NPG + 2), BF16)
        self.QS = sb("QS", (16, D), F32)
        self.SS = sb("SS", (128, 128), F32)
        self.ES = sb("ES", (128, 128), F32)
        self.RS = sb("RS", (128, 4, NS), F32)
        self.PS = es.enter_context(nc.psum_tensor("PS", [128, NBANK, 512], F32))
        self.PSB = es.enter_context(nc.psum_tensor("PSB", [128, 2, 512], BF16))

        C = Cell
        self.Xc = [C("X%d" % i) for i in range(TPG + 1)]
        self.XBc = [C("XB%d" % i) for i in range(TPG + 1)]
        self.Gc = [[C("G%d_%d" % (c, i)) for i in range(TPG + 1)] for c in range(GCH)]
        self.Fc = [C("F%d" % j) for j in range(8)]
        self.STc = [C("ST%d" % j) for j in range(5)]
        self.TAc = [C("TA%d" % j) for j in range(NTA)]
        self.WSc = [C("WS%d" % j) for j in range(NSLOT)]
        self.WSTc = [C("WST%d" % j) for j in range(NSTG - 1)]
        self.STGc = [C("STG%d" % j) for j in range(2)]
        self.bankc = [C("bank%d" % j) for j in range(NBANK)]
        _psb = C("psb")
        self.PSBc = [_psb, _psb]
        self.constc = C("const")
        self.PRAWc = C("praw")
        self.PARc = C("par")
        self.DWTc = C("dwt")
        self.POOLWc = C("poolw")
        self.WTc = C("wt")
        self.GBc = C("gb")
        self.MEMTc = C("memt")
        self.KTc = [C("kt%d" % l) for l in range(L)]
        self.VBc = [C("vb%d" % l) for l in range(L)]
        self.CAc = [[C() for j in range(4)] for l in range(L)]
        self.CCc = [[C() for j in range(4)] for l in range(L)]
        self.CDc = [[C() for j in range(4)] for l in range(L)]
        self.SPOOLc = C("spool")
        self.SCONVc = [C("sconv%d" % j) for j in range(4)]
        self.SSCc = [C("ssc%d" % j) for j in range(4)]
        self.KVSc = [C("kvs%d" % j) for j in range(NKV)]
        self.EXTBc = [C("extb%d" % j) for j in range(4)]
        self.QSc = C("qs")
        self.SSc = C("ss")
        self.ESc = C("es")
        self.RSc = C("rs")

    def next_bank(self):
        b = self.bank_i
        self.bank_i = (self.bank_i + 1) % NBANK
        return b

    def next_ta(self):
        a = self.ta_i
        self.ta_i = (self.ta_i + 1) % NTA
        return a

    def next_stg(self):
        a = self.stg_i
        self.stg_i = (self.stg_i + 1) % 2
        return a

    def mm(self, pairs, n, reads, m=128, bank=None, c0=0, writes_extra=()):
        b = self.next_bank() if bank is None else bank
        out = self.PS[0:m, b, c0:c0 + n]
        npair = len(pairs)

        def fn(e):
            ins = None
            for i, (l, r) in enumerate(pairs):
                ins = e.matmul(out, lhsT=l, rhs=r, start=(i == 0), stop=(i == npair - 1))
            return ins

        self.s.op("pe", reads, [self.bankc[b]] + list(writes_extra), fn)
        return b

    def act(self, out, in_, func, reads, writes, **kw):
        self.s.op("act", reads, writes, lambda e: e.activation(out=out, in_=in_, func=func, **kw))

    def tt(self, eng, out, in0, in1, op, reads, writes):
        self.s.op(eng, reads, writes, lambda e: e.tensor_tensor(out=out, in0=in0, in1=in1, op=op))

    def stt(self, eng, out, in0, scalar, in1, op0, op1, reads, writes):
        self.s.op(eng, reads, writes, lambda e: e.scalar_tensor_tensor(out=out, in0=in0, scalar=scalar, in1=in1,
                                                                      op0=op0, op1=op1))

    def ts(self, eng, out, in0, s1, s2, op0, op1, reads, writes):
        if s2 is None:
            self.s.op(eng, reads, writes, lambda e: e.tensor_scalar(out=out, in0=in0, scalar1=s1, scalar2=None, op0=op0))
        else:
            self.s.op(eng, reads, writes, lambda e: e.tensor_scalar(out=out, in0=in0, scalar1=s1, scalar2=s2,
                                                                    op0=op0, op1=op1))

    def cp(self, eng, out, in_, reads, writes):
        if eng == "act":
            self.s.op("act", reads, writes, lambda e: e.activation(out=out, in_=in_, func=AF.Copy))
        else:
            self.s.op(eng, reads, writes, lambda e: e.tensor_copy(out=out, in_=in_))

    def dma(self, q, sem, out, in_, reads, writes, slow=False):
        def fn(e, semh):
            if slow:
                return e.dma_start(out=out, in_=in_, allow_slow_non_contiguous=True).then_inc(semh, 16)
            return e.dma_start(out=out, in_=in_).then_inc(semh, 16)

        self.s.dma(q, sem, 1, reads, writes, fn)

    def setup_dma(self, q, out, in_, cell, slow=False):
        self.dma(q, "setup", out, in_, [], [], slow=slow)
        if cell not in self.setup_cells:
            self.setup_cells.append(cell)

    def tr_to_bank(self, in_ap, k, m, b, c0, reads, p0=0):
        out = self.PS[0:m, b, c0:c0 + k]
        idn = self.IDF[p0:p0 + k, p0:p0 + k]
        self.s.op("pe", reads + [self.constc], [self.bankc[b]],
                  lambda e: e.transpose(out=out, in_=in_ap, identity=idn))

    def out_rows(self, src_aps, m, dst_ap, reads, width=128):
        b = self.next_bank()
        for j, a in enumerate(src_aps):
            self.tr_to_bank(a, 128, m, b, j * 128, reads)
        sg = self.next_stg()
        w = 128 * len(src_aps)
        self.cp("act", self.STG[0:m, sg, 0:w], self.PS[0:m, b, 0:w], [self.bankc[b]], [self.STGc[sg]])
        self.dma("sp", "stg%d" % sg, dst_ap, self.STG[0:m, sg, 0:w], [self.STGc[sg]], [])

    def wload(self, kind, l, idx, which=0):
        d = self.d
        slot = self.ws_i
        self.ws_i = (self.ws_i + 1) % NSLOT
        sslot = self.wst_i
        self.wst_i = (self.wst_i + 1) % NSTG
        cell = self.WSc[slot]
        if sslot < NSTG - 1:
            scells = [self.WSTc[sslot]]
            stg_flat = self.WST[:, sslot, :]
        else:
            scells = [self.KVSc[0], self.KVSc[1]]
            stg_flat = self.KVS[:, 0:2, :].rearrange("p a c -> p (a c)")

        def src_cols(w2d, c0, cw, r0=0, kt=None):
            v = w2d.rearrange("(k p) n -> p k n", p=128)
            if kt is not None:
                v = v[:, r0:r0 + kt, :]
            return v[:, :, c0:c0 + cw]

        def dst(kt, cw, e0=0):
            return stg_flat[:, e0:e0 + kt * cw].rearrange("p (k c) -> p k c", c=cw)

        dmas = []
        if kind in ("w1", "w3"):
            cw = 256
            dmas.append((dst(KD, cw), src_cols(d["ffn%d_%s" % (which + 1, kind)][l], idx * 256, cw)))
            ne = KD * cw
        elif kind == "w2":
            cw = 128
            dc, half = idx // 2, idx % 2
            dmas.append((dst(11, cw), src_cols(d["ffn%d_w2" % (which + 1)][l], dc * 128, cw, r0=half * 11, kt=11)))
            ne = 11 * cw
        elif kind in ("w_in", "w_o", "xa_wq", "xa_wk", "xa_wv", "xa_wo"):
            cw = 256
            dmas.append((dst(KD, cw), src_cols(d[kind][l], idx * 256, cw)))
            ne = KD * cw
        elif kind == "gate":
            cw = 256
            j, pr = idx // 2, idx % 2
            wd = stg_flat[:, 0:KD * 256].rearrange("p (k b c) -> p k b c", b=2, c=128)
            for b2 in range(2):
                dmas.append((wd[:, :, b2, :], src_cols(d["w_gate"][l], (pr * 2 + b2) * 1024 + j * 128, 128)))
            ne = KD * cw
        elif kind == "proj":
            cw = 128
            for bi, nm in enumerate(("pool_proj", "gmlp_proj", "conv_proj", "sc_proj")):
                dmas.append((dst(4, cw, e0=bi * 512), src_cols(d[nm][l], idx * 128, cw)))
            ne = 2048
        else:
            raise ValueError(kind)
        n = len(dmas)
        spec = (kind, l, idx, which)
        cached = self.wcache.get(spec)
        if cached is not None:
            si, ccell = cached
            src = self.wscr[si, :, 0:ne]

            def fnl(e, semh):
                return e.dma_start(out=self.WS[:, slot, 0:ne], in_=src).then_inc(semh, 16)

            self.wst_i = (self.wst_i - 1) % NSTG
            self.s.dma("sp", "wsl%d" % slot, 1, [ccell], [cell], fnl)
            return Slab(slot, cell, cw)

        def fn(e, semh):
            ins = None
            for (o, i) in dmas:
                ins = e.dma_start(out=o, in_=i).then_inc(semh, 16)
            return ins

        self.s.dma("sp", "wst%d" % sslot, n, [], scells, fn)
        self.cast_i = getattr(self, "cast_i", 0) + 1
        self.cp("act" if self.cast_i % 2 else "dve", self.WS[:, slot, 0:ne], stg_flat[:, 0:ne], scells, [cell])
        sl = Slab(slot, cell, cw)
        if spec in self.wreuse:
            si = len(self.wcache)
            ccell = Cell("wc%d" % si)
            self.wcache[spec] = (si, ccell)
            sl.store = (si, ccell, ne)
        return sl

    def wplan(self, specs):
        seen, reuse = set(), set()
        for sp in specs:
            if sp in seen:
                reuse.add(sp)
            seen.add(sp)
        self.wreuse = reuse
        self.wcache = {}
        self.wscr = self.nc.dram_tensor("wscratch", [max(1, len(reuse)), 128, SLOT_E], BF16, kind="Internal").ap()
        self.wq_specs = list(specs)
        self.wq_loaded = []
        self.wq_pos = 0
        self.ws_i = 0
        self.wst_i = 0

    def wtake(self, kind, l, idx, which=0, keep=1):
        spec = (kind, l, idx, which)
        while len(self.wq_loaded) < min(max(self.wq_pos + NSLOT - keep, self.wq_pos + 1), len(self.wq_specs)):
            sp = self.wq_specs[len(self.wq_loaded)]
            self.wq_loaded.append(self.wload(*sp))
        assert self.wq_specs[self.wq_pos] == spec, (self.wq_specs[self.wq_pos], spec)
        sl = self.wq_loaded[self.wq_pos]
        self.wq_pos += 1
        st_ = getattr(sl, "store", None)
        if st_ is not None:
            si, ccell, ne = st_
            sl.store = None
            dstc = self.wscr[si, :, 0:ne]
            src = self.WS[:, sl.slot, 0:ne]

            def fns(e, semh):
                return e.dma_start(out=dstc, in_=src).then_inc(semh, 16)

            self.s.dma("sp", "wss%d" % sl.slot, 1, [sl.cell], [ccell], fns)
        return sl

    def wap(self, sl, k, c0, cn, e0=0):
        base = e0 + k * sl.cw + c0
        return self.WS[:, sl.slot, base:base + cn]

    def specs_setup(self):
        sp = []
        for l in range(L):
            for nm in ("xa_wk", "xa_wv"):
                for i in range(4):
                    sp.append((nm, l, i, 0))
        return sp

    def specs_ffn(self, l, which):
        sp = []
        for s_ in range(11):
            sp.append(("w1", l, s_, which))
            sp.append(("w3", l, s_, which))
        for i in range(16):
            sp.append(("w2", l, i, which))
        return sp

    def specs_layer(self, l):
        sp = self.specs_ffn(l, 0)
        for i in range(16):
            sp.append(("w_in", l, i, 0))
        for j in range(8):
            sp.append(("gate", l, 2 * j, 0))
            sp.append(("gate", l, 2 * j + 1, 0))
            sp.append(("proj", l, j, 0))
        for nm in ("w_o", "xa_wq", "xa_wo"):
            for i in range(4):
                sp.append((nm, l, i, 0))
        sp += self.specs_ffn(l, 1)
        return sp

    def tiles_of(self, gi):
        tl = []
        for t in range(TPG):
            tl.append(dict(kind="p", off=t * TILE, n=TILE, seq0=(gi * TPG + t) * TILE, ti=t))
        if gi == NGRP - 1:
            tl.append(dict(kind="s", off=NPG, n=NS, seq0=0, ti=TPG))
        return tl

    def Xv(self, k, t):
        return self.X[:, k, t["off"]:t["off"] + t["n"]]

    def X3(self, t):
        return self.X[:, :, t["off"]:t["off"] + t["n"]]

    def XBv(self, k, t):
        return self.XB[:, k, t["off"]:t["off"] + t["n"]]

    def XB3(self, t):
        return self.XB[:, :, t["off"]:t["off"] + t["n"]]

    def Gv(self, c, t):
        return self.G[:, c, t["off"]:t["off"] + t["n"]]

    def G3(self, c0, cn, t):
        return self.G[:, c0:c0 + cn, t["off"]:t["off"] + t["n"]]

    def Fv(self, j, t):
        return self.F[:, j, HP + t["off"]:HP + t["off"] + t["n"]]

    def F3(self, j0, jn, t):
        return self.F[:, j0:j0 + jn, HP + t["off"]:HP + t["off"] + t["n"]]

    def par(self, l, col):
        return self.PAR[:, l, col:col + 1]

    def layernorm(self, t, C, src3, src_cells, zb3, zb_cells, zq3, zq_cells, eps, l, gcol, bcol, crit, post=None,
                  chunk_cells=None, pre=False):
        n = t["n"]
        ST = self.ST
        inv = 1.0 / (C * 128.0)
        if not pre:
            self.cp("dve", zb3, src3, src_cells, zb_cells)
            self.act(zq3, src3, AF.Square, src_cells, zq_cells)
        bs = self.mm([(self.ONESB[:, :], zb3[:, c, :]) for c in range(C)], n, zb_cells + [self.constc])
        bq = self.mm([(self.ONESB[:, :], zq3[:, c, :]) for c in range(C)], n, zq_cells + [self.constc])
        mean, msq, rstd, nmr = ST[:, 0, 0:n], ST[:, 1, 0:n], ST[:, 2, 0:n], ST[:, 3, 0:n]
        Sc = self.STc
        self.ts("dve", mean, self.PS[:, bs, 0:n], inv, None, ALU.mult, None, [self.bankc[bs]], [Sc[0]])
        self.tt("dve", msq, mean, mean, ALU.mult, [Sc[0]], [Sc[1]])
        self.stt("dve", msq, self.PS[:, bq, 0:n], inv, msq, ALU.mult, ALU.subtract, [self.bankc[bq], Sc[1]], [Sc[1]])
        self.ts("dve", msq, msq, 0.0, None, ALU.max, None, [Sc[1]], [Sc[1]])
        self.act(rstd, msq, AF.Sqrt, [Sc[1]], [Sc[2]], bias=float(eps), scale=1.0)
        self.s.op("dve", [Sc[2]], [Sc[2]], lambda e: e.reciprocal(out=rstd, in_=rstd))
        self.stt("dve", nmr, mean, -1.0, rstd, ALU.mult, ALU.mult, [Sc[0], Sc[2]], [Sc[3]])
        if chunk_cells is None:
            cc = [Cell("lnc%d" % c) for c in range(C)]
            after = list(src_cells)
        else:
            cc = chunk_cells
            after = []
        for c in range(C):
            x = src3[:, c, :]
            self.s.op("dve", [Sc[2], cc[c]] if chunk_cells is not None else [Sc[2]], [cc[c]],
                      lambda e, x=x: e.tensor_tensor(out=x, in0=x, in1=rstd, op=ALU.mult), after=after)
            self.tt("dve", x, x, nmr, ALU.add, [cc[c], Sc[3]], [cc[c]])
            o, func, cells = crit(c)
            self.act(o, x, func, [cc[c], self.PARc], cells, scale=self.par(l, gcol + c), bias=self.par(l, bcol + c))
        if post is not None:
            for c in range(C):
                o, func, cells = post(c)
                self.ts("dve", o, src3[:, c, :], self.par(l, gcol + c), self.par(l, bcol + c), ALU.mult, ALU.add,
                        [cc[c], self.PARc], cells + [cc[c]])

    def ln_x(self, tiles, l, idx, eps, pre=False):
        gcol = 16 * idx
        bcol = 16 * idx + 8
        for t in tiles:
            ti = t["ti"]
            xc = [self.Xc[ti]]
            xbc = [self.XBc[ti]]
            zq_cells = [self.Gc[c][ti] for c in range(8)]
            self.layernorm(t, 8, self.X3(t), xc, self.XB3(t), xbc, self.G3(0, 8, t), zq_cells, eps, l, gcol, bcol,
                           crit=lambda c, t=t, xbc=xbc: (self.XBv(c, t), AF.Identity, xbc),
                           post=lambda c, t=t, xc=xc: (self.Xv(c, t), AF.Identity, xc), pre=pre)

    def ln_pre(self, dc, t):
        ti = t["ti"]
        self.act(self.Gv(dc, t), self.Xv(dc, t), AF.Square, [self.Xc[ti]], [self.Gc[dc][ti]])
        self.cp("act", self.XBv(dc, t), self.Xv(dc, t), [self.Xc[ti]], [self.XBc[ti]])

    def ffn(self, tiles, l, which):
        self.s.stage = 'ffn.up'
        for s_ in range(11):
            s1 = self.wtake("w1", l, s_, which)
            s3 = self.wtake("w3", l, s_, which)
            for fc in range(2):
                f = s_ * 2 + fc
                for t in tiles:
                    n, ti = t["n"], t["ti"]
                    b1 = self.mm([(self.wap(s1, k, fc * 128, 128), self.XBv(k, t)) for k in range(KD)], n,
                                 [s1.cell, self.XBc[ti]])
                    b3 = self.mm([(self.wap(s3, k, fc * 128, 128), self.XBv(k, t)) for k in range(KD)], n,
                                 [s3.cell, self.XBc[ti]])
                    a = self.next_ta()
                    self.act(self.TA[:, a, 0:n], self.PS[:, b1, 0:n], AF.Silu, [self.bankc[b1]], [self.TAc[a]])
                    self.tt("dve", self.Gv(f, t), self.TA[:, a, 0:n], self.PS[:, b3, 0:n], ALU.mult,
                            [self.TAc[a], self.bankc[b3]], [self.Gc[f][ti]])
        self.s.stage = 'ffn.down'
        for dc in range(8):
            s2a = self.wtake("w2", l, 2 * dc, which)
            s2b = self.wtake("w2", l, 2 * dc + 1, which)
            for t in tiles:
                n, ti = t["n"], t["ti"]
                b = self.mm([(self.wap(s2a if f < 11 else s2b, f % 11, 0, 128), self.Gv(f, t)) for f in range(KF)], n,
                            [s2a.cell, s2b.cell] + [self.Gc[f][ti] for f in range(KF)])
                self.stt("dve", self.Xv(dc, t), self.Xv(dc, t), 2.0 * ALPHA, self.PS[:, b, 0:n], ALU.mult, ALU.add,
                         [self.Xc[ti], self.bankc[b]], [self.Xc[ti]])
        self.s.stage = 'ffn.ln'
        self.ln_x(tiles, l, 0 if which == 0 else 3, 4.0 * EPS)

    def win_evac(self, sl, fc, tiles, fn_evac):
        for t in tiles:
            n, ti = t["n"], t["ti"]
            b = self.mm([(self.wap(sl, k, fc * 128, 128), self.XBv(k, t)) for k in range(KD)], n,
                        [sl.cell, self.XBc[ti]])
            fn_evac(t, b)

    def mixer(self, gi, tiles, l):
        F, G, TA, PS = self.F, self.G, self.TA, self.PS
        Fc, Gc, TAc, bankc = self.Fc, self.Gc, self.TAc, self.bankc
        ptiles = [t for t in tiles if t["kind"] == "p"]
        st = [t for t in tiles if t["kind"] == "s"]
        st = st[0] if st else None
        d, o = self.d, self.o
        first = (gi == 0)
        if st is not None:
            sp_rows = d["state_pool"][l].rearrange("b r c -> (b r) c")
            cv_rows = d["state_conv"][l].rearrange("b r c -> (b r) c")
            sc_rows = d["state_shortconv"][l].rearrange("b r c -> (b r) c")
            for (rows, nrow, per, dst_t, dcells, rr) in ((sp_rows, 240, 15, self.SPOOL, [self.SPOOLc] * 4, 15),
                                                         (cv_rows, 480, 30, self.SCONV, self.SCONVc, 31),
                                                         (sc_rows, 32, 2, self.SSC, self.SSCc, 3)):
                r0 = 0
                while r0 < nrow:
                    nb = min(128 // per, (nrow - r0) // per)
                    m = nb * per
                    sg = self.next_stg()
                    self.dma("sp", "stg%d" % sg, self.STG[0:m, sg, 0:512], rows[r0:r0 + m, :], [], [self.STGc[sg]])
                    b0 = r0 // per
                    for j in range(4):
                        b = self.next_bank()
                        self.tr_to_bank(self.STG[0:m, sg, j * 128:(j + 1) * 128], m, 128, b, 0, [self.STGc[sg]])
                        self.cp("act", dst_t[:, j, b0:b0 + nb, 0:per],
                                PS[:, b, 0:m].rearrange("p (b r) -> p b r", r=per), [bankc[b]], [dcells[j]])
                    r0 += m
            self.dma("sp", "out", o["pool_s"][l, :, 0:14, :], d["state_pool"][l, :, 1:15, :], [], [])
            self.dma("sp", "out", o["conv_s"][l, :, 0:29, :], d["state_conv"][l, :, 1:30, :], [], [])
            self.dma("sp", "out", o["sc_s"][l, :, 0:1, :], d["state_shortconv"][l, :, 1:2, :], [], [])

        npg = len(ptiles) * TILE
        c_lo, c_hi = HP, HP + npg

        self.s.stage = 'mix.A'
        def role(r, fn_evac):
            for hs in range(2):
                sl_ = self.wtake("w_in", l, 2 * r + hs)
                for c2 in range(2):
                    j_ = hs * 2 + c2
                    self.win_evac(sl_, c2, tiles, lambda t, b, j_=j_: fn_evac(j_, t, b))

        for j in range(4):
            self.cp("act", F[:, j, HP - 15:HP], self.CA[:, l, j, :], [self.CAc[l][j]], [Fc[j]])
        role(0, lambda j, t, b: self.cp("act", self.Fv(j, t), PS[:, b, 0:t["n"]], [bankc[b]], [Fc[j]]))
        for g in range(4):
            win = WINS[g]
            Lx = 15 + npg
            e0 = HP - 15
            B1, B2 = 4 + 2 * (g % 2), 5 + 2 * (g % 2)
            src, cur = g, None
            bufs = [B1, B2]
            sh = 1
            step = 0
            while sh < win:
                dstb = bufs[step % 2]
                lo = 2 * sh - 1
                self.tt("dve", F[:, dstb, e0 + lo:e0 + Lx], F[:, src, e0 + lo:e0 + Lx], F[:, src, e0 + lo - sh:e0 + Lx - sh],
                        ALU.add, [Fc[src]], [Fc[dstb]])
                src = dstb
                sh *= 2
                step += 1
            S = src
            for t in ptiles:
                ti = t["ti"]
                cs = slice(HP + t["off"], HP + t["off"] + t["n"])
                self.stt("dve", self.Gv(16 + g, t), F[:, S, cs], 1.0 / win, F[:, g, cs], ALU.mult, ALU.subtract,
                         [Fc[S], Fc[g]], [Gc[16 + g][ti]])
            if first:
                w1_ = win - 1
                a = self.next_ta()
                self.tt("dve", TA[:, a, 0:w1_], F[:, S, HP:HP + w1_], self.RC0[:, g, 0:w1_], ALU.mult,
                        [Fc[S], self.constc], [TAc[a]])
                self.tt("dve", G[:, 16 + g, 0:w1_], TA[:, a, 0:w1_], F[:, g, HP:HP + w1_], ALU.subtract,
                        [TAc[a], Fc[g]], [Gc[16 + g][0]])
            self.cp("act", self.CA[:, l, g, :], F[:, g, c_hi - 15:c_hi], [Fc[g]], [self.CAc[l][g]])
            if st is not None:
                a = self.next_ta()
                tmp = TA[:, a, 0:NS]
                self.s.op("dve", [self.SPOOLc], [TAc[a]],
                          lambda e, g=g, win=win, tmp=tmp: e.tensor_reduce(out=tmp, in_=self.SPOOL[:, g, :, 16 - win:15],
                                                                          axis=AX.X, op=ALU.add))
                self.tt("dve", tmp, tmp, self.Fv(g, st), ALU.add, [TAc[a], Fc[g]], [TAc[a]])
                self.stt("dve", self.Gv(16 + g, st), tmp, 1.0 / win, self.Fv(g, st), ALU.mult, ALU.subtract,
                         [TAc[a], Fc[g]], [Gc[16 + g][st["ti"]]])
        if st is not None:
            self.out_rows([self.Fv(j, st) for j in range(4)], NS, o["pool_s"][l, :, 14, :], [Fc[j] for j in range(4)])

        self.s.stage = 'mix.B'
        role(1, lambda j, t, b: self.cp("act", self.Fv(j, t), PS[:, b, 0:t["n"]], [bankc[b]], [Fc[j]]))
        role(2, lambda j, t, b: self.cp("act", self.Fv(4 + j, t), PS[:, b, 0:t["n"]], [bankc[b]], [Fc[4 + j]]))
        for g in range(4):
            for t in tiles:
                n, ti = t["n"], t["ti"]
                b = self.mm([(self.POOLW[:, l, g, :], self.Gv(16 + g, t))], n, [self.POOLWc, Gc[16 + g][ti]])
                self.act(self.Gv(g, t), PS[:, b, 0:n], AF.Identity, [bankc[b], self.PARc], [Gc[g][ti]],
                         scale=self.par(l, 96 + g))
        for t in tiles:
            ti = t["ti"]

            self.layernorm(t, 4, self.F3(4, 4, t), [Fc[4 + c] for c in range(4)],
                           self.G3(12, 4, t), [Gc[12 + c][ti] for c in range(4)],
                           self.G3(16, 4, t), [Gc[16 + c][ti] for c in range(4)], EPS, l, 100, 104,
                           crit=lambda c, t=t, ti=ti: (self.Gv(8 + c, t), AF.Identity, [Gc[8 + c][ti]]),
                           post=lambda c, t=t: (self.Fv(4 + c, t), AF.Identity, [Fc[4 + c]]),
                           chunk_cells=[Fc[4 + c] for c in range(4)])
        for t in ptiles:
            ti, n = t["ti"], t["n"]
            for c in range(4):
                pb = c % 2
                for h in range(4):
                    vin = self.G[:, 8 + h, t["off"] + c * 128:t["off"] + (c + 1) * 128]
                    outp = self.PSB[:, pb, h * 128:(h + 1) * 128]
                    self.s.op("pe", [Gc[8 + h][ti], self.constc], [self.PSBc[pb]],
                              lambda e, vin=vin, outp=outp: e.transpose(out=outp, in_=vin, identity=self.IDB[:, :]))
                self.cp("act", self.G[:, 20 + c, 0:512], self.PSB[:, pb, :], [self.PSBc[pb]], [Gc[20 + c][0]])
            for h in range(4):
                b = self.next_bank()

                def fn(e, b=b, h=h):
                    ins = None
                    for c in range(4):
                        ins = e.matmul(PS[:, b, c * 128:(c + 1) * 128], lhsT=self.G[:, 20 + c, h * 128:(h + 1) * 128],
                                       rhs=self.WT[:, l, h, :], start=True, stop=True)
                    return ins

                self.s.op("pe", [Gc[20 + c][0] for c in range(4)] + [self.WTc], [bankc[b]], fn)
                a = self.next_ta()
                self.tt("dve", TA[:, a, :].rearrange("p (c t) -> p c t", c=4),
                        PS[:, b, :].rearrange("p (c t) -> p c t", c=4),
                        self.GB[:, l, h * 128:(h + 1) * 128].unsqueeze(1).broadcast_to([128, 4, 128]), ALU.add,
                        [bankc[b], self.GBc], [TAc[a]])
                self.tt("dve", self.Gv(4 + h, t), TA[:, a, 0:n], self.Fv(h, t), ALU.mult, [TAc[a], Fc[h]], [Gc[4 + h][ti]])
        if st is not None:
            self.out_rows([self.Fv(4 + j, st) for j in range(4)], NS, o["gv_s"][l, :, :], [Fc[4 + j] for j in range(4)])
            for h in range(4):
                a = self.next_ta()
                self.ts("dve", TA[:, a, 0:NS], self.Fv(4 + h, st), self.GS[:, l, h:h + 1], self.GS[:, l, 4 + h:5 + h],
                        ALU.mult, ALU.add, [Fc[4 + h], self.GBc], [TAc[a]])
                self.tt("dve", self.Gv(4 + h, st), TA[:, a, 0:NS], self.Fv(h, st), ALU.mult, [TAc[a], Fc[h]],
                        [Gc[4 + h][st["ti"]]])

        self.s.stage = 'mix.C'
        for j in range(4):
            self.cp("act", F[:, j, HP - 30:HP], self.CC[:, l, j, :], [self.CCc[l][j]], [Fc[j]])
        role(3, lambda j, t, b: self.cp("act", self.Fv(j, t), PS[:, b, 0:t["n"]], [bankc[b]], [Fc[j]]))

        def ev_glu(j, t, b):
            a = self.next_ta()
            n = t["n"]
            self.act(TA[:, a, 0:n], PS[:, b, 0:n], AF.Sigmoid, [bankc[b]], [TAc[a]])
            self.tt("dve", self.Fv(j, t), self.Fv(j, t), TA[:, a, 0:n], ALU.mult, [Fc[j], TAc[a]], [Fc[j]])

        role(4, ev_glu)
        for j in range(4):
            self.cp("act", self.CC[:, l, j, :], F[:, j, c_hi - 30:c_hi], [Fc[j]], [self.CCc[l][j]])
            self.cp("act", self.EXTB[:, j, 0:30 + npg], F[:, j, c_lo - 30:c_hi], [Fc[j]], [self.EXTBc[j]])
            g0_ = 8 + 8 * (j % 2)
            dgv = G[:, g0_:g0_ + 8, :].rearrange("p a b -> p (a b)")[:, 0:31 * 128].rearrange("p (k c) -> p k c", c=128)
            dg_cells = [Gc[c][t_["ti"]] for c in range(g0_, g0_ + 8) for t_ in tiles]
            self.tt("dve", dgv, self.IDB[:, :].unsqueeze(1).broadcast_to([128, 31, 128]),
                    self.DWT[:, l, j, :].unsqueeze(2).broadcast_to([128, 31, 128]), ALU.mult,
                    [self.constc, self.DWTc], dg_cells)
            for t in ptiles:
                n = t["n"]
                b = self.mm([(dgv[:, k, :], self.EXTB[:, j, t["off"] + k:t["off"] + k + n]) for k in range(31)], n,
                            [self.EXTBc[j]] + dg_cells)
                self.act(self.Fv(4 + j, t), PS[:, b, 0:n], AF.Identity, [bankc[b], self.PARc], [Fc[4 + j]],
                         bias=self.par(l, 108 + j), scale=1.0)
            if st is not None:
                self.cp("act", self.SCONV[:, j, :, 30], self.Fv(j, st), [Fc[j]], [self.SCONVc[j]])
                a = self.next_ta()
                pr = TA[:, a, 0:NS * 31].rearrange("p (b r) -> p b r", r=31)
                self.tt("dve", pr, self.SCONV[:, j, :, :], self.DWT[:, l, j, :].unsqueeze(1).broadcast_to([128, NS, 31]),
                        ALU.mult, [self.SCONVc[j], self.DWTc], [TAc[a]])
                a2 = self.next_ta()
                tmp = TA[:, a2, 0:NS]
                self.s.op("dve", [TAc[a]], [TAc[a2]],
                          lambda e, pr=pr, tmp=tmp: e.tensor_reduce(out=tmp, in_=pr, axis=AX.X, op=ALU.add))
                self.ts("dve", self.Fv(4 + j, st), tmp, self.par(l, 108 + j), None, ALU.add, None,
                        [TAc[a2], self.PARc], [Fc[4 + j]])
        if st is not None:
            self.out_rows([self.Fv(j, st) for j in range(4)], NS, o["conv_s"][l, :, 29, :], [Fc[j] for j in range(4)])
        self.s.stage = 'mix.D'
        role(5, lambda j, t, b: self.cp("act", self.Fv(j, t), PS[:, b, 0:t["n"]], [bankc[b]], [Fc[j]]))
        for t in tiles:
            ti = t["ti"]

            self.layernorm(t, 4, self.F3(4, 4, t), [Fc[4 + c] for c in range(4)],
                           self.G3(12, 4, t), [Gc[12 + c][ti] for c in range(4)],
                           self.G3(16, 4, t), [Gc[16 + c][ti] for c in range(4)], EPS, l, 112, 116,
                           crit=lambda c, t=t, ti=ti: (self.Gv(8 + c, t), AF.Silu, [Gc[8 + c][ti]]),
                           chunk_cells=[Fc[4 + c] for c in range(4)])

        for j in range(4):
            self.cp("act", F[:, 4 + j, HP - 2:HP], self.CD[:, l, j, :], [self.CDc[l][j]], [Fc[4 + j]])
        role(6, lambda j, t, b: self.cp("act", self.Fv(4 + j, t), PS[:, b, 0:t["n"]], [bankc[b]], [Fc[4 + j]]))
        role(7, lambda j, t, b: self.tt("dve", self.Fv(4 + j, t), self.Fv(4 + j, t), PS[:, b, 0:t["n"]], ALU.mult,
                                        [Fc[4 + j], bankc[b]], [Fc[4 + j]]))
        for j in range(4):
            self.cp("act", self.CD[:, l, j, :], F[:, 4 + j, c_hi - 2:c_hi], [Fc[4 + j]], [self.CDc[l][j]])
            for t in ptiles:
                ti, n = t["ti"], t["n"]
                c0 = HP + t["off"]
                a = self.next_ta()
                y = TA[:, a, 0:n]
                self.ts("dve", y, F[:, 4 + j, c0 - 2:c0 - 2 + n], self.SCW[:, l, j, 0:1], None, ALU.mult, None,
                        [Fc[4 + j], self.DWTc], [TAc[a]])
                for k in (1, 2):
                    self.stt("dve", y, F[:, 4 + j, c0 - 2 + k:c0 - 2 + k + n], self.SCW[:, l, j, k:k + 1], y,
                             ALU.mult, ALU.add, [Fc[4 + j], TAc[a], self.DWTc], [TAc[a]])
                self.tt("dve", self.Gv(12 + j, t), y, self.Fv(j, t), ALU.mult, [TAc[a], Fc[j]], [Gc[12 + j][ti]])
            if st is not None:
                self.cp("act", self.SSC[:, j, :, 2], self.Fv(4 + j, st), [Fc[4 + j]], [self.SSCc[j]])
                a = self.next_ta()
                pr = TA[:, a, 0:NS * 3].rearrange("p (b r) -> p b r", r=3)
                self.tt("dve", pr, self.SSC[:, j, :, :], self.SCW[:, l, j, :].unsqueeze(1).broadcast_to([128, NS, 3]),
                        ALU.mult, [self.SSCc[j], self.DWTc], [TAc[a]])
                a2 = self.next_ta()
                tmp = TA[:, a2, 0:NS]
                self.s.op("dve", [TAc[a]], [TAc[a2]],
                          lambda e, pr=pr, tmp=tmp: e.tensor_reduce(out=tmp, in_=pr, axis=AX.X, op=ALU.add))
                self.tt("dve", self.Gv(12 + j, st), tmp, self.Fv(j, st), ALU.mult, [TAc[a2], Fc[j]],
                        [Gc[12 + j][st["ti"]]])
        if st is not None:
            self.out_rows([self.Fv(4 + j, st) for j in range(4)], NS, o["sc_s"][l, :, 1, :], [Fc[4 + j] for j in range(4)])

        self.s.stage = 'mix.gate'
        for j in range(8):
            ga = self.wtake("gate", l, 2 * j)
            gb_ = self.wtake("gate", l, 2 * j + 1)
            ps_ = self.wtake("proj", l, j, keep=2)
            for t in tiles:
                n, ti = t["n"], t["ti"]
                tas = []
                for bi in range(4):
                    gs = ga if bi < 2 else gb_
                    go = (bi % 2) * 128
                    bg = self.mm([(self.WS[:, gs.slot, k * 256 + go:k * 256 + go + 128], self.XBv(k, t))
                                  for k in range(KD)], n, [gs.cell, self.XBc[ti]])
                    a = self.next_ta()
                    tas.append(a)
                    self.act(TA[:, a, 0:n], PS[:, bg, 0:n], AF.Sigmoid, [bankc[bg], self.PARc], [TAc[a]],
                             bias=self.par(l, 64 + bi * 8 + j), scale=1.0)
                for bi in range(4):
                    by = self.mm([(self.WS[:, ps_.slot, bi * 512 + k * 128:bi * 512 + k * 128 + 128], self.Gv(4 * bi + k, t))
                                  for k in range(4)], n, [ps_.cell] + [Gc[4 * bi + k][ti] for k in range(4)])
                    a = tas[bi]
                    self.tt("dve", TA[:, a, 0:n], TA[:, a, 0:n], PS[:, by, 0:n], ALU.mult, [TAc[a], bankc[by]], [TAc[a]])
                a0, a1, a2, a3 = tas
                self.tt("dve", TA[:, a0, 0:n], TA[:, a0, 0:n], TA[:, a1, 0:n], ALU.add, [TAc[a0], TAc[a1]], [TAc[a0]])
                self.tt("dve", TA[:, a2, 0:n], TA[:, a2, 0:n], TA[:, a3, 0:n], ALU.add, [TAc[a2], TAc[a3]], [TAc[a2]])
                self.tt("dve", self.Gv(16 + j, t), TA[:, a0, 0:n], TA[:, a2, 0:n], ALU.add, [TAc[a0], TAc[a2]],
                        [Gc[16 + j][ti]])
        self.s.stage = 'mix.wo'
        for i in range(4):
            so = self.wtake("w_o", l, i)
            for c in range(2):
                dc = i * 2 + c
                for t in tiles:
                    n, ti = t["n"], t["ti"]
                    b = self.mm([(self.wap(so, k, c * 128, 128), self.Gv(16 + k, t)) for k in range(KD)], n,
                                [so.cell] + [Gc[16 + k][ti] for k in range(KD)])
                    self.stt("dve", self.Xv(dc, t), self.Xv(dc, t), ALPHA, PS[:, b, 0:n], ALU.mult, ALU.add,
                             [self.Xc[ti], bankc[b]], [self.Xc[ti]])
                    self.ln_pre(dc, t)
        self.s.stage = 'mix.ln2'
        self.ln_x(tiles, l, 1, EPS, pre=True)

    def attn(self, gi, tiles, l):
        G, TA, PS = self.G, self.TA, self.PS
        Gc, TAc, bankc = self.Gc, self.TAc, self.bankc
        ptiles = [t for t in tiles if t["kind"] == "p"]
        st = [t for t in tiles if t["kind"] == "s"]
        st = st[0] if st else None
        d = self.d
        self.s.stage = 'attn.q'
        for i in range(4):
            sq = self.wtake("xa_wq", l, i)
            for c in range(2):
                dc = i * 2 + c
                for t in ptiles:
                    n, ti = t["n"], t["ti"]
                    b = self.mm([(self.wap(sq, k, c * 128, 128), self.XBv(k, t)) for k in range(KD)], n,
                                [sq.cell, self.XBc[ti]])
                    self.cp("act", self.Gv(dc, t), PS[:, b, 0:n], [bankc[b]], [Gc[dc][ti]])
            if st is not None:
                b = self.mm([(self.XBv(k, st), self.wap(sq, k, 0, 256)) for k in range(KD)], 256,
                            [sq.cell, self.XBc[st["ti"]]], m=NS)
                self.cp("act", self.QS[0:NS, i * 256:(i + 1) * 256], PS[0:NS, b, 0:256], [bankc[b]], [self.QSc])
        self.s.stage = 'attn.core'
        scale = 256.0 ** -0.5
        for t in ptiles:
            n, ti = t["n"], t["ti"]
            for h in range(4):
                for mt in range(2):
                    b = self.mm([(self.KT[:, l, 2 * h + dd, mt * 128:(mt + 1) * 128], self.Gv(2 * h + dd, t)) for dd in range(2)],
                                n, [self.KTc[l], Gc[2 * h][ti], Gc[2 * h + 1][ti]])
                    self.act(self.Gv(8 + 2 * h + mt, t), PS[:, b, 0:n], AF.Exp, [bankc[b]], [Gc[8 + 2 * h + mt][ti]],
                             scale=scale)
                b = self.mm([(self.ONESB[:, :], self.Gv(8 + 2 * h + mt, t)) for mt in range(2)], n,
                            [self.constc, Gc[8 + 2 * h][ti], Gc[9 + 2 * h][ti]])
                a = self.next_ta()
                self.s.op("dve", [bankc[b]], [TAc[a]],
                          lambda e, a=a, b=b, n=n: e.reciprocal(out=TA[:, a, 0:n], in_=PS[:, b, 0:n]))
                for c in range(2):
                    bo = self.mm([(self.VB[:, l, mt, h * 256 + c * 128:h * 256 + (c + 1) * 128], self.Gv(8 + 2 * h + mt, t))
                                  for mt in range(2)], n, [self.VBc[l], Gc[8 + 2 * h][ti], Gc[9 + 2 * h][ti]])
                    self.tt("dve", self.Gv(16 + 2 * h + c, t), PS[:, bo, 0:n], TA[:, a, 0:n], ALU.mult,
                            [bankc[bo], TAc[a]], [Gc[16 + 2 * h + c][ti]])
        if st is not None:
            sti = st["ti"]
            selv = self.ST[0:NS, 0:4, :].rearrange("p a (b m) -> p (a b) m", m=128)
            self.cp("dve", selv, self.IDF[0:NS, 0:NS].unsqueeze(2).broadcast_to([NS, NS, 128]), [self.constc],
                    [self.STc[0], self.STc[1], self.STc[2], self.STc[3]])
            for bsm in range(NS):
                def fnq(e, bsm=bsm):
                    e.matmul(PS[:, 5, :], lhsT=selv[:, bsm, :], rhs=self.QS[0:NS, 0:512], start=True, stop=True)
                    return e.matmul(PS[:, 6, :], lhsT=selv[:, bsm, :], rhs=self.QS[0:NS, 512:1024], start=True, stop=True)

                self.s.op("pe", [self.QSc, self.STc[0], self.STc[1], self.STc[2], self.STc[3]], [bankc[5], bankc[6]], fnq)
                qb = PS[:, 5:7, :].rearrange("p a c -> p (a c)")
                for mt in range(2):
                    kv = self.kv_i
                    self.kv_i = (self.kv_i + 1) % NKV
                    self.dma("sp", "kv%d" % kv, self.KVS[:, kv, :], d["cache_mem_k"][l, bsm, mt * 128:(mt + 1) * 128, :],
                             [], [self.KVSc[kv]])
                    prod = TA[:, 0:2, :].rearrange("p a c -> p (a c)")
                    self.tt("dve", prod, self.KVS[:, kv, :], qb, ALU.mult, [self.KVSc[kv], bankc[5], bankc[6]],
                            [TAc[0], TAc[1]])
                    pr4 = TA[:, 0:2, :].rearrange("p a (m c) -> p (a m) c", m=2)
                    c0 = bsm * 8 + mt * 4
                    self.s.op("dve", [TAc[0], TAc[1]], [self.SSc],
                              lambda e, c0=c0, pr4=pr4: e.tensor_reduce(out=self.SS[:, c0:c0 + 4], in_=pr4,
                                                                        axis=AX.X, op=ALU.add))
            self.act(self.ES[:, :], self.SS[:, :], AF.Exp, [self.SSc], [self.ESc], scale=scale)
            b = self.mm([(self.ONESF[:, :], self.ES[:, :])], 128, [self.constc, self.ESc])
            a = self.next_ta()
            self.cp("act", TA[:, a, 0:128], PS[:, b, 0:128], [bankc[b]], [TAc[a]])
            den = TA[:, a, 0:128].rearrange("p (b m h) -> p m h b", m=2, h=4)
            self.tt("dve", self.RS[:, :, :], den[:, 0, :, :], den[:, 1, :, :], ALU.add, [TAc[a]], [self.RSc])
            self.s.op("dve", [self.RSc], [self.RSc], lambda e: e.reciprocal(out=self.RS[:, :, :], in_=self.RS[:, :, :]))
            bo2 = (self.next_bank(), self.next_bank())
            for bsm in range(NS):
                for mt in range(2):
                    kv = self.kv_i
                    self.kv_i = (self.kv_i + 1) % NKV
                    self.dma("sp", "kv%d" % kv, self.KVS[:, kv, :], d["cache_mem_v"][l, bsm, mt * 128:(mt + 1) * 128, :],
                             [], [self.KVSc[kv]])

                    def fnv(e, bsm=bsm, kv=kv, bo2=bo2, mt=mt):
                        ins = None
                        for c in range(8):
                            col = bsm * 8 + mt * 4 + c // 2
                            ins = e.matmul(PS[:, bo2[mt], c * NS + bsm:c * NS + bsm + 1],
                                           lhsT=self.KVS[:, kv, c * 128:(c + 1) * 128],
                                           rhs=self.ES[:, col:col + 1], start=True, stop=True)
                        return ins

                    self.s.op("pe", [self.KVSc[kv], self.ESc], [bankc[bo2[mt]]], fnv)
            for h in range(4):
                rs_b = self.RS[:, h, :].unsqueeze(1).broadcast_to([128, 2, NS])
                tmps = []
                for mt in range(2):
                    a = self.next_ta()
                    tv = TA[:, a, 0:2 * NS].rearrange("p (c b) -> p c b", c=2)
                    self.tt("dve", tv, PS[:, bo2[mt], 2 * h * NS:(2 * h + 2) * NS].rearrange("p (c b) -> p c b", c=2),
                            rs_b, ALU.mult, [bankc[bo2[mt]], self.RSc], [TAc[a]])
                    tmps.append((a, tv))
                self.tt("dve", G[:, 16 + 2 * h:18 + 2 * h, st["off"]:st["off"] + NS], tmps[0][1], tmps[1][1], ALU.add,
                        [TAc[tmps[0][0]], TAc[tmps[1][0]]], [Gc[16 + 2 * h][sti], Gc[17 + 2 * h][sti]])
        self.s.stage = 'attn.wo'
        for i in range(4):
            so = self.wtake("xa_wo", l, i)
            for c in range(2):
                dc = i * 2 + c
                for t in tiles:
                    n, ti = t["n"], t["ti"]
                    b = self.mm([(self.wap(so, k, c * 128, 128), self.Gv(16 + k, t)) for k in range(KD)], n,
                                [so.cell] + [Gc[16 + k][ti] for k in range(KD)])
                    self.stt("dve", self.Xv(dc, t), self.Xv(dc, t), ALPHA, PS[:, b, 0:n], ALU.mult, ALU.add,
                             [self.Xc[ti], bankc[b]], [self.Xc[ti]])
                    self.ln_pre(dc, t)
        self.s.stage = 'attn.ln3'
        self.ln_x(tiles, l, 2, EPS, pre=True)

    def setup(self):
        d, o = self.d, self.o
        PS, bankc = self.PS, self.bankc
        cst = self.constc
        pool_ops = []

        def P(fn, rd=False):
            self.s.op("pool", [cst] if rd else [], [cst], fn)

        P(lambda e: e.memset(self.IDF[:, :], 0.0))
        P(lambda e: e.affine_select(out=self.IDF[:, :], in_=self.IDF[:, :], pattern=[[-1, 128]], compare_op=ALU.not_equal,
                                    fill=1.0, base=0, channel_multiplier=1), rd=True)
        P(lambda e: e.tensor_copy(out=self.IDB[:, :], in_=self.IDF[:, :]), rd=True)
        P(lambda e: e.memset(self.ONESB[:, :], 1.0))
        P(lambda e: e.memset(self.ONESF[:, :], 1.0))
        for g in range(4):
            win = WINS[g]
            P(lambda e, g=g, win=win: e.memset(self.RC0[:, g, :], 1.0 / win))
            for tcol in range(win - 1):
                P(lambda e, g=g, tcol=tcol: e.memset(self.RC0[:, g, tcol:tcol + 1], 1.0 / (tcol + 1)))
        for l in range(L):
            for j in range(4):
                self.s.op("pool", [], [self.CAc[l][j]], lambda e, l=l, j=j: e.memset(self.CA[:, l, j, :], 0.0))
                self.s.op("pool", [], [self.CCc[l][j]], lambda e, l=l, j=j: e.memset(self.CC[:, l, j, :], 0.0))
                self.s.op("pool", [], [self.CDc[l][j]], lambda e, l=l, j=j: e.memset(self.CD[:, l, j, :], 0.0))
        TAc, STc = self.TAc, self.STc
        for l in range(L):
            r = 0
            for nm in ("ln1_g", "ln1_b", "ln2_g", "ln2_b", "ln3_g", "ln3_b", "ln4_g", "ln4_b"):
                self.setup_dma("sp", self.PRAW[r:r + 8, l, :], d[nm][l].rearrange("(k p) -> k p", p=128), self.PRAWc)
                r += 8
            self.setup_dma("sp", self.PRAW[r:r + 32, l, :], d["b_gate"][l].rearrange("(k p) -> k p", p=128), self.PRAWc)
            r += 32
            for nm in ("pool_scale", "gmlp_ln_g", "gmlp_ln_b", "conv_db", "conv_ln_g", "conv_ln_b"):
                self.setup_dma("sp", self.PRAW[r:r + 4, l, :], d[nm][l].rearrange("(k p) -> k p", p=128), self.PRAWc)
                r += 4
            assert r == 120
            self.setup_dma("sp", self.TA[0:31, l, :], d["conv_dw"][l], TAc[l])
            self.setup_dma("sp", self.TA[32:35, l, :], d["sc_w"][l], TAc[l])
            self.setup_dma("sp", self.ST[:, l, :].rearrange("p (h s) -> p h s", h=4),
                           d["gmlp_ws"][l].rearrange("h t s -> t h s"), STc[l])
            self.setup_dma("sp", self.GB[:, l, :], d["gmlp_b"][l:l + 1].rearrange("o h t -> o (h t)").partition_broadcast(128),
                           self.GBc)
            self.setup_dma("sp", self.GS[:, l, 0:4],
                           d["gmlp_ws"][l:l + 1, :, 0, 0:1].rearrange("o h a -> o (h a)").partition_broadcast(128),
                           self.GBc, slow=True)
            self.setup_dma("sp", self.GS[:, l, 4:8],
                           d["gmlp_b"][l:l + 1, :, 0:1].rearrange("o h a -> o (h a)").partition_broadcast(128),
                           self.GBc, slow=True)
            self.setup_dma("sp", self.F[:, l, 0:512].rearrange("p (g dd) -> p g dd", g=4),
                           d["pool_w"][l].rearrange("g c dd -> c g dd"), self.Fc[l])
        self.setup_dma("sp", self.STG[:, :, :], d["mem_prompt"].rearrange("(mt p) dd -> p mt dd", p=128), self.STGc[0])
        self.setup_cells.append(self.STGc[1])
        tot = self.s.count["setup"]
        for c in self.setup_cells:
            c.w = ("setup", tot)
            c.r = {}
        for l in range(L):
            self.cp("act", self.POOLW[:, l, :, :], self.F[:, l, 0:512].rearrange("p (g dd) -> p g dd", g=4),
                    [self.Fc[l]], [self.POOLWc])
        for l in range(L):
            for h in range(4):
                wv = self.ST[:, l, h * 128:(h + 1) * 128]
                self.s.op("pool", [STc[l]], [STc[l]],
                          lambda e, wv=wv: e.affine_select(out=wv, in_=wv, pattern=[[-1, 128]], compare_op=ALU.is_ge,
                                                           fill=0.0, base=0, channel_multiplier=1))
        for l in range(L):
            b = self.next_bank()
            self.tr_to_bank(self.PRAW[0:120, l, :], 120, 128, b, 0, [self.PRAWc])
            self.cp("act", self.PAR[:, l, 0:120], PS[:, b, 0:120], [bankc[b]], [self.PARc])
            for j in range(4):
                b = self.next_bank()
                self.tr_to_bank(self.TA[0:31, l, j * 128:(j + 1) * 128], 31, 128, b, 0, [TAc[l]])
                self.cp("act", self.DWT[:, l, j, :], PS[:, b, 0:31], [bankc[b]], [self.DWTc])
                b = self.next_bank()
                self.tr_to_bank(self.TA[32:35, l, j * 128:(j + 1) * 128], 3, 128, b, 0, [TAc[l]], p0=32)
                self.cp("act", self.SCW[:, l, j, :], PS[:, b, 0:3], [bankc[b]], [self.DWTc])
            for h in range(4):
                wv = self.ST[:, l, h * 128:(h + 1) * 128]
                b = self.next_bank()
                self.tr_to_bank(wv, 128, 128, b, 0, [STc[l]])
                self.cp("act", self.WT[:, l, h, :], PS[:, b, 0:128], [bankc[b]], [self.WTc])
        for k in range(KD):
            b = self.next_bank()
            for mt in range(2):
                self.tr_to_bank(self.STG[:, mt, k * 128:(k + 1) * 128], 128, 128, b, mt * 128, [self.STGc[0], self.STGc[1]])
            self.cp("act", self.G[:, k, 0:NMEM], PS[:, b, 0:256], [bankc[b]], [self.Gc[k][0]])
        for l in range(L):
            for (nm, okey) in (("xa_wk", "mk_p"), ("xa_wv", "mv_p")):
                for i in range(4):
                    sl = self.wtake(nm, l, i)
                    for mt in range(2):
                        b = self.mm([(self.G[:, k, mt * 128:(mt + 1) * 128], self.wap(sl, k, 0, 256)) for k in range(KD)],
                                    256, [self.Gc[k][0] for k in range(KD)] + [sl.cell])
                        sg = self.next_stg()
                        self.cp("act", self.STG[:, sg, 0:256], PS[:, b, 0:256], [bankc[b]], [self.STGc[sg]])
                        self.dma("sp", "stg%d" % sg, o[okey][l, mt * 128:(mt + 1) * 128, i * 256:(i + 1) * 256],
                                 self.STG[:, sg, 0:256], [self.STGc[sg]], [])
                        if nm == "xa_wv":
                            self.cp("dve", self.VB[:, l, mt, i * 256:(i + 1) * 256], PS[:, b, 0:256], [bankc[b]], [self.VBc[l]])
                    if nm == "xa_wk":
                        for c in range(2):
                            dc = i * 2 + c
                            b = self.mm([(self.wap(sl, k, c * 128, 128), self.G[:, k, 0:NMEM]) for k in range(KD)], NMEM,
                                        [self.Gc[k][0] for k in range(KD)] + [sl.cell])
                            self.cp("dve", self.KT[:, l, dc, :], PS[:, b, 0:NMEM], [bankc[b]], [self.KTc[l]])

    def load_x(self, gi, tiles):
        d = self.d
        PS, bankc = self.PS, self.bankc
        for t in tiles:
            ti = t["ti"]
            if t["kind"] == "p":
                for tb in range(t["n"] // 128):
                    sg = self.next_stg()
                    r0 = t["seq0"] + tb * 128
                    self.dma("sp", "stg%d" % sg, self.STG[:, sg, :], d["x_prompt"][r0:r0 + 128, :], [], [self.STGc[sg]])
                    for half in range(2):
                        b = self.next_bank()
                        for c in range(4):
                            k = half * 4 + c
                            self.tr_to_bank(self.STG[:, sg, k * 128:(k + 1) * 128], 128, 128, b, c * 128, [self.STGc[sg]])
                        col = t["off"] + tb * 128
                        self.cp("act", self.X[:, half * 4:half * 4 + 4, col:col + 128],
                                PS[:, b, :].rearrange("p (c t) -> p c t", c=4), [bankc[b]], [self.Xc[ti]])
                        self.cp("dve", self.XB[:, half * 4:half * 4 + 4, col:col + 128],
                                self.X[:, half * 4:half * 4 + 4, col:col + 128], [self.Xc[ti]], [self.XBc[ti]])
            else:
                sg = self.next_stg()
                self.dma("sp", "stg%d" % sg, self.STG[0:NS, sg, :], d["x_sample"], [], [self.STGc[sg]])
                b = self.next_bank()
                for k in range(KD):
                    self.tr_to_bank(self.STG[0:NS, sg, k * 128:(k + 1) * 128], NS, 128, b, k * NS, [self.STGc[sg]])
                self.cp("act", self.X3(t), PS[:, b, 0:KD * NS].rearrange("p (k t) -> p k t", k=KD), [bankc[b]], [self.Xc[ti]])
                self.cp("dve", self.XB3(t), self.X3(t), [self.Xc[ti]], [self.XBc[ti]])

    def store_y(self, gi, tiles):
        o = self.o
        PS, bankc = self.PS, self.bankc
        for t in tiles:
            ti = t["ti"]
            if t["kind"] == "p":
                for tb in range(t["n"] // 128):
                    sg = self.next_stg()
                    col = t["off"] + tb * 128
                    for half in range(2):
                        b = self.next_bank()
                        for c in range(4):
                            k = half * 4 + c
                            self.tr_to_bank(self.X[:, k, col:col + 128], 128, 128, b, c * 128, [self.Xc[ti]])
                        self.cp("act" if half == 0 else "dve", self.STG[:, sg, half * 512:(half + 1) * 512], PS[:, b, :],
                                [bankc[b]], [self.STGc[sg]])
                    r0 = t["seq0"] + tb * 128
                    self.dma("sp", "stg%d" % sg, o["y_prompt"][r0:r0 + 128, :], self.STG[:, sg, :], [self.STGc[sg]], [])
            else:
                sg = self.next_stg()
                for half in range(2):
                    b = self.next_bank()
                    for c in range(4):
                        k = half * 4 + c
                        self.tr_to_bank(self.Xv(k, t), 128, NS, b, c * 128, [self.Xc[ti]])
                    self.cp("act", self.STG[0:NS, sg, half * 512:(half + 1) * 512], PS[0:NS, b, :], [bankc[b]], [self.STGc[sg]])
                self.dma("sp", "stg%d" % sg, o["y_sample"], self.STG[0:NS, sg, :], [self.STGc[sg]], [])

    def final_outputs(self):
        o = self.o
        for l in range(L):
            self.out_rows([self.CA[:, l, j, :] for j in range(4)], 15, o["pool_p"][l], self.CAc[l])
            self.out_rows([self.CC[:, l, j, :] for j in range(4)], 30, o["conv_p"][l], self.CCc[l])
            self.out_rows([self.CD[:, l, j, :] for j in range(4)], 2, o["sc_p"][l], self.CDc[l])

    def build(self):
        nc = self.nc
        self.declare_dram()
        with ExitStack() as es:
            self.alloc(es)
            specs = self.specs_setup()
            for gi in range(NGRP):
                for l in range(L):
                    specs += self.specs_layer(l)
            self.wplan(specs)
            steps = [self.setup]
            for gi in range(NGRP):
                tiles = self.tiles_of(gi)
                steps.append(lambda gi=gi, tiles=tiles: (setattr(self.s, "group", gi), self.load_x(gi, tiles)))
                for l in range(L):
                    steps.append(lambda tiles=tiles, l=l: self.ffn(tiles, l, 0))
                    steps.append(lambda gi=gi, tiles=tiles, l=l: self.mixer(gi, tiles, l))
                    steps.append(lambda gi=gi, tiles=tiles, l=l: self.attn(gi, tiles, l))
                    steps.append(lambda tiles=tiles, l=l: self.ffn(tiles, l, 1))
                steps.append(lambda gi=gi, tiles=tiles: self.store_y(gi, tiles))
            steps.append(self.final_outputs)
            limit = self.step_limit if self.step_limit is not None else len(steps)
            for st_ in steps[:limit]:
                st_()
            if limit >= len(steps):
                assert self.wq_pos == len(self.wq_specs)
            self.emit(es)
        return nc

    def emit(self, es):
        nc = self.nc
        s = self.s
        semnames = list(ENGS) + s.dma_sems
        sems = {}
        for nm in semnames:
            sems[nm] = es.enter_context(nc.semaphore("s_" + nm))
        block = es.enter_context(nc.Block())
        final_waits = [(nm, s.count[nm]) for nm in s.dma_sems]

        def run(engname, e):
            if engname in ("dve", "act") and s.count.get("pool", 0):
                e.wait_ge(sems["pool"], s.count["pool"])
            for (need, fn, semkey, inc) in s.streams[engname]:
                for k, v in need.items():
                    e.wait_ge(sems[k], v)
                if inc:
                    fn(e).then_inc(sems[semkey], 1)
                else:
                    fn(e, sems[semkey])
            if engname == "sp":
                for nm, v in final_waits:
                    e.wait_ge(sems[nm], v)

        @block.tensor
        def _(e):
            run("pe", e)

        @block.scalar
        def _(e):
            run("act", e)

        @block.vector
        def _(e):
            run("dve", e)

        @block.gpsimd
        def _(e):
            run("pool", e)

        @block.sync
        def _(e):
            run("sp", e)


_WEIGHT_KEYS = ["ln1_g", "ln1_b", "ffn1_w1", "ffn1_w3", "ffn1_w2", "w_in", "w_gate", "b_gate", "pool_w", "pool_scale",
                "pool_proj", "gmlp_ln_g", "gmlp_ln_b", "gmlp_ws", "gmlp_b", "gmlp_proj", "conv_dw", "conv_db",
                "conv_ln_g", "conv_ln_b", "conv_proj", "sc_w", "sc_proj", "w_o", "ln2_g", "ln2_b", "xa_wq", "xa_wk",
                "xa_wv", "xa_wo", "ln3_g", "ln3_b", "ffn2_w1", "ffn2_w3", "ffn2_w2", "ln4_g", "ln4_b"]


def kernel(**inputs):
    inp = {k: np.ascontiguousarray(np.asarray(v), dtype=np.float32) for k, v in inputs.items()}
    nc = Builder().build()
    in_maps = []
    for c in range(NCORES):
        m = {k: inp[k] for k in _WEIGHT_KEYS}
        m["x_prompt"] = np.ascontiguousarray(inp["x_prompt"][c])
        m["x_sample"] = np.ascontiguousarray(inp["x_sample"][c * NS:(c + 1) * NS, 0, :])
        m["mem_prompt"] = np.ascontiguousarray(inp["mem_prompt"][c])
        m["state_pool"] = np.ascontiguousarray(inp["state_pool"][:, c * NS:(c + 1) * NS])
        m["state_conv"] = np.ascontiguousarray(inp["state_conv"][:, c * NS:(c + 1) * NS])
        m["state_shortconv"] = np.ascontiguousarray(inp["state_shortconv"][:, c * NS:(c + 1) * NS])
        m["cache_mem_k"] = np.ascontiguousarray(inp["cache_mem_k"][:, c * NS:(c + 1) * NS]).reshape(L, NS, NMEM, D)
        m["cache_mem_v"] = np.ascontiguousarray(inp["cache_mem_v"][:, c * NS:(c + 1) * NS]).reshape(L, NS, NMEM, D)
        in_maps.append(m)
    res = run_bass_kernel_spmd(nc, in_maps, core_ids=list(range(NCORES)))
    R = res.results
    y_prompt = np.stack([R[c]["y_prompt"] for c in range(NCORES)], 0)
    y_sample = np.concatenate([R[c]["y_sample"] for c in range(NCORES)], 0)[:, None, :]
    pool_p = np.stack([R[c]["pool_p"] for c in range(NCORES)], 1)
    conv_p = np.stack([R[c]["conv_p"] for c in range(NCORES)], 1)
    sc_p = np.stack([R[c]["sc_p"] for c in range(NCORES)], 1)
    mk_p = np.stack([R[c]["mk_p"] for c in range(NCORES)], 1).reshape(L, NCORES, NMEM, 4, 256)
    mv_p = np.stack([R[c]["mv_p"] for c in range(NCORES)], 1).reshape(L, NCORES, NMEM, 4, 256)
    pool_s = np.concatenate([R[c]["pool_s"] for c in range(NCORES)], 1)
    conv_s = np.concatenate([R[c]["conv_s"] for c in range(NCORES)], 1)
    sc_s = np.concatenate([R[c]["sc_s"] for c in range(NCORES)], 1)
    gv_s = np.concatenate([R[c]["gv_s"] for c in range(NCORES)], 1)[:, :, None, :]
    outs = (y_prompt, y_sample, pool_p, conv_p, sc_p, mk_p, mv_p, pool_s, conv_s, sc_s, gv_s)
    return tuple(np.ascontiguousarray(a, dtype=np.float32) for a in outs)
```
